# Optimizing a Trainium2 kernel written in Bass

```python
import math
import jax
import jax.numpy as jnp
from jax import lax
import numpy as np

D_MODEL = 2048
BATCH = 4
SEQ = 4096
DEPTH = 4

CTX_LEN = 256
GRID_W = 64
EPS = 1e-6
N_MOD = 6
D_MIX = D_MODEL
HY_W = D_MIX // 2
HY_GROUPS = 8
HY_ORDER = 2
HY_PROJ = (HY_ORDER + 1) * HY_W
HY_CONV = 3
HY_POS_EMB = 33
HY_FILT_H = 64
HY_TARGET = 1e-2
HY_FAST = 0.3
HY_SLOW = 1.5
SSD_W = D_MIX - HY_W
SSD_HEADDIM = 64
SSD_HEADS = SSD_W // SSD_HEADDIM
SSD_GROUPS = 2
SSD_HPG = SSD_HEADS // SSD_GROUPS
SSD_STATE = 128
SSD_CONV = 3
SSD_CHUNK = 128
SSD_XBC = SSD_W + 2 * SSD_GROUPS * SSD_STATE
SSD_DT = 2 * SSD_HEADS
PROJ_W = HY_PROJ + SSD_XBC + SSD_DT + SSD_W
D_FF = 4 * D_MODEL

kernel_name = 'hyena_ssd_parallel_prefix_dit'


def rmsnorm(x, w):
    xf = x.astype(jnp.float32)
    y = xf * lax.rsqrt(jnp.mean(xf * xf, axis=-1, keepdims=True) + EPS)
    return (y * w).astype(x.dtype)


def centred_dwconv(u, w, bias, grid):
    b, L, C = u.shape
    if grid is not None:
        u = u.reshape(b, grid[0], grid[1], C)
    K = w.shape[0]
    pad = K // 2
    T = u.shape[-2]
    up = jnp.pad(u, [(0, 0)] * (u.ndim - 2) + [(pad, pad), (0, 0)])
    y = bias + w[0] * up[..., 0:T, :]
    for k in range(1, K):
        y = y + w[k] * up[..., k:k + T, :]
    return y.reshape(b, L, C)


def hyena_kernel_fft(L, fw1, fb1, freq, fw2, fb2, fw3):
    f32 = jnp.float32
    t = jnp.linspace(0.0, 1.0, L, dtype=f32)[:, None]
    bands = (HY_POS_EMB - 1) // 2
    w = 2.0 * math.pi * jnp.arange(L, dtype=f32)[:, None] / L
    f = jnp.linspace(1e-4, bands - 1, bands, dtype=f32)[None, :]
    feats = jnp.concatenate([t, jnp.cos(f * w), -jnp.sin(f * w)], axis=-1)
    hdn = jnp.sin(freq * (feats @ fw1 + fb1))
    hdn = jnp.sin(freq * (hdn @ fw2 + fb2))
    h = (hdn @ fw3).astype(f32).reshape(L, HY_ORDER, 2, HY_W)
    deltas = jnp.linspace(math.log(HY_TARGET) / HY_SLOW, math.log(HY_TARGET) / HY_FAST, HY_W, dtype=f32)
    h = h * jnp.exp(-t[:, :, None, None] * jnp.abs(deltas))
    k = jnp.concatenate([h[:, :, 0], jnp.zeros((1, HY_ORDER, HY_W), f32), h[:0:-1, :, 1]], axis=0)
    k = k / (jnp.sum(jnp.abs(k), axis=0, keepdims=True) + EPS)
    return jnp.fft.rfft(k, axis=0)


def fft_conv(z, kf):
    L = z.shape[1]
    zf = jnp.fft.rfft(z, n=2 * L, axis=1)
    return jnp.fft.irfft(zf * kf[None], n=2 * L, axis=1)[:, :L]


def segsum(a):
    T = a.shape[-1]
    cs = jnp.cumsum(a, axis=-1)
    diff = cs[..., :, None] - cs[..., None, :]
    mask = jnp.tril(jnp.ones((T, T), dtype=bool))
    return jnp.where(mask, diff, -jnp.inf)


def ssd_scan(xdt, da, bm, cm, init, want_y):
    b, l, g, r, p = xdt.shape
    nc = l // SSD_CHUNK
    xdt = xdt.reshape(b, nc, SSD_CHUNK, g, r, p)
    bc = bm.reshape(b, nc, SSD_CHUNK, g, -1)
    a = da.reshape(b, nc, SSD_CHUNK, g, r).transpose(0, 3, 4, 1, 2)
    a_cs = jnp.cumsum(a, axis=-1)
    decay_to_end = jnp.exp(a_cs[..., -1:] - a_cs)
    states = jnp.einsum('bcsgn,bgrcs,bcsgrp->bcgrpn', bc, decay_to_end, xdt)
    chunk_a = jnp.pad(a_cs[..., -1], ((0, 0), (0, 0), (0, 0), (1, 0)))
    states_all = jnp.concatenate([init[:, None], states], axis=1)
    carried = jnp.einsum('bgrzc,bcgrpn->bzgrpn', jnp.exp(segsum(chunk_a)), states_all)
    final = carried[:, -1]
    if not want_y:
        return final
    cc = cm.reshape(b, nc, SSD_CHUNK, g, -1)
    cb = jnp.einsum('bclgn,bcsgn->bgcls', cc, bc)
    y_diag = jnp.einsum('bgcls,bgrcls,bcsgrp->bclgrp', cb, jnp.exp(segsum(a)), xdt)
    y_off = jnp.einsum('bclgn,bcgrpn,bgrcl->bclgrp', cc, carried[:, :-1], jnp.exp(a_cs))
    y = (y_diag + y_off).reshape(b, l, g, r, p)
    return y, final


def ssd_branch(h, init_f, init_b, grid, lp, want_y):
    b, L, _ = h.shape
    f32 = jnp.float32
    w_in = lp['w_in']
    o0 = HY_PROJ
    o1 = o0 + SSD_XBC
    o2 = o1 + SSD_DT
    xbc = jax.nn.silu(centred_dwconv(h @ w_in[:, o0:o1], lp['ssd_conv_w'], lp['ssd_conv_b'], grid)).astype(f32)
    gn = SSD_GROUPS * SSD_STATE
    xs = xbc[..., :SSD_W].reshape(b, L, SSD_GROUPS, SSD_HPG, SSD_HEADDIM)
    bm = xbc[..., SSD_W:SSD_W + gn].reshape(b, L, SSD_GROUPS, SSD_STATE)
    cm = xbc[..., SSD_W + gn:].reshape(b, L, SSD_GROUPS, SSD_STATE)
    dt = jax.nn.softplus((h @ w_in[:, o1:o2]).astype(f32).reshape(b, L, 2, SSD_HEADS) + lp['dt_bias'])
    dt = dt.reshape(b, L, 2, SSD_GROUPS, SSD_HPG)
    a = (-jnp.exp(lp['a_log'].astype(f32))).reshape(2, SSD_GROUPS, SSD_HPG)

    def run(d, init, flip):
        dt_d = dt[:, :, d]
        seqs = (xs * dt_d[..., None], dt_d * a[d], bm, cm)
        if flip:
            seqs = tuple(s[:, ::-1] for s in seqs)
        return ssd_scan(seqs[0], seqs[1], seqs[2], seqs[3], init, want_y)

    if not want_y:
        return run(0, init_f, False), run(1, init_b, True)
    y_f, s_f = run(0, init_f, False)
    y_b, s_b = run(1, init_b, True)
    y = y_f + y_b[:, ::-1] + lp['ssd_d'].reshape(SSD_GROUPS, SSD_HPG)[..., None] * xs
    zg = jax.nn.silu((h @ w_in[:, o2:]).astype(f32))
    y = (y.reshape(b, L, SSD_W) * zg).reshape(b, L, SSD_GROUPS, SSD_W // SSD_GROUPS)
    y = rmsnorm(y, lp['ssd_norm_w'].reshape(SSD_GROUPS, -1)).reshape(b, L, SSD_W)
    return y.astype(h.dtype), s_f, s_b


def token_mixer(h, init_f, init_b, grid, lp):
    b, L, _ = h.shape
    u = centred_dwconv(h @ lp['w_in'][:, :HY_PROJ], lp['hy_conv_w'], lp['hy_conv_b'], grid).astype(jnp.float32)
    v, *gates = jnp.split(u, HY_ORDER + 1, axis=-1)
    kf = hyena_kernel_fft(L, lp['filt_w1'], lp['filt_b1'], lp['filt_freq'], lp['filt_w2'], lp['filt_b2'], lp['filt_w3'])
    z = v
    for o, gate in enumerate(gates):
        z = gate * (fft_conv(z, kf[:, o]) + z * lp['hy_bias'][o])
    y_hy = rmsnorm(z.reshape(b, L, HY_GROUPS, HY_W // HY_GROUPS), lp['hy_norm_w'].reshape(HY_GROUPS, -1))
    y_hy = y_hy.reshape(b, L, HY_W).astype(h.dtype)
    y_ssd, s_f, s_b = ssd_branch(h, init_f, init_b, grid, lp, True)
    out = jnp.concatenate([y_hy, y_ssd], axis=-1) @ lp['w_out']
    return out, s_f, s_b


def sqrelu_mlp(h, w1, w2):
    return jnp.square(jax.nn.relu(h @ w1)) @ w2


def setup_inputs(seed: int = 0) -> dict:
    key = jax.random.key(seed)
    ks = jax.random.split(key, 32)

    def nrm(k, shape, scale):
        return scale * jax.random.normal(k, shape, jnp.float32)

    D = D_MODEL
    dt0 = jnp.exp(jax.random.uniform(ks[20], (DEPTH, 2, SSD_HEADS), jnp.float32, math.log(1e-3), math.log(1e-1)))
    return {
        'x': nrm(ks[0], (BATCH, SEQ, D), 1.0),
        'c': nrm(ks[1], (BATCH, D), 1.0),
        'ctx': nrm(ks[2], (BATCH, CTX_LEN, D), 1.0),
        'c_ctx': nrm(ks[3], (D,), 1.0),
        'w_ada': nrm(ks[4], (DEPTH, D, N_MOD * D), 0.5 * D ** -0.5),
        'b_ada': nrm(ks[5], (DEPTH, N_MOD * D), 0.02),
        'norm1_w': 1.0 + nrm(ks[6], (DEPTH, D), 0.05),
        'w_in': nrm(ks[7], (DEPTH, D, PROJ_W), D ** -0.5),
        'hy_conv_w': nrm(ks[8], (DEPTH, HY_CONV, HY_PROJ), HY_CONV ** -0.5),
        'hy_conv_b': nrm(ks[9], (DEPTH, HY_PROJ), 0.02),
        'filt_w1': nrm(ks[10], (DEPTH, HY_POS_EMB, HY_FILT_H), HY_POS_EMB ** -0.5),
        'filt_b1': nrm(ks[11], (DEPTH, HY_FILT_H), 0.1),
        'filt_freq': 1.0 + nrm(ks[12], (DEPTH, HY_FILT_H), 0.1),
        'filt_w2': nrm(ks[13], (DEPTH, HY_FILT_H, HY_FILT_H), HY_FILT_H ** -0.5),
        'filt_b2': nrm(ks[14], (DEPTH, HY_FILT_H), 0.1),
        'filt_w3': nrm(ks[15], (DEPTH, HY_FILT_H, HY_ORDER * 2 * HY_W), HY_FILT_H ** -0.5),
        'hy_bias': nrm(ks[16], (DEPTH, HY_ORDER, HY_W), 0.1),
        'hy_norm_w': 1.0 + nrm(ks[17], (DEPTH, HY_W), 0.05),
        'ssd_conv_w': nrm(ks[18], (DEPTH, SSD_CONV, SSD_XBC), SSD_CONV ** -0.5),
        'ssd_conv_b': nrm(ks[19], (DEPTH, SSD_XBC), 0.02),
        'dt_bias': dt0 + jnp.log(-jnp.expm1(-dt0)),
        'a_log': jnp.log(jax.random.uniform(ks[21], (DEPTH, 2, SSD_HEADS), jnp.float32, 1.0, 16.0)),
        'ssd_d': 1.0 + nrm(ks[22], (DEPTH, SSD_HEADS), 0.05),
        'ssd_norm_w': 1.0 + nrm(ks[23], (DEPTH, SSD_W), 0.05),
        'w_out': nrm(ks[24], (DEPTH, D_MIX, D), D_MIX ** -0.5),
        'norm2_w': 1.0 + nrm(ks[25], (DEPTH, D), 0.05),
        'w_mlp1': nrm(ks[26], (DEPTH, D, D_FF), D ** -0.5),
        'w_mlp2': nrm(ks[27], (DEPTH, D_FF, D), D_FF ** -0.5),
        'final_norm_w': 1.0 + nrm(ks[28], (D,), 0.05),
    }


def reference(x, c, ctx, c_ctx, w_ada, b_ada, norm1_w, w_in, hy_conv_w, hy_conv_b, filt_w1, filt_b1,
              filt_freq, filt_w2, filt_b2, filt_w3, hy_bias, hy_norm_w, ssd_conv_w, ssd_conv_b, dt_bias,
              a_log, ssd_d, ssd_norm_w, w_out, norm2_w, w_mlp1, w_mlp2, final_norm_w):
    rows = x.shape[1] // GRID_W
    grid = (rows, GRID_W)
    zero_state = jnp.zeros((x.shape[0], SSD_GROUPS, SSD_HPG, SSD_HEADDIM, SSD_STATE), jnp.float32)
    h_ctx = ctx
    for i in range(DEPTH):
        lp = dict(w_in=w_in[i], hy_conv_w=hy_conv_w[i], hy_conv_b=hy_conv_b[i], filt_w1=filt_w1[i],
                  filt_b1=filt_b1[i], filt_freq=filt_freq[i], filt_w2=filt_w2[i], filt_b2=filt_b2[i],
                  filt_w3=filt_w3[i], hy_bias=hy_bias[i], hy_norm_w=hy_norm_w[i], ssd_conv_w=ssd_conv_w[i],
                  ssd_conv_b=ssd_conv_b[i], dt_bias=dt_bias[i], a_log=a_log[i], ssd_d=ssd_d[i],
                  ssd_norm_w=ssd_norm_w[i], w_out=w_out[i])
        mod = (jax.nn.silu(c) @ w_ada[i] + b_ada[i])[:, None, :]
        mod_c = jax.nn.silu(c_ctx) @ w_ada[i] + b_ada[i]
        sh1, sc1, g1, sh2, sc2, g2 = jnp.split(mod, N_MOD, axis=-1)
        csh1, csc1, cg1, csh2, csc2, cg2 = jnp.split(mod_c, N_MOD, axis=-1)
        hc = rmsnorm(h_ctx, norm1_w[i]) * (1.0 + csc1) + csh1
        if i < DEPTH - 1:
            out_c, s_f, s_b = token_mixer(hc, zero_state, zero_state, None, lp)
            h_ctx = h_ctx + cg1 * out_c
            hc2 = rmsnorm(h_ctx, norm2_w[i]) * (1.0 + csc2) + csh2
            h_ctx = h_ctx + cg2 * sqrelu_mlp(hc2, w_mlp1[i], w_mlp2[i])
        else:
            s_f, s_b = ssd_branch(hc, zero_state, zero_state, None, lp, False)
        hl = rmsnorm(x, norm1_w[i]) * (1.0 + sc1) + sh1
        out_l, _, _ = token_mixer(hl, s_f, s_b, grid, lp)
        x = x + g1 * out_l
        hl2 = rmsnorm(x, norm2_w[i]) * (1.0 + sc2) + sh2
        x = x + g2 * sqrelu_mlp(hl2, w_mlp1[i], w_mlp2[i])
    return rmsnorm(x, final_norm_w)
```

```python
import numpy as np
import concourse.bass as bass
import concourse.mybir as mybir
from concourse.bass_utils import run_bass_kernel_spmd

F32 = mybir.dt.float32
ALU = mybir.AluOpType
AF = mybir.ActivationFunctionType
AX = mybir.AxisListType

SAME_ENGINE_SYNC = True
NDMA_SLOTS = 6


class Buf:
    __slots__ = ("name", "w", "r", "untracked")

    def __init__(self, name):
        self.name = name
        self.w = None
        self.r = {}
        self.untracked = False


class V:
    __slots__ = ("buf", "ap")

    def __init__(self, buf, ap):
        self.buf = buf
        self.ap = ap

    def __getitem__(self, idx):
        return V(self.buf, self.ap[idx])


class Tile:
    def __init__(self, ap, name, buf=None):
        self.ap = ap
        self.buf = buf if buf is not None else Buf(name)

    def __getitem__(self, idx):
        return V(self.buf, self.ap[idx])

    def sub(self, idx, name):
        return Tile(self.ap[idx], name)


class Prog:
    ENG = ("pe", "dve", "act", "pool", "sp")

    def __init__(self, nc):
        self.nc = nc
        self.items = {e: [] for e in self.ENG}
        self.cnt = {e: 0 for e in self.ENG}
        self.waited = {e: {} for e in self.ENG}
        self.dma_use = {}
        self.dma_rr = {"sp": 0, "pool": 0, "act": 0}
        self.semkeys = set()
        self.out_deps = []
        self.nalloc = 0

    def sb(self, name, shape, dtype=F32):
        h = self.nc.alloc_sbuf_tensor(name, list(shape), dtype)
        return Tile(h, name)

    def ps(self, name, shape, dtype=F32):
        h = self.nc.alloc_psum_tensor(name, list(shape), dtype)
        return Tile(h, name)

    def _need(self, eng, key, val):
        if self.waited[eng].get(key, 0) >= val:
            return
        self.waited[eng][key] = val
        self.items[eng].append(("wait", key, val))

    def _deps(self, eng, reads, writes, is_dma):
        deps = []
        for b in reads:
            if b.w is not None:
                deps.append(b.w)
        for b in writes:
            if b.w is not None:
                deps.append(b.w)
            deps.extend(b.r.items())
        for key, val in deps:
            if key[0] == "E" and key[1] == eng and not is_dma:
                if eng == "pe" or not SAME_ENGINE_SYNC:
                    continue
            self._need(eng, key, val)

    def _mark(self, reads, writes, key, val):
        for b in reads:
            b.r[key] = val
        for b in writes:
            b.w = (key, val)
            b.r = {}

    def op(self, eng, fn, reads, writes):
        reads = [x.buf if not isinstance(x, Buf) else x for x in reads]
        writes = [x.buf if not isinstance(x, Buf) else x for x in writes]
        self._deps(eng, reads, writes, False)
        self.cnt[eng] += 1
        key = ("E", eng)
        self.semkeys.add(key)
        self.items[eng].append(("ins", fn, key, 1))
        self._mark(reads, writes, key, self.cnt[eng])

    def dma(self, out, in_, queue=None, **kw):
        if queue is None:
            queue = "sp" if self.dma_rr["sp"] <= self.dma_rr["pool"] else "pool"
        slot = self.dma_rr[queue] % NDMA_SLOTS
        self.dma_rr[queue] += 1
        key = ("D", queue, slot)
        self.semkeys.add(key)
        uses = self.dma_use.get(key, 0)
        if uses > 0:
            self._need(queue, key, 16 * uses)
        wr = [] if out.buf.untracked else [out.buf]
        self._deps(queue, [in_.buf], wr, True)
        self.dma_use[key] = uses + 1
        oap, iap = out.ap, in_.ap
        self.items[queue].append(("ins", lambda e: e.dma_start(out=oap, in_=iap, **kw), key, 16))
        self._mark([in_.buf], wr, key, 16 * (uses + 1))
        if out.buf.untracked:
            self.out_deps.append((key, 16 * (uses + 1)))

    def mm(self, out, lhsT, rhs, start, stop):
        o, l, r = out.ap, lhsT.ap, rhs.ap
        self.op("pe", lambda e: e.matmul(o, l, r, start=start, stop=stop), [lhsT, rhs], [out])

    def transpose(self, out, in_, ident):
        o, i, d = out.ap, in_.ap, ident.ap
        self.op("pe", lambda e: e.transpose(o, i, d), [in_, ident], [out])

    def act(self, out, in_, func, bias=0.0, scale=1.0, accum_out=None, eng="act"):
        rd = [in_]
        wr = [out]
        b = bias
        s = scale
        if isinstance(bias, V):
            rd.append(bias); b = bias.ap
        if isinstance(scale, V):
            rd.append(scale); s = scale.ap
        kw = {}
        if accum_out is not None:
            wr.append(accum_out); kw["accum_out"] = accum_out.ap
        o, i = out.ap, in_.ap
        self.op("act", lambda e: e.activation(o, i, func, bias=b, scale=s, **kw), rd, wr)

    def tt(self, out, in0, in1, op, eng="dve"):
        o, a, b = out.ap, in0.ap, in1.ap
        self.op(eng, lambda e: e.tensor_tensor(o, a, b, op), [in0, in1], [out])

    def ts(self, out, in0, s1, s2, op0, op1=None, eng="dve", accum_out=None):
        rd = [in0]
        wr = [out]
        a1, a2 = s1, s2
        if isinstance(s1, V):
            rd.append(s1); a1 = s1.ap
        if isinstance(s2, V):
            rd.append(s2); a2 = s2.ap
        o, i = out.ap, in0.ap
        kw = {}
        if accum_out is not None:
            wr.append(accum_out); kw["accum_out"] = accum_out.ap
        if op1 is None:
            self.op(eng, lambda e: e.tensor_scalar(o, i, a1, None, op0, **kw), rd, wr)
        else:
            self.op(eng, lambda e: e.tensor_scalar(o, i, a1, a2, op0, op1, **kw), rd, wr)

    def stt(self, out, in0, scalar, in1, op0, op1, eng="dve"):
        rd = [in0, in1]
        s = scalar
        if isinstance(scalar, V):
            rd.append(scalar); s = scalar.ap
        o, a, b = out.ap, in0.ap, in1.ap
        self.op(eng, lambda e: e.scalar_tensor_tensor(o, a, s, b, op0, op1), rd, [out])

    def copy(self, out, in_, eng="dve"):
        o, i = out.ap, in_.ap
        if eng == "act":
            self.op("act", lambda e: e.copy(o, i), [in_], [out])
        else:
            self.op(eng, lambda e: e.tensor_copy(o, i), [in_], [out])

    def memset(self, out, val, eng="dve"):
        o = out.ap
        self.op(eng, lambda e: e.memset(o, val), [], [out])

    def reduce(self, out, in_, op, axis=AX.X, eng="dve"):
        o, i = out.ap, in_.ap
        self.op(eng, lambda e: e.tensor_reduce(o, i, axis, op), [in_], [out])

    def recip(self, out, in_):
        o, i = out.ap, in_.ap
        self.op("dve", lambda e: e.reciprocal(o, i), [in_], [out])

    def final_wait(self, bufs=()):
        for b in bufs:
            b = b.buf if not isinstance(b, Buf) else b
            if b.w is not None:
                self._need("sp", b.w[0], b.w[1])
        for key, val in self.out_deps:
            self._need("sp", key, val)

    def emit(self):
        nc = self.nc
        sems = {}
        for key in sorted(self.semkeys):
            sems[key] = nc.alloc_semaphore("s_" + "_".join(str(k) for k in key))
        items = self.items

        def replay(eng_obj, lst):
            for it in lst:
                if it[0] == "wait":
                    eng_obj.wait_ge(sems[it[1]], it[2])
                else:
                    it[1](eng_obj).then_inc(sems[it[2]], it[3])

        with nc.Block() as block:
            @block.tensor
            def _(e):
                replay(e, items["pe"])

            @block.vector
            def _(e):
                replay(e, items["dve"])

            @block.scalar
            def _(e):
                replay(e, items["act"])

            @block.gpsimd
            def _(e):
                replay(e, items["pool"])

            @block.sync
            def _(e):
                replay(e, items["sp"])
        return sems


D = 2048
NKC = 16
NT = 2176
EPS = 1e-6
PROJ = 5664
DFF = 8192


def new_nc():
    return bass.Bass("TRN2", target_bir_lowering=False)


def din(nc, name, shape):
    return Tile(nc.dram_tensor(name, list(shape), F32, kind="ExternalInput").ap(), name)


def dout(nc, name, shape):
    t = Tile(nc.dram_tensor(name, list(shape), F32, kind="ExternalOutput").ap(), name)
    t.buf.untracked = True
    return t


def build_P():
    nc = new_nc()
    cc = din(nc, "cc", [128, 16, 5])
    wada = din(nc, "wada", [4, 2048, 1536])
    bada = din(nc, "bada", [128, 4, 12])
    modp = dout(nc, "modp", [128, 4, 12, 5])
    p = Prog(nc)
    cct = p.sb("cct", [128, 16, 5])
    bt = p.sb("bt", [128, 4, 12])
    mo = p.sb("mo", [128, 4, 12, 5])
    wb = [p.sb("wb%d" % i, [128, 16, 512]) for i in range(2)]
    pss = [p.ps("ps%d" % i, [128, 8]) for i in range(2)]
    p.dma(cct[:], cc[:])
    p.dma(bt[:], bada[:])
    p.act(cct[:], cct[:], AF.Silu)
    n = 0
    for l in range(4):
        wv = Tile(wada.ap[l].rearrange("(kc p) c -> p kc c", p=128), "wv", wada.buf)
        for blk in range(3):
            wt = wb[(l * 3 + blk) % 2]
            p.dma(wt[:], wv[:, :, blk * 512:(blk + 1) * 512])
            for j in range(4):
                ch = blk * 4 + j
                ps = pss[n % 2]
                n += 1
                for kc in range(16):
                    p.mm(ps[:, 0:5], wt[:, kc, j * 128:(j + 1) * 128], cct[:, kc, :], kc == 0, kc == 15)
                p.ts(mo[:, l, ch, :], ps[:, 0:5], bt[:, l, ch:ch + 1], None, ALU.add)
    p.dma(modp[:], mo[:])
    p.final_wait()
    p.emit()
    return nc


def rms_mod(p, xt, W, hl, gs_col, sh_col, ones, sqb, ss_ps, rstd):
    for kc in range(NKC):
        sq = sqb[kc % 2]
        p.act(sq[:, :W], xt[:, kc, :W], AF.Square)
        p.mm(ss_ps[:, :W], ones[:, :], sq[:, :W], kc == 0, kc == NKC - 1)
    p.ts(rstd[:, :W], ss_ps[:, :W], 1.0 / D, EPS, ALU.mult, ALU.add)
    p.act(rstd[:, :W], rstd[:, :W], AF.Sqrt)
    p.recip(rstd[:, :W], rstd[:, :W])
    for kc in range(NKC):
        p.stt(hl[:, kc, :W], xt[:, kc, :W], gs_col(kc), rstd[:, :W], ALU.mult, ALU.mult)
        p.act(hl[:, kc, :W], hl[:, kc, :W], AF.Identity, bias=sh_col(kc), scale=1.0)


TILES_A = [(0, 128, 1)] + [(128 + 512 * j, 512, 0) for j in range(4)]


def build_A():
    nc = new_nc()
    xT = din(nc, "xT", [D, NT])
    modT = din(nc, "modT", [128, 96, 2])
    n1 = din(nc, "n1", [128, 16])
    win = din(nc, "win", [D, PROJ])
    uT = dout(nc, "uT", [PROJ, NT])
    p = Prog(nc)
    ones = p.sb("ones", [128, 128])
    p.memset(ones[:], 1.0)
    mt = p.sb("mt", [128, 96, 2])
    n1t = p.sb("n1t", [128, 16])
    gs = p.sb("gs", [128, 16, 2])
    p.dma(mt[:], modT[:])
    p.dma(n1t[:], n1[:])
    for r in range(2):
        p.stt(gs[:, :, r], mt[:, 16:32, r], 1.0, n1t[:, :], ALU.add, ALU.mult)
    xb = [p.sb("xb%d" % i, [128, 16, 512]) for i in range(2)]
    hl = p.sb("hl", [128, 16, 512])
    sqb = [p.sb("sq%d" % i, [128, 512]) for i in range(2)]
    rstd = p.sb("rstd", [128, 512])
    wb = [p.sb("wb%d" % i, [128, 16, 512]) for i in range(2)]
    wdt = p.sb("wdt", [128, 16, 32])
    ob = [p.sb("ob%d" % i, [128, 512]) for i in range(3)]
    ss_ps = p.ps("ss_ps", [128, 512])
    pss = [p.ps("ps%d" % i, [128, 512]) for i in range(3)]
    xv = Tile(xT.ap.rearrange("(kc p) t -> p kc t", p=128), "xv", xT.buf)
    wv = Tile(win.ap.rearrange("(kc p) c -> p kc c", p=128), "wv", win.buf)
    p.dma(wdt[:], wv[:, :, 5632:5664])
    nw = 0
    no = 0
    for ti, (t0, W, r) in enumerate(TILES_A):
        xt = xb[ti % 2]
        p.dma(xt[:, :, :W], xv[:, :, t0:t0 + W])
        rms_mod(p, xt, W, hl, lambda kc: gs[:, kc, r:r + 1], lambda kc: mt[:, kc, r:r + 1], ones, sqb, ss_ps, rstd)
        for og in range(12):
            if og < 11:
                wt = wb[nw % 2]
                nw += 1
                p.dma(wt[:], wv[:, :, og * 512:(og + 1) * 512])
                subs = [(wt, j * 128, 128, (og * 4 + j) * 128) for j in range(4)]
            else:
                subs = [(wdt, 0, 32, 5632)]
            for (wt_, c0, M, row0) in subs:
                ps = pss[no % 3]
                o = ob[no % 3]
                for kc in range(NKC):
                    p.mm(ps[:M, :W], wt_[:, kc, c0:c0 + M], hl[:, kc, :W], kc == 0, kc == NKC - 1)
                p.copy(o[:M, :W], ps[:M, :W], eng="act" if no % 2 else "dve")
                p.dma(uT[row0:row0 + M, t0:t0 + W], o[:M, :W])
                no += 1
    p.final_wait()
    p.emit()
    return nc


TILES_C = [(0, 128, 1)] + [(128 + 256 * j, 256, 0) for j in range(8)]


def build_C(final):
    nc = new_nc()
    xT = din(nc, "xT", [D, NT])
    yT = din(nc, "yT", [D, NT])
    modT = din(nc, "modT", [128, 96, 2])
    n2 = din(nc, "n2", [128, 16])
    wout = din(nc, "wout", [D, D])
    w1 = din(nc, "w1", [D, DFF])
    w2 = din(nc, "w2", [DFF, D])
    xo = dout(nc, "xo", [D, NT])
    if final:
        fnw = din(nc, "fnw", [128, 16])
        fo = dout(nc, "fo", [D, 2048])
    p = Prog(nc)
    ones = p.sb("ones", [128, 128])
    p.memset(ones[:], 1.0)
    mt = p.sb("mt", [128, 96, 2])
    n2t = p.sb("n2t", [128, 16])
    gs = p.sb("gs", [128, 16, 2])
    p.dma(mt[:], modT[:])
    p.dma(n2t[:], n2[:])
    for r in range(2):
        p.stt(gs[:, :, r], mt[:, 64:80, r], 1.0, n2t[:, :], ALU.add, ALU.mult)
    if final:
        fnt = p.sb("fnt", [128, 16])
        p.dma(fnt[:], fnw[:])
    WT = 256
    xt = p.sb("xt", [128, 16, WT])
    yt = p.sb("yt", [128, 16, WT])
    hh = p.sb("hh", [128, 64, WT])
    sqb = [p.sb("sq%d" % i, [128, WT]) for i in range(2)]
    rstd = p.sb("rstd", [128, WT])
    rl = [p.sb("rl%d" % i, [128, WT]) for i in range(2)]
    wb = [p.sb("wb%d" % i, [128, 16, 512]) for i in range(2)]
    wb2 = [Tile(w.ap.rearrange("p a (b c) -> p (a b) c", c=128), "wb2", w.buf) for w in wb]
    ss_ps = p.ps("ss_ps", [128, WT])
    pss = [p.ps("ps%d" % i, [128, WT]) for i in range(3)]
    xv = Tile(xT.ap.rearrange("(kc p) t -> p kc t", p=128), "xv", xT.buf)
    yv = Tile(yT.ap.rearrange("(kc p) t -> p kc t", p=128), "yv", yT.buf)
    xov = Tile(xo.ap.rearrange("(kc p) t -> p kc t", p=128), "xov", xo.buf)
    wov = Tile(wout.ap.rearrange("(kc p) c -> p kc c", p=128), "wov", wout.buf)
    w1v = Tile(w1.ap.rearrange("(kc p) c -> p kc c", p=128), "w1v", w1.buf)
    w2v = Tile(w2.ap.rearrange("(kc p) c -> p kc c", p=128), "w2v", w2.buf)
    if final:
        fov = Tile(fo.ap.rearrange("(kc p) t -> p kc t", p=128), "fov", fo.buf)
    nw = 0
    npz = 0
    for ti, (t0, W, r) in enumerate(TILES_C):
        p.dma(xt[:, :, :W], xv[:, :, t0:t0 + W])
        p.dma(yt[:, :, :W], yv[:, :, t0:t0 + W])
        for dg in range(4):
            wt = wb[nw % 2]
            nw += 1
            p.dma(wt[:], wov[:, :, dg * 512:(dg + 1) * 512])
            for j in range(4):
                dc = dg * 4 + j
                ps = pss[npz % 3]
                npz += 1
                for mc in range(NKC):
                    p.mm(ps[:, :W], wt[:, mc, j * 128:(j + 1) * 128], yt[:, mc, :W], mc == 0, mc == NKC - 1)
                p.stt(xt[:, dc, :W], ps[:, :W], mt[:, 32 + dc, r:r + 1], xt[:, dc, :W], ALU.mult, ALU.add)
        rms_mod(p, xt, W, yt, lambda kc: gs[:, kc, r:r + 1], lambda kc: mt[:, 48 + kc, r:r + 1], ones, sqb, ss_ps, rstd)
        for hg in range(16):
            wt = wb[nw % 2]
            nw += 1
            p.dma(wt[:], w1v[:, :, hg * 512:(hg + 1) * 512])
            for j in range(4):
                hc = hg * 4 + j
                ps = pss[npz % 3]
                rr = rl[npz % 2]
                npz += 1
                for kc in range(NKC):
                    p.mm(ps[:, :W], wt[:, kc, j * 128:(j + 1) * 128], yt[:, kc, :W], kc == 0, kc == NKC - 1)
                p.act(rr[:, :W], ps[:, :W], AF.Relu)
                p.tt(hh[:, hc, :W], rr[:, :W], rr[:, :W], ALU.mult)
        for dc in range(16):
            wt = wb2[nw % 2]
            nw += 1
            for q in range(4):
                p.dma(wt[:, q * 16:(q + 1) * 16, :], w2v[:, q * 16:(q + 1) * 16, dc * 128:(dc + 1) * 128])
            ps = pss[npz % 3]
            npz += 1
            for hc in range(64):
                p.mm(ps[:, :W], wt[:, hc, :], hh[:, hc, :W], hc == 0, hc == 63)
            p.stt(xt[:, dc, :W], ps[:, :W], mt[:, 80 + dc, r:r + 1], xt[:, dc, :W], ALU.mult, ALU.add)
        p.dma(xov[:, :, t0:t0 + W], xt[:, :, :W])
        if final and r == 0:
            for kc in range(NKC):
                sq = sqb[kc % 2]
                p.act(sq[:, :W], xt[:, kc, :W], AF.Square)
                p.mm(ss_ps[:, :W], ones[:, :], sq[:, :W], kc == 0, kc == NKC - 1)
            p.ts(rstd[:, :W], ss_ps[:, :W], 1.0 / D, EPS, ALU.mult, ALU.add)
            p.act(rstd[:, :W], rstd[:, :W], AF.Sqrt)
            p.recip(rstd[:, :W], rstd[:, :W])
            for kc in range(NKC):
                p.stt(yt[:, kc, :W], xt[:, kc, :W], fnt[:, kc:kc + 1], rstd[:, :W], ALU.mult, ALU.mult)
            p.dma(fov[:, :, t0 - 128:t0 - 128 + W], yt[:, :, :W])
    p.final_wait()
    p.emit()
    return nc


NTOK = 4352
SEQS = [(0, 256, 256), (256, 4096, 64)]
MAGIC = 12582912.0
TWO_PI = 2.0 * np.pi


DEBUG_B = False


def dscr(nc, name, shape):
    if DEBUG_B and (name.startswith("kt_256_0_0") or name == "z1_0" or name == "uc0"):
        return Tile(nc.dram_tensor(name, list(shape), F32, kind="ExternalOutput").ap(), name)
    return Tile(nc.dram_tensor(name, list(shape), F32).ap(), name)


def build_B():
    nc = new_nc()
    uh = din(nc, "uh", [2832, NTOK])
    hcw = din(nc, "hcw", [128, 18, 3]); hcb = din(nc, "hcb", [128, 18])
    hbias = din(nc, "hbias", [128, 2, 4]); hnw = din(nc, "hnw", [128, 4])
    fw1 = din(nc, "fw1", [33, 64]); fb1 = din(nc, "fb1", [64, 1]); ffq = din(nc, "ffq", [64, 1])
    fw2 = din(nc, "fw2", [64, 64]); fb2 = din(nc, "fb2", [64, 1]); fw3 = din(nc, "fw3", [64, 2, 2, 512])
    absd = din(nc, "absd", [128, 512])
    ident = din(nc, "ident", [128, 128]); umat = din(nc, "umat", [128, 2, 128])
    dtb = din(nc, "dtb", [128, 16]); alog = din(nc, "alog", [128, 16]); dsk = din(nc, "dsk", [128, 512]); snw = din(nc, "snw", [128, 512])
    consts = {}
    for (s0, L, RL) in SEQS:
        consts[L] = dict(feats=din(nc, "feats%d" % L, [33, L]), ntnf=din(nc, "ntnf%d" % L, [128, L // 128]),
                         ntnb=din(nc, "ntnb%d" % L, [128, L // 128]), cs=din(nc, "cs%d" % L, [L, L]), ss=din(nc, "ss%d" % L, [L, L]),
                         ph=din(nc, "ph%d" % L, [128, 2, L]))
    yh = dout(nc, "yh", [1024, NTOK])
    p = Prog(nc)
    uc = [dscr(nc, "uc%d" % j, [128, NTOK]) for j in range(18)]
    z1 = [dscr(nc, "z1_%d" % j, [128, NTOK]) for j in range(4)]
    KT = {}
    for (s0, L, RL) in SEQS:
        for o in range(2):
            for cq in range(4):
                for ri in range(2):
                    KT[(L, o, cq, ri)] = dscr(nc, "kt_%d_%d_%d_%d" % (L, o, cq, ri), [128, L])

    idt = p.sb("idt", [128, 128]); p.dma(idt[:], ident[:])
    um = p.sb("um", [128, 2, 128]); p.dma(um[:], umat[:])
    ones = p.sb("ones", [128, 128]); p.memset(ones[:], 1.0)
    cwt = p.sb("cwt", [128, 18, 3]); p.dma(cwt[:], hcw[:])
    cbt = p.sb("cbt", [128, 18]); p.dma(cbt[:], hcb[:])
    hbt = p.sb("hbt", [128, 2, 4]); p.dma(hbt[:], hbias[:])
    hnt = p.sb("hnt", [128, 4]); p.dma(hnt[:], hnw[:])
    big = [p.sb("big%d" % i, [128, 8192 + (8 if i == 3 else 0)]) for i in range(4)]
    mslab = [p.sb("msl%d" % i, [128, 4, 512]) for i in range(2)]
    wk = [p.sb("wk%d" % i, [128, 512]) for i in range(8)]
    psA = [p.ps("psA%d" % i, [128, 512]) for i in range(4)]
    psT = [p.ps("psT%d" % i, [128, 512]) for i in range(2)]
    psS = p.ps("psS", [128, 512])
    cnt = {"wk": 0, "ms": 0, "pt": 0}

    def nwk():
        cnt["wk"] += 1
        return wk[cnt["wk"] % 8]

    def npt():
        cnt["pt"] += 1
        return psT[cnt["pt"] % 2]

    for j in range(18):
        srow = j * 128 if j < 12 else 1536 + (j - 12) * 128
        for (s0, L, RL) in SEQS:
            for b0 in range(0, L, 512):
                W = min(512, L - b0)
                c0 = s0 + b0
                ui = nwk(); uo = nwk()
                p.dma(ui[:, :W], uh[srow:srow + 128, c0:c0 + W])
                p.ts(uo[:, :W], ui[:, :W], cwt[:, j, 1:2], cbt[:, j:j + 1], ALU.mult, ALU.add)
                uiv = ui.ap[:, :W].rearrange("p (a b) -> p a b", b=RL)
                uov = uo.ap[:, :W].rearrange("p (a b) -> p a b", b=RL)
                p.stt(V(uo.buf, uov[:, :, 1:RL]), V(ui.buf, uiv[:, :, 0:RL - 1]), cwt[:, j, 0:1], V(uo.buf, uov[:, :, 1:RL]), ALU.mult, ALU.add)
                p.stt(V(uo.buf, uov[:, :, 0:RL - 1]), V(ui.buf, uiv[:, :, 1:RL]), cwt[:, j, 2:3], V(uo.buf, uov[:, :, 0:RL - 1]), ALU.mult, ALU.add)
                if j >= 12:
                    p.act(uo[:, :W], uo[:, :W], AF.Silu)
                p.dma(uc[j][:, c0:c0 + W], uo[:, :W])

    f1t = p.sb("f1t", [33, 64]); p.dma(f1t[:], fw1[:])
    f2t = p.sb("f2t", [64, 64]); p.dma(f2t[:], fw2[:])
    f3t = p.sb("f3t", [64, 2, 2, 512]); p.dma(f3t[:], fw3[:])
    fbt = p.sb("fbt", [64, 3]); p.dma(fbt[:, 0:1], fb1[:]); p.dma(fbt[:, 1:2], fb2[:]); p.dma(fbt[:, 2:3], ffq[:])
    abt = p.sb("abt", [128, 512]); p.dma(abt[:], absd[:])
    hd1 = Tile(big[3].ap[:64, 0:4096], "hd1", big[3].buf)
    hd2 = Tile(big[3].ap[:64, 4096:8200], "hd2", big[3].buf)
    ntf = p.sb("ntf", [128, 32]); ntb = p.sb("ntb", [128, 32])
    nrm = p.sb("nrm", [128, 2])
    nacc = p.sb("nacc", [128, 2])
    pht = p.sb("pht", [128, 2, 512])

    def sin_layer(dst, src_fn, lhsT, bias_col, L):
        for b0 in range(0, L, 512):
            W = min(512, L - b0)
            ps = npt()
            p.mm(ps[:64, :W], lhsT, src_fn(b0, W), True, True)
            a = nwk(); n = nwk()
            p.ts(a[:64, :W], ps[:64, :W], fbt[:, bias_col:bias_col + 1], fbt[:, 2:3], ALU.add, ALU.mult)
            p.ts(n[:64, :W], a[:64, :W], 1.0 / TWO_PI, MAGIC, ALU.mult, ALU.add)
            p.ts(n[:64, :W], n[:64, :W], MAGIC, None, ALU.subtract)
            p.stt(a[:64, :W], n[:64, :W], -TWO_PI, a[:64, :W], ALU.mult, ALU.add)
            p.act(dst[:, b0:b0 + W], a[:64, :W], AF.Sin)

    def mat_pass(M, L, nacc, body, evac):
        FB = min(512, L)
        NRC = L // 128
        Mv = Tile(M.ap.rearrange("(rc p) f -> p rc f", p=128), "mv", M.buf)
        for fb in range(L // FB):
            for g in range(0, NRC, 4):
                ng = min(4, NRC - g)
                cnt["ms"] += 1
                sl = mslab[cnt["ms"] % 2]
                p.dma(sl[:, :ng, :FB], Mv[:, g:g + ng, fb * FB:(fb + 1) * FB])
                for q in range(ng):
                    rc = g + q
                    body(rc, sl[:, q, :FB], fb, rc == 0, rc == NRC - 1)
            evac(fb, FB)

    for (s0, L, RL) in SEQS:
        C = consts[L]
        NTC = L // 128
        FB = min(512, L)
        p.dma(ntf[:, :NTC], C["ntnf"][:]); p.dma(ntb[:, :NTC], C["ntnb"][:])
        for b0 in range(0, L, 512):
            W = min(512, L - b0)
            ft = nwk()
            p.dma(ft[:33, :W], C["feats"][:, b0:b0 + W])
            ps = npt()
            p.mm(ps[:64, :W], f1t[:, :], ft[:33, :W], True, True)
            a = nwk(); n = nwk()
            p.ts(a[:64, :W], ps[:64, :W], fbt[:, 0:1], fbt[:, 2:3], ALU.add, ALU.mult)
            p.ts(n[:64, :W], a[:64, :W], 1.0 / TWO_PI, MAGIC, ALU.mult, ALU.add)
            p.ts(n[:64, :W], n[:64, :W], MAGIC, None, ALU.subtract)
            p.stt(a[:64, :W], n[:64, :W], -TWO_PI, a[:64, :W], ALU.mult, ALU.add)
            p.act(hd1[:, b0:b0 + W], a[:64, :W], AF.Sin)
        p.memset(hd2[:, L:L + 8], 0.0)
        sin_layer(hd2, lambda b0, W: hd1[:, b0:b0 + W], f2t[:, :], 1, L)

        for cp in range(2):
            ch0 = cp * 256
            for o in range(2):
                ksum, kdiff, prT = big[0], big[1], big[2]
                for tc in range(NTC):
                    pf = npt(); pb = npt()
                    p.mm(pf[:, :256], hd2[:, tc * 128:tc * 128 + 128], f3t[:, o, 0, ch0:ch0 + 256], True, True)
                    p.mm(pb[:, :256], hd2[:, tc * 128 + 1:tc * 128 + 129], f3t[:, o, 1, ch0:ch0 + 256], True, True)
                    wf = nwk(); wbk = nwk()
                    p.act(wf[:, :256], abt[:, ch0:ch0 + 256], AF.Exp, scale=ntf[:, tc:tc + 1])
                    p.act(wbk[:, :256], abt[:, ch0:ch0 + 256], AF.Exp, scale=ntb[:, tc:tc + 1])
                    p.tt(wf[:, :256], pf[:, :256], wf[:, :256], ALU.mult)
                    p.tt(wbk[:, :256], pb[:, :256], wbk[:, :256], ALU.mult)
                    p.tt(ksum[:, tc * 256:(tc + 1) * 256], wf[:, :256], wbk[:, :256], ALU.add)
                    p.tt(kdiff[:, tc * 256:(tc + 1) * 256], wf[:, :256], wbk[:, :256], ALU.subtract)
                    p.act(wf[:, :256], wf[:, :256], AF.Abs)
                    p.act(wbk[:, :256], wbk[:, :256], AF.Abs)
                    p.tt(wf[:, :256], wf[:, :256], wbk[:, :256], ALU.add)
                    for q in range(2):
                        p.mm(psS[:, q * 8:q * 8 + 1], wf[:, q * 128:(q + 1) * 128], ones[:, 0:1], True, True)
                    if tc == 0:
                        p.copy(nacc[:, 0:1], psS[:, 0:1]); p.copy(nacc[:, 1:2], psS[:, 8:9])
                    else:
                        p.tt(nacc[:, 0:1], nacc[:, 0:1], psS[:, 0:1], ALU.add); p.tt(nacc[:, 1:2], nacc[:, 1:2], psS[:, 8:9], ALU.add)
                p.ts(nrm[:, :], nacc[:, 0:2], float(L), EPS * L, ALU.mult, ALU.add)
                p.recip(nrm[:, :], nrm[:, :])

                def bodyP(rc, sl, fb, first, last):
                    for q in range(2):
                        p.mm(psA[q][:, :FB], ksum[:, rc * 256 + q * 128: rc * 256 + (q + 1) * 128], sl, first, last)

                def evacP(fb, FBw):
                    for q in range(2):
                        p.copy(prT[:, q * L + fb * FBw: q * L + (fb + 1) * FBw], psA[q][:, :FBw], eng="act")
                mat_pass(C["cs"], L, 2, bodyP, evacP)

                def bodyQ(rc, sl, fb, first, last):
                    for q in range(2):
                        p.mm(psA[q][:, :FB], kdiff[:, rc * 256 + q * 128: rc * 256 + (q + 1) * 128], sl, first, last)

                def evacQ(fb, FBw, o=o, cp=cp):
                    p.dma(pht[:, :, :FBw], C["ph"][:, :, fb * FBw:(fb + 1) * FBw])
                    for q in range(2):
                        pr = prT[:, q * L + fb * FBw: q * L + (fb + 1) * FBw]
                        t1 = nwk(); t2 = nwk(); kr = nwk(); ki = nwk()
                        p.tt(t1[:, :FBw], pr, pht[:, 0, :FBw], ALU.mult)
                        p.tt(t2[:, :FBw], psA[q][:, :FBw], pht[:, 1, :FBw], ALU.mult)
                        p.tt(t1[:, :FBw], t1[:, :FBw], t2[:, :FBw], ALU.add)
                        p.ts(kr[:, :FBw], t1[:, :FBw], nrm[:, q:q + 1], None, ALU.mult)
                        p.tt(t1[:, :FBw], pr, pht[:, 1, :FBw], ALU.mult)
                        p.tt(t2[:, :FBw], psA[q][:, :FBw], pht[:, 0, :FBw], ALU.mult)
                        p.tt(t1[:, :FBw], t1[:, :FBw], t2[:, :FBw], ALU.subtract)
                        p.ts(ki[:, :FBw], t1[:, :FBw], nrm[:, q:q + 1], None, ALU.mult)
                        p.dma(KT[(L, o, cp * 2 + q, 0)][:, fb * FBw:(fb + 1) * FBw], kr[:, :FBw])
                        p.dma(KT[(L, o, cp * 2 + q, 1)][:, fb * FBw:(fb + 1) * FBw], ki[:, :FBw])
                mat_pass(C["ss"], L, 2, bodyQ, evacQ)

        for cp in range(2):
            for o in range(2):
                ztm, aT, u1f, u2f = big[0], big[1], big[2], big[3]
                zsrc = [uc[cp * 2 + q] if o == 0 else z1[cp * 2 + q] for q in range(2)]
                gate = [uc[4 + 4 * o + cp * 2 + q] for q in range(2)]
                for q in range(2):
                    for b0 in range(0, L, 512):
                        W = min(512, L - b0)
                        zi = nwk()
                        p.dma(zi[:, :W], zsrc[q][:, s0 + b0:s0 + b0 + W])
                        for k in range(W // 128):
                            tc = b0 // 128 + k
                            pt = npt()
                            p.transpose(pt[:, :128], zi[:, k * 128:(k + 1) * 128], idt[:, :])
                            p.copy(ztm[:, tc * 256 + q * 128: tc * 256 + (q + 1) * 128], pt[:, :128], eng="act" if k % 2 else "dve")

                def bodyA(rc, sl, fb, first, last):
                    for q in range(2):
                        p.mm(psA[q][:, :FB], ztm[:, rc * 256 + q * 128: rc * 256 + (q + 1) * 128], sl, first, last)

                def evacA(fb, FBw):
                    for q in range(2):
                        p.copy(aT[:, q * L + fb * FBw: q * L + (fb + 1) * FBw], psA[q][:, :FBw], eng="act")
                mat_pass(C["cs"], L, 2, bodyA, evacA)

                def evacB(fb, FBw, o=o, cp=cp):
                    for q in range(2):
                        kr = nwk(); ki = nwk(); t1 = nwk(); t2 = nwk(); bT = nwk()
                        p.dma(kr[:, :FBw], KT[(L, o, cp * 2 + q, 0)][:, fb * FBw:(fb + 1) * FBw])
                        p.dma(ki[:, :FBw], KT[(L, o, cp * 2 + q, 1)][:, fb * FBw:(fb + 1) * FBw])
                        a_ = aT[:, q * L + fb * FBw: q * L + (fb + 1) * FBw]
                        p.copy(bT[:, :FBw], psA[q][:, :FBw], eng="act")
                        p.tt(t1[:, :FBw], kr[:, :FBw], a_, ALU.mult)
                        p.tt(t2[:, :FBw], ki[:, :FBw], bT[:, :FBw], ALU.mult)
                        p.tt(t1[:, :FBw], t1[:, :FBw], t2[:, :FBw], ALU.add)
                        p.tt(t2[:, :FBw], kr[:, :FBw], bT[:, :FBw], ALU.mult)
                        p.tt(kr[:, :FBw], ki[:, :FBw], a_, ALU.mult)
                        p.tt(t2[:, :FBw], t2[:, :FBw], kr[:, :FBw], ALU.subtract)
                        for k in range(FBw // 128):
                            fc = fb * (FBw // 128) + k
                            for (src, dstb) in ((t1, u1f), (t2, u2f)):
                                pt = npt()
                                p.transpose(pt[:, :128], src[:, k * 128:(k + 1) * 128], idt[:, :])
                                p.copy(dstb[:, fc * 256 + q * 128: fc * 256 + (q + 1) * 128], pt[:, :128], eng="act")
                mat_pass(C["ss"], L, 2, bodyA, evacB)

                FBi = min(512, L)
                NRC = L // 128
                csv = Tile(C["cs"].ap.rearrange("(rc p) f -> p rc f", p=128), "csv", C["cs"].buf)
                ssv = Tile(C["ss"].ap.rearrange("(rc p) f -> p rc f", p=128), "ssv", C["ss"].buf)
                for tb in range(L // FBi):
                    for mi, (Mv, uf) in enumerate(((csv, u1f), (ssv, u2f))):
                        for g in range(0, NRC, 4):
                            ng = min(4, NRC - g)
                            cnt["ms"] += 1
                            sl = mslab[cnt["ms"] % 2]
                            p.dma(sl[:, :ng, :FBi], Mv[:, g:g + ng, tb * FBi:(tb + 1) * FBi])
                            for qq in range(ng):
                                rc = g + qq
                                for q in range(2):
                                    p.mm(psA[2 + q][:, :FBi], uf[:, rc * 256 + q * 128: rc * 256 + (q + 1) * 128], sl[:, qq, :FBi],
                                         mi == 0 and rc == 0, mi == 1 and rc == NRC - 1)
                    c0 = s0 + tb * FBi
                    for q in range(2):
                        cq = cp * 2 + q
                        zi = nwk(); gi = nwk(); r_ = nwk()
                        p.dma(zi[:, :FBi], zsrc[q][:, c0:c0 + FBi])
                        p.dma(gi[:, :FBi], gate[q][:, c0:c0 + FBi])
                        p.stt(r_[:, :FBi], zi[:, :FBi], hbt[:, o, cq:cq + 1], psA[2 + q][:, :FBi], ALU.mult, ALU.add)
                        p.tt(r_[:, :FBi], r_[:, :FBi], gi[:, :FBi], ALU.mult)
                        if o == 0:
                            p.dma(z1[cq][:, c0:c0 + FBi], r_[:, :FBi])
                        else:
                            sq = nwk(); rs = nwk()
                            p.act(sq[:, :FBi], r_[:, :FBi], AF.Square)
                            pt = npt()
                            p.mm(pt[:, :FBi], ones[:, :], sq[:, :FBi], True, True)
                            p.ts(rs[:, :FBi], pt[:, :FBi], 1.0 / 128.0, EPS, ALU.mult, ALU.add)
                            p.act(rs[:, :FBi], rs[:, :FBi], AF.Sqrt)
                            p.recip(rs[:, :FBi], rs[:, :FBi])
                            p.stt(r_[:, :FBi], r_[:, :FBi], hnt[:, cq:cq + 1], rs[:, :FBi], ALU.mult, ALU.mult)
                            p.dma(yh[cq * 128:(cq + 1) * 128, c0:c0 + FBi], r_[:, :FBi])

    dbt = p.sb("dbt", [128, 16]); p.dma(dbt[:], dtb[:])
    an = p.sb("an", [128, 16]); p.dma(an[:], alog[:])
    p.act(an[:], an[:], AF.Exp)
    p.ts(an[:], an[:], -1.0, None, ALU.mult)
    dskt = p.sb("dskt", [128, 512]); p.dma(dskt[:], dsk[:])
    snt = p.sb("snt", [128, 512]); p.dma(snt[:], snw[:])
    NCH = NTOK // 128
    dtt = p.sb("dtt", [128, NCH, 16])
    att = p.sb("att", [128, NCH, 16])
    yac = Tile(big[0].ap, "yac", big[0].buf)
    yac2 = Tile(big[1].ap, "yac2", big[1].buf)
    yac3 = Tile(big[2].ap, "yac3", big[2].buf)

    def ysl(c):
        if c < 16:
            return big[0][:, c * 512:(c + 1) * 512]
        if c < 32:
            return big[1][:, (c - 16) * 512:(c - 15) * 512]
        return big[2][:, (c - 32) * 512:(c - 31) * 512]

    for b0 in range(0, NTOK, 512):
        W = min(512, NTOK - b0)
        di = nwk()
        p.dma(di[:16, :W], uh[2816:2832, b0:b0 + W])
        for k in range(W // 128):
            c = b0 // 128 + k
            pt = npt()
            p.transpose(pt[:, :16], di[:16, k * 128:(k + 1) * 128], idt[:16, :16])
            p.tt(dtt[:, c, :], pt[:, :16], dbt[:, :], ALU.add)
    p.act(dtt[:], dtt[:], AF.Exp)
    p.act(dtt[:], dtt[:], AF.Ln, bias=1.0)
    for c in range(NCH):
        p.tt(att[:, c, :], dtt[:, c, :], an[:, :], ALU.mult)

    ST = p.sb("ST", [128, 512])
    xst = p.sb("xst", [128, 512]); xdt = p.sb("xdt", [128, 512]); xdd = p.sb("xdd", [128, 512])
    btk = p.sb("btk", [128, 128]); bT_ = p.sb("bT_", [128, 128]); cT_ = p.sb("cT_", [128, 128])
    cbm = p.sb("cbm", [128, 128]); cst = p.sb("cst", [128, 8]); tot = p.sb("tot", [128, 8]); dsc = p.sb("dsc", [128, 8]); dte = p.sb("dte", [128, 8])
    psY = psA[0]; psSt = psA[1]; psB = [psA[2], psA[3]]
    hb = [[p.sb("hb%d_%d" % (i, k), [128, 128]) for k in range(5)] for i in range(2)]

    for d in range(2):
        Xm = um[:, d, :]
        p.memset(ST[:], 0.0)
        order = [0, 1] + list(range(2, NCH)) if d == 0 else [1, 0] + list(range(NCH - 1, 1, -1))
        for c in order:
            t0 = c * 128
            for k in range(4):
                xi = nwk()
                p.dma(xi[:, :128], uc[12 + k][:, t0:t0 + 128])
                pt = npt()
                p.transpose(pt[:, :128], xi[:, :128], idt[:, :])
                p.copy(xst[:, k * 128:(k + 1) * 128], pt[:, :128], eng="act")
            p.dma(bT_[:], uc[16][:, t0:t0 + 128])
            p.dma(cT_[:], uc[17][:, t0:t0 + 128])
            pt = npt()
            p.transpose(pt[:, :128], bT_[:, :], idt[:, :])
            p.copy(btk[:], pt[:, :128], eng="act")
            a8 = att[:, c, d * 8:(d + 1) * 8]
            pt = npt()
            p.mm(pt[:, 0:8], Xm, a8, True, True)
            p.copy(cst[:], pt[:, 0:8])
            pt = npt()
            p.mm(pt[:, 0:8], ones[:, :], a8, True, True)
            p.copy(tot[:], pt[:, 0:8])
            p.tt(dte[:], tot[:], cst[:], ALU.subtract)
            p.act(dte[:], dte[:], AF.Exp)
            p.tt(dsc[:], dte[:], dtt[:, c, d * 8:(d + 1) * 8], ALU.mult)
            p.act(tot[:], tot[:], AF.Exp)
            pt = npt()
            p.mm(pt[:, :128], bT_[:, :], cT_[:, :], True, True)
            p.tt(cbm[:], pt[:, :128], Xm, ALU.mult)
            for r in range(8):
                rs_ = slice(r * 64, (r + 1) * 64)
                p.ts(xdt[:, rs_], xst[:, rs_], dtt[:, c, d * 8 + r:d * 8 + r + 1], None, ALU.mult)
                p.ts(xdd[:, rs_], xst[:, rs_], dsc[:, r:r + 1], None, ALU.mult)
            for r in range(8):
                rs_ = slice(r * 64, (r + 1) * 64)
                H = hb[r % 2]
                ax, dm, eb, csr, mT = H
                pb = psB[r % 2]
                p.ts(ax[:], Xm, att[:, c, d * 8 + r:d * 8 + r + 1], None, ALU.mult)
                p.mm(pb[:, :128], ones[:, :], ax[:], True, True)
                p.ts(dm[:], pb[:, :128], cst[:, r:r + 1], 0.0, ALU.subtract, ALU.min)
                p.act(dm[:], dm[:], AF.Exp)
                p.tt(mT[:], dm[:], cbm[:], ALU.mult)
                p.act(eb[:], pb[:, :128], AF.Exp)
                p.tt(csr[:], cT_[:], eb[:], ALU.mult)
                p.mm(psY[:, rs_], mT[:], xdt[:, rs_], True, False)
                p.mm(psY[:, rs_], csr[:], ST[:, rs_], False, True)
            ya = ysl(c)
            if d == 0:
                t_ = nwk()
                p.tt(t_[:, :], xst[:], dskt[:], ALU.mult)
                p.tt(ya, psY[:, :], t_[:, :], ALU.add)
            else:
                p.tt(ya, ya, psY[:, :], ALU.add)
            p.mm(psSt[:, :], btk[:], xdd[:], True, True)
            for r in range(8):
                rs_ = slice(r * 64, (r + 1) * 64)
                p.stt(ST[:, rs_], ST[:, rs_], tot[:, r:r + 1], psSt[:, rs_], ALU.mult, ALU.add)

    ztl = [p.sb("ztl%d" % i, [128, 512]) for i in range(2)]
    for c in range(NCH):
        t0 = c * 128
        zt = ztl[c % 2]
        for k in range(4):
            zi = nwk()
            p.dma(zi[:, :128], uh[2304 + k * 128:2304 + (k + 1) * 128, t0:t0 + 128])
            pt = npt()
            p.transpose(pt[:, :128], zi[:, :128], idt[:, :])
            p.act(zt[:, k * 128:(k + 1) * 128], pt[:, :128], AF.Silu)
        ya = ysl(c)
        p.tt(zt[:, :], zt[:, :], ya, ALU.mult)
        sq = nwk()
        ssq = nwk()
        p.act(sq[:, :], zt[:, :], AF.Square, accum_out=ssq[:, 0:1])
        p.ts(ssq[:, 0:1], ssq[:, 0:1], 1.0 / 512.0, EPS, ALU.mult, ALU.add)
        p.act(ssq[:, 0:1], ssq[:, 0:1], AF.Sqrt)
        p.recip(ssq[:, 0:1], ssq[:, 0:1])
        p.stt(zt[:, :], zt[:, :], ssq[:, 0:1], snt[:, :], ALU.mult, ALU.mult)
        for k in range(4):
            pt = npt()
            p.transpose(pt[:, :128], zt[:, k * 128:(k + 1) * 128], idt[:, :])
            o_ = nwk()
            p.copy(o_[:, :128], pt[:, :128], eng="act")
            p.dma(yh[512 + k * 128:512 + (k + 1) * 128, t0:t0 + 128], o_[:, :128])
    p.final_wait()
    p.emit()
    return nc


def silu_rows_layout(c, c_ctx):
    rows = np.concatenate([c, c_ctx[None, :]], axis=0)
    return np.ascontiguousarray(rows.reshape(5, 16, 128).transpose(2, 1, 0))


def vecT(v):
    return np.ascontiguousarray(v.reshape(-1, 128).T)


def win_perm():
    idx = []
    for h in range(2):
        for base in (0, 1024, 2048):
            idx += list(range(base + h * 512, base + (h + 1) * 512))
        o0 = 3072
        idx += list(range(o0 + h * 512, o0 + (h + 1) * 512))
        idx += list(range(o0 + 1024 + h * 128, o0 + 1024 + (h + 1) * 128))
        idx += list(range(o0 + 1280 + h * 128, o0 + 1280 + (h + 1) * 128))
        o2 = 3072 + 1536 + 32
        idx += list(range(o2 + h * 512, o2 + (h + 1) * 512))
    o1 = 3072 + 1536
    for h in range(2):
        idx += list(range(o1 + h * 8, o1 + (h + 1) * 8))
        idx += list(range(o1 + 16 + h * 8, o1 + 16 + (h + 1) * 8))
    return np.array(idx)


def run_P(inputs):
    cc = silu_rows_layout(inputs["c"], inputs["c_ctx"])
    maps = []
    for j in range(8):
        wada = np.ascontiguousarray(inputs["w_ada"][:, :, j * 1536:(j + 1) * 1536])
        bada = np.ascontiguousarray(inputs["b_ada"][:, j * 1536:(j + 1) * 1536].reshape(4, 12, 128).transpose(2, 0, 1))
        maps.append({"cc": cc, "wada": wada, "bada": bada})
    res = run_bass_kernel_spmd(build_P(), maps, core_ids=list(range(8)))
    parts = [r["modp"] for r in res.results]
    modT = np.concatenate(parts, axis=2)
    return np.ascontiguousarray(modT.transpose(1, 0, 2, 3))


def mod_for_core(modT, l, b):
    return np.ascontiguousarray(modT[l][:, :, [b, 4]])


def b_consts():
    out = {}
    out["ident"] = np.eye(128, dtype=np.float32)
    U = np.triu(np.ones((128, 128), np.float32))
    out["umat"] = np.ascontiguousarray(np.stack([U, U.T], axis=1))
    for L in (256, 4096):
        N = 2 * L
        t = np.linspace(0.0, 1.0, L, dtype=np.float32)
        bands = 16
        w = (2.0 * np.pi * np.arange(L, dtype=np.float32) / L).astype(np.float32)
        f = np.linspace(1e-4, bands - 1, bands, dtype=np.float32)
        fw = (f[None, :] * w[:, None]).astype(np.float32)
        feats = np.concatenate([t[:, None], np.cos(fw), -np.sin(fw)], axis=-1).astype(np.float32)
        out["feats%d" % L] = np.ascontiguousarray(feats.T)
        out["ntnf%d" % L] = np.ascontiguousarray((-t).reshape(L // 128, 128).T)
        tb = np.concatenate([t[1:], [0.0]]).astype(np.float32)
        out["ntnb%d" % L] = np.ascontiguousarray((-tb).reshape(L // 128, 128).T)
        idx = np.arange(L, dtype=np.float64) + 0.5
        ang = 2.0 * np.pi * np.outer(idx, idx) / N
        out["cs%d" % L] = np.cos(ang).astype(np.float32)
        out["ss%d" % L] = np.sin(ang).astype(np.float32)
        wv = 2.0 * np.pi * idx / N
        ph = np.stack([np.cos(wv / 2), np.sin(wv / 2)], axis=0).astype(np.float32)
        out["ph%d" % L] = np.ascontiguousarray(np.broadcast_to(ph[None], (128, 2, L)))
    return out


def b_params(inputs, l, h):
    P = {}
    hw = inputs["hy_conv_w"][l]; hb = inputs["hy_conv_b"][l]
    sw = inputs["ssd_conv_w"][l]; sbb = inputs["ssd_conv_b"][l]
    cw = np.zeros((128, 18, 3), np.float32); cb = np.zeros((128, 18), np.float32)
    for j in range(18):
        if j < 12:
            ch0 = (j // 4) * 1024 + h * 512 + (j % 4) * 128
            cw[:, j, :] = hw[:, ch0:ch0 + 128].T; cb[:, j] = hb[ch0:ch0 + 128]
        else:
            k = j - 12
            ch0 = h * 512 + k * 128 if k < 4 else (1024 + h * 128 if k == 4 else 1280 + h * 128)
            cw[:, j, :] = sw[:, ch0:ch0 + 128].T; cb[:, j] = sbb[ch0:ch0 + 128]
    P["hcw"] = cw; P["hcb"] = cb
    P["hbias"] = np.ascontiguousarray(inputs["hy_bias"][l][:, h * 512:(h + 1) * 512].reshape(2, 4, 128).transpose(2, 0, 1))
    P["hnw"] = np.ascontiguousarray(inputs["hy_norm_w"][l][h * 512:(h + 1) * 512].reshape(4, 128).T)
    P["fw1"] = np.ascontiguousarray(inputs["filt_w1"][l]); P["fb1"] = np.ascontiguousarray(inputs["filt_b1"][l][:, None])
    P["ffq"] = np.ascontiguousarray(inputs["filt_freq"][l][:, None]); P["fw2"] = np.ascontiguousarray(inputs["filt_w2"][l])
    P["fb2"] = np.ascontiguousarray(inputs["filt_b2"][l][:, None])
    P["fw3"] = np.ascontiguousarray(inputs["filt_w3"][l].reshape(64, 2, 2, 1024)[:, :, :, h * 512:(h + 1) * 512])
    deltas = np.linspace(np.log(1e-2) / 1.5, np.log(1e-2) / 0.3, 1024, dtype=np.float32)
    P["absd"] = np.ascontiguousarray(np.broadcast_to(np.abs(deltas)[h * 512:(h + 1) * 512][None], (128, 512)))
    dtb = np.concatenate([inputs["dt_bias"][l][0, h * 8:(h + 1) * 8], inputs["dt_bias"][l][1, h * 8:(h + 1) * 8]])
    al = np.concatenate([inputs["a_log"][l][0, h * 8:(h + 1) * 8], inputs["a_log"][l][1, h * 8:(h + 1) * 8]])
    P["dtb"] = np.ascontiguousarray(np.broadcast_to(dtb[None], (128, 16)))
    P["alog"] = np.ascontiguousarray(np.broadcast_to(al[None], (128, 16)))
    P["dsk"] = np.ascontiguousarray(np.broadcast_to(np.repeat(inputs["ssd_d"][l][h * 8:(h + 1) * 8], 64)[None], (128, 512)))
    P["snw"] = np.ascontiguousarray(np.broadcast_to(inputs["ssd_norm_w"][l][h * 512:(h + 1) * 512][None], (128, 512)))
    return P


def uh_for_core(uT_all, b, h):
    u0 = uT_all[b * 2 + 0]; u1 = uT_all[b * 2 + 1]
    rows = list(range(h * 2816, (h + 1) * 2816)) + list(range(5632 + h * 16, 5632 + (h + 1) * 16))
    ctx = np.concatenate([u0[rows, :128], u1[rows, :128]], axis=1)
    lat = np.concatenate([u0[rows, 128:], u1[rows, 128:]], axis=1)
    return np.ascontiguousarray(np.concatenate([ctx, lat], axis=1))


def yT_for_core(yh_all, b, h):
    y0 = yh_all[b * 2 + 0]; y1 = yh_all[b * 2 + 1]
    full = np.concatenate([y0[:512], y1[:512], y0[512:], y1[512:]], axis=0)
    return np.ascontiguousarray(np.concatenate([full[:, h * 128:(h + 1) * 128], full[:, 256 + h * 2048:256 + (h + 1) * 2048]], axis=1))


_NC_CACHE = {}


def _prog(name, fn):
    if name not in _NC_CACHE:
        _NC_CACHE[name] = fn()
    return _NC_CACHE[name]


def _run(name, fn, maps):
    res = run_bass_kernel_spmd(fn(), maps, core_ids=list(range(8)))
    return res.results


def kernel(**inputs):
    inputs = {k: np.asarray(v, dtype=np.float32) for k, v in inputs.items()}
    modT = run_P(inputs)
    K = b_consts()
    perm = win_perm()
    xT = []
    for core in range(8):
        b, h = core // 2, core % 2
        xT.append(np.ascontiguousarray(np.concatenate(
            [inputs["ctx"][b, h * 128:(h + 1) * 128].T, inputs["x"][b, h * 2048:(h + 1) * 2048].T], axis=1)))
    fnw = vecT(inputs["final_norm_w"])
    out = None
    for l in range(4):
        winp = np.ascontiguousarray(inputs["w_in"][l][:, perm])
        n1 = vecT(inputs["norm1_w"][l]); n2 = vecT(inputs["norm2_w"][l])
        mods = [mod_for_core(modT, l, core // 2) for core in range(8)]
        resA = _run("A", build_A, [{"xT": xT[c], "modT": mods[c], "n1": n1, "win": winp} for c in range(8)])
        uT_all = [r["uT"] for r in resA]
        bp = [b_params(inputs, l, h) for h in range(2)]
        mapsB = []
        for core in range(8):
            b, h = core // 2, core % 2
            m = dict(K); m.update(bp[h]); m["uh"] = uh_for_core(uT_all, b, h)
            mapsB.append(m)
        resB = _run("B", build_B, mapsB)
        yh_all = [r["yh"] for r in resB]
        final = (l == 3)
        mapsC = []
        for core in range(8):
            b, h = core // 2, core % 2
            m = {"xT": xT[core], "yT": yT_for_core(yh_all, b, h), "modT": mods[core], "n2": n2,
                 "wout": inputs["w_out"][l], "w1": inputs["w_mlp1"][l], "w2": inputs["w_mlp2"][l]}
            if final:
                m["fnw"] = fnw
            mapsC.append(m)
        resC = _run("C%d" % final, lambda: build_C(final), mapsC)
        xT = [r["xo"] for r in resC]
        if final:
            out = np.empty((4, 4096, 2048), np.float32)
            for core in range(8):
                b, h = core // 2, core % 2
                out[b, h * 2048:(h + 1) * 2048, :] = resC[core]["fo"].T
    return out
```

```python
import numpy as np
import concourse.bass as bass
import concourse.mybir as mybir
from concourse.bass_utils import run_bass_kernel_spmd

F32 = mybir.dt.float32
ALU = mybir.AluOpType
AF = mybir.ActivationFunctionType
AX = mybir.AxisListType

SAME_ENGINE_SYNC = True
NDMA_SLOTS = 6


class Buf:
    __slots__ = ("name", "w", "r", "untracked", "nodep")

    def __init__(self, name):
        self.name = name
        self.w = None
        self.r = {}
        self.untracked = False
        self.nodep = False


class V:
    __slots__ = ("buf", "ap")

    def __init__(self, buf, ap):
        self.buf = buf
        self.ap = ap

    def __getitem__(self, idx):
        return V(self.buf, self.ap[idx])


class Tile:
    def __init__(self, ap, name, buf=None):
        self.ap = ap
        self.buf = buf if buf is not None else Buf(name)

    def __getitem__(self, idx):
        return V(self.buf, self.ap[idx])

    def sub(self, idx, name):
        return Tile(self.ap[idx], name)


class Prog:
    ENG = ("pe", "dve", "act", "pool", "sp")

    def __init__(self, nc):
        self.nc = nc
        self.items = {e: [] for e in self.ENG}
        self.cnt = {e: 0 for e in self.ENG}
        self.waited = {e: {} for e in self.ENG}
        self.dma_use = {}
        self.dma_rr = {"sp": 0, "pool": 0, "act": 0}
        self.semkeys = set()
        self.out_deps = []
        self.nalloc = 0

    SB_WORDS = 53200

    def init_mem(self):
        self.sb_all = self.nc.alloc_sbuf_tensor("sb_all", [128, self.SB_WORDS], F32)
        self.ps_all = self.nc.alloc_psum_tensor("ps_all", [128, 4096], F32)
        self.sb_base = 0
        self.sb_off = 0
        self.ps_off = 0

    @staticmethod
    def _shape_view(ap, shape):
        if len(shape) == 2:
            return ap
        if len(shape) == 3:
            return ap.rearrange("p (a b) -> p a b", a=shape[1])
        return ap.rearrange("p (a b c) -> p a b c", a=shape[1], b=shape[2])

    def sb(self, name, shape, dtype=F32):
        n = int(np.prod(shape[1:]))
        n = (n + 7) // 8 * 8
        assert self.sb_off + n <= self.SB_WORDS, ("SBUF carve overflow", name, self.sb_off, n)
        ap = self.sb_all[0:shape[0], self.sb_off:self.sb_off + int(np.prod(shape[1:]))]
        self.sb_off += n
        return Tile(self._shape_view(ap, shape), name)

    def ps(self, name, shape, dtype=F32):
        assert shape[1] <= 512 and self.ps_off + 512 <= 4096, ("PSUM carve overflow", name)
        ap = self.ps_all[0:shape[0], self.ps_off:self.ps_off + shape[1]]
        self.ps_off += 512
        return Tile(ap, name)

    def persist_done(self):
        self.sb_base = self.sb_off

    def stage_begin(self):
        self.barrier()
        self.sb_off = self.sb_base
        self.ps_off = 0

    def barrier(self):
        latest = {}
        for e in self.ENG:
            if self.cnt[e] > 0:
                latest[("E", e)] = self.cnt[e]
        for key, uses in self.dma_use.items():
            latest[key] = 16 * uses
        for e in self.ENG:
            for key, val in latest.items():
                if key == ("E", "pe") and e == "pe":
                    continue
                self._need(e, key, val)

    def _need(self, eng, key, val):
        if self.waited[eng].get(key, 0) >= val:
            return
        self.waited[eng][key] = val
        self.items[eng].append(("wait", key, val))

    def _deps(self, eng, reads, writes, is_dma):
        deps = []
        for b in reads:
            if b.w is not None:
                deps.append(b.w)
        for b in writes:
            if b.w is not None:
                deps.append(b.w)
            deps.extend(b.r.items())
        for key, val in deps:
            if key[0] == "E" and key[1] == eng and not is_dma:
                if eng == "pe" or not SAME_ENGINE_SYNC:
                    continue
            self._need(eng, key, val)

    def _mark(self, reads, writes, key, val):
        for b in reads:
            b.r[key] = val
        for b in writes:
            b.w = (key, val)
            b.r = {}

    def op(self, eng, fn, reads, writes):
        reads = [x.buf if not isinstance(x, Buf) else x for x in reads]
        writes = [x.buf if not isinstance(x, Buf) else x for x in writes]
        self._deps(eng, reads, writes, False)
        self.cnt[eng] += 1
        key = ("E", eng)
        self.semkeys.add(key)
        self.items[eng].append(("ins", fn, key, 1))
        self._mark(reads, writes, key, self.cnt[eng])

    def dma(self, out, in_, queue=None, **kw):
        if queue is None:
            queue = "sp" if self.dma_rr["sp"] <= self.dma_rr["pool"] else "pool"
        slot = self.dma_rr[queue] % NDMA_SLOTS
        self.dma_rr[queue] += 1
        key = ("D", queue, slot)
        self.semkeys.add(key)
        uses = self.dma_use.get(key, 0)
        if uses > 0:
            self._need(queue, key, 16 * uses)
        wr = [] if (out.buf.untracked or out.buf.nodep) else [out.buf]
        rd_ = [] if in_.buf.nodep else [in_.buf]
        self._deps(queue, rd_, wr, True)
        self.dma_use[key] = uses + 1
        oap, iap = out.ap, in_.ap
        self.items[queue].append(("ins", lambda e: e.dma_start(out=oap, in_=iap, **kw), key, 16))
        self._mark(rd_, wr, key, 16 * (uses + 1))
        if out.buf.untracked:
            self.out_deps.append((key, 16 * (uses + 1)))

    def mm(self, out, lhsT, rhs, start, stop):
        o, l, r = out.ap, lhsT.ap, rhs.ap
        self.op("pe", lambda e: e.matmul(o, l, r, start=start, stop=stop), [lhsT, rhs], [out])

    def transpose(self, out, in_, ident):
        o, i, d = out.ap, in_.ap, ident.ap
        self.op("pe", lambda e: e.transpose(o, i, d), [in_, ident], [out])

    def act(self, out, in_, func, bias=0.0, scale=1.0, accum_out=None, eng="act"):
        rd = [in_]
        wr = [out]
        b = bias
        s = scale
        if isinstance(bias, V):
            rd.append(bias); b = bias.ap
        if isinstance(scale, V):
            rd.append(scale); s = scale.ap
        kw = {}
        if accum_out is not None:
            wr.append(accum_out); kw["accum_out"] = accum_out.ap
        o, i = out.ap, in_.ap
        self.op("act", lambda e: e.activation(o, i, func, bias=b, scale=s, **kw), rd, wr)

    def tt(self, out, in0, in1, op, eng="dve"):
        o, a, b = out.ap, in0.ap, in1.ap
        self.op(eng, lambda e: e.tensor_tensor(o, a, b, op), [in0, in1], [out])

    def ts(self, out, in0, s1, s2, op0, op1=None, eng="dve", accum_out=None):
        rd = [in0]
        wr = [out]
        a1, a2 = s1, s2
        if isinstance(s1, V):
            rd.append(s1); a1 = s1.ap
        if isinstance(s2, V):
            rd.append(s2); a2 = s2.ap
        o, i = out.ap, in0.ap
        kw = {}
        if accum_out is not None:
            wr.append(accum_out); kw["accum_out"] = accum_out.ap
        if op1 is None:
            self.op(eng, lambda e: e.tensor_scalar(o, i, a1, None, op0, **kw), rd, wr)
        else:
            self.op(eng, lambda e: e.tensor_scalar(o, i, a1, a2, op0, op1, **kw), rd, wr)

    def stt(self, out, in0, scalar, in1, op0, op1, eng="dve"):
        rd = [in0, in1]
        s = scalar
        if isinstance(scalar, V):
            rd.append(scalar); s = scalar.ap
        o, a, b = out.ap, in0.ap, in1.ap
        self.op(eng, lambda e: e.scalar_tensor_tensor(o, a, s, b, op0, op1), rd, [out])

    def copy(self, out, in_, eng="dve"):
        o, i = out.ap, in_.ap
        if eng == "act":
            self.op("act", lambda e: e.copy(o, i), [in_], [out])
        else:
            self.op(eng, lambda e: e.tensor_copy(o, i), [in_], [out])

    def memset(self, out, val, eng="dve"):
        o = out.ap
        self.op(eng, lambda e: e.memset(o, val), [], [out])

    def reduce(self, out, in_, op, axis=AX.X, eng="dve"):
        o, i = out.ap, in_.ap
        self.op(eng, lambda e: e.tensor_reduce(o, i, axis, op), [in_], [out])

    def recip(self, out, in_):
        o, i = out.ap, in_.ap
        self.op("dve", lambda e: e.reciprocal(o, i), [in_], [out])

    def final_wait(self, bufs=()):
        for b in bufs:
            b = b.buf if not isinstance(b, Buf) else b
            if b.w is not None:
                self._need("sp", b.w[0], b.w[1])
        for key, val in self.out_deps:
            self._need("sp", key, val)

    def emit(self):
        nc = self.nc
        sems = {}
        for key in sorted(self.semkeys):
            sems[key] = nc.alloc_semaphore("s_" + "_".join(str(k) for k in key))
        items = self.items

        def replay(eng_obj, lst):
            for it in lst:
                if it[0] == "wait":
                    eng_obj.wait_ge(sems[it[1]], it[2])
                else:
                    it[1](eng_obj).then_inc(sems[it[2]], it[3])

        with nc.Block() as block:
            @block.tensor
            def _(e):
                replay(e, items["pe"])

            @block.vector
            def _(e):
                replay(e, items["dve"])

            @block.scalar
            def _(e):
                replay(e, items["act"])

            @block.gpsimd
            def _(e):
                replay(e, items["pool"])

            @block.sync
            def _(e):
                replay(e, items["sp"])
        return sems


D = 2048
NKC = 16
NTOK = 4352
EPS = 1e-6
PROJ = 5664
DFF = 8192
SEQS = [(0, 256, 256), (256, 4096, 64)]
MAGIC = 12582912.0
TWO_PI = 2.0 * np.pi


def din(nc, name, shape):
    return Tile(nc.dram_tensor(name, list(shape), F32, kind="ExternalInput").ap(), name)


def dout(nc, name, shape):
    t = Tile(nc.dram_tensor(name, list(shape), F32, kind="ExternalOutput").ap(), name)
    t.buf.untracked = True
    return t


def dscr(nc, name, shape, nodep=False):
    t = Tile(nc.dram_tensor(name, list(shape), F32).ap(), name)
    t.buf.nodep = nodep
    return t


def sub(t, ap):
    return Tile(ap, t.buf.name, t.buf)


def stage_P(p, cc, wada, bada, mo):
    p.stage_begin()
    cct = p.sb("cct", [128, 16, 2])
    bt = p.sb("bt", [128, 4, 96])
    wb = [p.sb("wb%d" % i, [128, 16, 512]) for i in range(2)]
    pss = [p.ps("ps%d" % i, [128, 8]) for i in range(2)]
    p.dma(cct[:], cc[:])
    p.dma(bt[:], bada[:])
    p.act(cct[:], cct[:], AF.Silu)
    n = 0
    for l in range(4):
        wv = sub(wada, wada.ap[l].rearrange("(kc p) c -> p kc c", p=128))
        for blk in range(24):
            wt = wb[(l * 24 + blk) % 2]
            p.dma(wt[:], wv[:, :, blk * 512:(blk + 1) * 512])
            for j in range(4):
                ch = blk * 4 + j
                ps = pss[n % 2]
                n += 1
                for kc in range(16):
                    p.mm(ps[:, 0:2], wt[:, kc, j * 128:(j + 1) * 128], cct[:, kc, :], kc == 0, kc == 15)
                p.ts(mo[:, l, ch, :], ps[:, 0:2], bt[:, l, ch:ch + 1], None, ALU.add)


def rms_stats(p, xt, W, ones, sqb, ss_ps, rstd):
    for kc in range(NKC):
        sq = sqb[kc % 2]
        p.act(sq[:, :W], xt[:, kc, :W], AF.Square)
        p.mm(ss_ps[:, :W], ones[:, :], sq[:, :W], kc == 0, kc == NKC - 1)
    p.ts(rstd[:, :W], ss_ps[:, :W], 1.0 / D, EPS, ALU.mult, ALU.add)
    p.act(rstd[:, :W], rstd[:, :W], AF.Sqrt)
    p.recip(rstd[:, :W], rstd[:, :W])


def rms_mod(p, xt, W, hl, gs_col, sh_col, ones, sqb, ss_ps, rstd):
    rms_stats(p, xt, W, ones, sqb, ss_ps, rstd)
    for kc in range(NKC):
        p.stt(hl[:, kc, :W], xt[:, kc, :W], gs_col(kc), rstd[:, :W], ALU.mult, ALU.mult)
        p.act(hl[:, kc, :W], hl[:, kc, :W], AF.Identity, bias=sh_col(kc), scale=1.0)


TILES_A = [(0, 256, 1)] + [(256 + 512 * j, 512, 0) for j in range(8)]


def stage_A(p, ones, xT, mt, n1, win, uT):
    p.stage_begin()
    n1t = p.sb("n1t", [128, 16])
    gs = p.sb("gs", [128, 16, 2])
    p.dma(n1t[:], n1[:])
    for r in range(2):
        p.stt(gs[:, :, r], mt[:, 16:32, r], 1.0, n1t[:, :], ALU.add, ALU.mult)
    xb = [p.sb("xb%d" % i, [128, 16, 512]) for i in range(2)]
    hl = p.sb("hl", [128, 16, 512])
    sqb = [p.sb("sq%d" % i, [128, 512]) for i in range(2)]
    rstd = p.sb("rstd", [128, 512])
    wb = [p.sb("wb%d" % i, [128, 16, 512]) for i in range(2)]
    wdt = p.sb("wdt", [128, 16, 32])
    ob = [p.sb("ob%d" % i, [128, 512]) for i in range(3)]
    ss_ps = p.ps("ss_ps", [128, 512])
    pss = [p.ps("ps%d" % i, [128, 512]) for i in range(3)]
    xv = sub(xT, xT.ap.rearrange("(kc p) t -> p kc t", p=128))
    wv = sub(win, win.ap.rearrange("(kc p) c -> p kc c", p=128))
    p.dma(wdt[:], wv[:, :, 5632:5664])
    nw = 0
    no = 0
    for ti, (t0, W, r) in enumerate(TILES_A):
        xt = xb[ti % 2]
        p.dma(xt[:, :, :W], xv[:, :, t0:t0 + W])
        rms_mod(p, xt, W, hl, lambda kc: gs[:, kc, r:r + 1], lambda kc: mt[:, kc, r:r + 1], ones, sqb, ss_ps, rstd)
        for og in range(12):
            if og < 11:
                wt = wb[nw % 2]
                nw += 1
                p.dma(wt[:], wv[:, :, og * 512:(og + 1) * 512])
                subs = [(wt, j * 128, 128, (og * 4 + j) * 128) for j in range(4)]
            else:
                subs = [(wdt, 0, 32, 5632)]
            for (wt_, c0, M, row0) in subs:
                ps = pss[no % 3]
                o = ob[no % 3]
                for kc in range(NKC):
                    p.mm(ps[:M, :W], wt_[:, kc, c0:c0 + M], hl[:, kc, :W], kc == 0, kc == NKC - 1)
                p.copy(o[:M, :W], ps[:M, :W], eng="act" if no % 2 else "dve")
                p.dma(uT[row0:row0 + M, t0:t0 + W], o[:M, :W])
                no += 1


TILES_C = [(0, 256, 1)] + [(256 + 512 * j, 512, 0) for j in range(8)]


def stage_C(p, ones, xT, yT, mt, n2, wout, w1, w2, xo, fnw=None, fo=None):
    final = fo is not None
    p.stage_begin()
    n2t = p.sb("n2t", [128, 16])
    gs = p.sb("gs", [128, 16, 2])
    p.dma(n2t[:], n2[:])
    for r in range(2):
        p.stt(gs[:, :, r], mt[:, 64:80, r], 1.0, n2t[:, :], ALU.add, ALU.mult)
    if final:
        fnt = p.sb("fnt", [128, 16])
        p.dma(fnt[:], fnw[:])
    WT = 512
    xt = p.sb("xt", [128, 16, WT])
    yt = p.sb("yt", [128, 16, WT])
    hh = p.sb("hh", [128, 32, WT])
    sqb = [p.sb("sq%d" % i, [128, WT]) for i in range(2)]
    rstd = p.sb("rstd", [128, WT])
    rl = [p.sb("rl%d" % i, [128, WT]) for i in range(2)]
    wflat = [p.sb("wb%d" % i, [128, 8192]) for i in range(2)]
    wb = [sub(w, w.ap.rearrange("p (a b) -> p a b", a=16)) for w in wflat]
    wb2 = [sub(w, w.ap.rearrange("p (a b) -> p a b", a=64)) for w in wflat]
    ss_ps = p.ps("ss_ps", [128, WT])
    pss = [p.ps("ps%d" % i, [128, WT]) for i in range(3)]
    xv = sub(xT, xT.ap.rearrange("(kc p) t -> p kc t", p=128))
    yv = sub(yT, yT.ap.rearrange("(kc p) t -> p kc t", p=128))
    xov = sub(xo, xo.ap.rearrange("(kc p) t -> p kc t", p=128))
    wov = sub(wout, wout.ap.rearrange("(kc p) c -> p kc c", p=128))
    w1v = sub(w1, w1.ap.rearrange("(kc p) c -> p kc c", p=128))
    w2v = sub(w2, w2.ap.rearrange("(kc p) c -> p kc c", p=128))
    if final:
        fov = sub(fo, fo.ap.rearrange("(kc p) t -> p kc t", p=128))
    nw = 0
    npz = 0
    for ti, (t0, W, r) in enumerate(TILES_C):
        p.dma(xt[:, :, :W], xv[:, :, t0:t0 + W])
        p.dma(yt[:, :, :W], yv[:, :, t0:t0 + W])
        for dg in range(4):
            wt = wb[nw % 2]
            nw += 1
            p.dma(wt[:], wov[:, :, dg * 512:(dg + 1) * 512])
            for j in range(4):
                dc = dg * 4 + j
                ps = pss[npz % 3]
                npz += 1
                for mc in range(NKC):
                    p.mm(ps[:, :W], wt[:, mc, j * 128:(j + 1) * 128], yt[:, mc, :W], mc == 0, mc == NKC - 1)
                p.stt(xt[:, dc, :W], ps[:, :W], mt[:, 32 + dc, r:r + 1], xt[:, dc, :W], ALU.mult, ALU.add)
        rms_mod(p, xt, W, yt, lambda kc: gs[:, kc, r:r + 1], lambda kc: mt[:, 48 + kc, r:r + 1], ones, sqb, ss_ps, rstd)
        for hf in range(2):
            for hg in range(8):
                wt = wb[nw % 2]
                nw += 1
                c0 = (hf * 8 + hg) * 512
                p.dma(wt[:], w1v[:, :, c0:c0 + 512])
                for j in range(4):
                    hc = hg * 4 + j
                    ps = pss[npz % 3]
                    rr = rl[npz % 2]
                    npz += 1
                    for kc in range(NKC):
                        p.mm(ps[:, :W], wt[:, kc, j * 128:(j + 1) * 128], yt[:, kc, :W], kc == 0, kc == NKC - 1)
                    p.act(rr[:, :W], ps[:, :W], AF.Relu)
                    p.tt(hh[:, hc, :W], rr[:, :W], rr[:, :W], ALU.mult)
            for dg in range(8):
                wt = wb2[nw % 2]
                nw += 1
                for q in range(2):
                    dc = dg * 2 + q
                    p.dma(wt[:, q * 32:(q + 1) * 32, :], w2v[:, hf * 32:(hf + 1) * 32, dc * 128:(dc + 1) * 128])
                for q in range(2):
                    dc = dg * 2 + q
                    ps = pss[npz % 3]
                    npz += 1
                    for hc in range(32):
                        p.mm(ps[:, :W], wt[:, q * 32 + hc, :], hh[:, hc, :W], hc == 0, hc == 31)
                    p.stt(xt[:, dc, :W], ps[:, :W], mt[:, 80 + dc, r:r + 1], xt[:, dc, :W], ALU.mult, ALU.add)
        p.dma(xov[:, :, t0:t0 + W], xt[:, :, :W])
        if final and r == 0:
            rms_stats(p, xt, W, ones, sqb, ss_ps, rstd)
            for kc in range(NKC):
                p.stt(yt[:, kc, :W], xt[:, kc, :W], fnt[:, kc:kc + 1], rstd[:, :W], ALU.mult, ALU.mult)
            p.dma(fov[:, :, t0 - 256:t0 - 256 + W], yt[:, :, :W])


def stage_B(p, nc, l, h, idt, um, ones, uT, yT, PR, consts, uc, z1, KT):
    p.stage_begin()
    rb = h * 2816
    hcw, hcb, hbias, hnw = PR["hcw"], PR["hcb"], PR["hbias"], PR["hnw"]
    fw1, fb1, ffq, fw2, fb2, fw3, absd = PR["fw1"], PR["fb1"], PR["ffq"], PR["fw2"], PR["fb2"], PR["fw3"], PR["absd"]
    dtb, alog, dsk, snw = PR["dtb"], PR["alog"], PR["dsk"], PR["snw"]
    cwt = p.sb("cwt", [128, 18, 3]); p.dma(cwt[:], hcw[:])
    cbt = p.sb("cbt", [128, 18]); p.dma(cbt[:], hcb[:])
    hbt = p.sb("hbt", [128, 2, 4]); p.dma(hbt[:], hbias[:])
    hnt = p.sb("hnt", [128, 4]); p.dma(hnt[:], hnw[:])
    big = [p.sb("big%d" % i, [128, 8192 + (8 if i == 3 else 0)]) for i in range(4)]
    wk = [p.sb("wk%d" % i, [128, 512]) for i in range(8)]
    psA = [p.ps("psA%d" % i, [128, 512]) for i in range(4)]
    psT = [p.ps("psT%d" % i, [128, 512]) for i in range(2)]
    psS = p.ps("psS", [128, 512])
    cnt = {"wk": 0, "ms": 0, "pt": 0}

    def nwk():
        cnt["wk"] += 1
        return wk[cnt["wk"] % 8]

    def npt():
        cnt["pt"] += 1
        return psT[cnt["pt"] % 2]

    for j in range(18):
        srow = j * 128 if j < 12 else 1536 + (j - 12) * 128
        for (s0, L, RL) in SEQS:
            for b0 in range(0, L, 512):
                W = min(512, L - b0)
                c0 = s0 + b0
                ui = nwk(); uo = nwk()
                p.dma(ui[:, :W], uT[rb + srow:rb + srow + 128, c0:c0 + W])
                p.ts(uo[:, :W], ui[:, :W], cwt[:, j, 1:2], cbt[:, j:j + 1], ALU.mult, ALU.add)
                uiv = ui.ap[:, :W].rearrange("p (a b) -> p a b", b=RL)
                uov = uo.ap[:, :W].rearrange("p (a b) -> p a b", b=RL)
                p.stt(V(uo.buf, uov[:, :, 1:RL]), V(ui.buf, uiv[:, :, 0:RL - 1]), cwt[:, j, 0:1], V(uo.buf, uov[:, :, 1:RL]), ALU.mult, ALU.add)
                p.stt(V(uo.buf, uov[:, :, 0:RL - 1]), V(ui.buf, uiv[:, :, 1:RL]), cwt[:, j, 2:3], V(uo.buf, uov[:, :, 0:RL - 1]), ALU.mult, ALU.add)
                if j >= 12:
                    p.act(uo[:, :W], uo[:, :W], AF.Silu)
                p.dma(uc[j][:, c0:c0 + W], uo[:, :W])

    mark_ = p.sb_off
    mslab = [p.sb("msl%d" % i, [128, 4, 512]) for i in range(2)]
    f1t = p.sb("f1t", [33, 64]); p.dma(f1t[:], fw1[:])
    f2t = p.sb("f2t", [64, 64]); p.dma(f2t[:], fw2[:])
    f3t = p.sb("f3t", [64, 2, 2, 512]); p.dma(f3t[:], fw3[:])
    fbt = p.sb("fbt", [64, 3]); p.dma(fbt[:, 0:1], fb1[:]); p.dma(fbt[:, 1:2], fb2[:]); p.dma(fbt[:, 2:3], ffq[:])
    abt = p.sb("abt", [128, 512]); p.dma(abt[:], absd[:])
    hd1 = Tile(big[3].ap[:64, 0:4096], "hd1", big[3].buf)
    hd2 = Tile(big[3].ap[:64, 4096:8200], "hd2", big[3].buf)
    ntf = p.sb("ntf", [128, 32]); ntb = p.sb("ntb", [128, 32])
    nrm = p.sb("nrm", [128, 2])
    nacc = p.sb("nacc", [128, 2])
    pht = p.sb("pht", [128, 2, 512])

    def sin_layer(dst, src_fn, lhsT, bias_col, L):
        for b0 in range(0, L, 512):
            W = min(512, L - b0)
            ps = npt()
            p.mm(ps[:64, :W], lhsT, src_fn(b0, W), True, True)
            a = nwk(); n = nwk()
            p.ts(a[:64, :W], ps[:64, :W], fbt[:, bias_col:bias_col + 1], fbt[:, 2:3], ALU.add, ALU.mult)
            p.ts(n[:64, :W], a[:64, :W], 1.0 / TWO_PI, MAGIC, ALU.mult, ALU.add)
            p.ts(n[:64, :W], n[:64, :W], MAGIC, None, ALU.subtract)
            p.stt(a[:64, :W], n[:64, :W], -TWO_PI, a[:64, :W], ALU.mult, ALU.add)
            p.act(dst[:, b0:b0 + W], a[:64, :W], AF.Sin)

    def mat_pass(M, L, nacc, body, evac):
        FB = min(512, L)
        NRC = L // 128
        Mv = Tile(M.ap.rearrange("(rc p) f -> p rc f", p=128), "mv", M.buf)
        for fb in range(L // FB):
            for g in range(0, NRC, 4):
                ng = min(4, NRC - g)
                cnt["ms"] += 1
                sl = mslab[cnt["ms"] % 2]
                p.dma(sl[:, :ng, :FB], Mv[:, g:g + ng, fb * FB:(fb + 1) * FB])
                for q in range(ng):
                    rc = g + q
                    body(rc, sl[:, q, :FB], fb, rc == 0, rc == NRC - 1)
            evac(fb, FB)

    for (s0, L, RL) in SEQS:
        C = consts[L]
        NTC = L // 128
        FB = min(512, L)
        p.dma(ntf[:, :NTC], C["ntnf"][:]); p.dma(ntb[:, :NTC], C["ntnb"][:])
        for b0 in range(0, L, 512):
            W = min(512, L - b0)
            ft = nwk()
            p.dma(ft[:33, :W], C["feats"][:, b0:b0 + W])
            ps = npt()
            p.mm(ps[:64, :W], f1t[:, :], ft[:33, :W], True, True)
            a = nwk(); n = nwk()
            p.ts(a[:64, :W], ps[:64, :W], fbt[:, 0:1], fbt[:, 2:3], ALU.add, ALU.mult)
            p.ts(n[:64, :W], a[:64, :W], 1.0 / TWO_PI, MAGIC, ALU.mult, ALU.add)
            p.ts(n[:64, :W], n[:64, :W], MAGIC, None, ALU.subtract)
            p.stt(a[:64, :W], n[:64, :W], -TWO_PI, a[:64, :W], ALU.mult, ALU.add)
            p.act(hd1[:, b0:b0 + W], a[:64, :W], AF.Sin)
        p.memset(hd2[:, L:L + 8], 0.0)
        sin_layer(hd2, lambda b0, W: hd1[:, b0:b0 + W], f2t[:, :], 1, L)

        for cp in range(2):
            ch0 = cp * 256
            for o in range(2):
                ksum, kdiff, prT = big[0], big[1], big[2]
                for tc in range(NTC):
                    pf = npt(); pb = npt()
                    p.mm(pf[:, :256], hd2[:, tc * 128:tc * 128 + 128], f3t[:, o, 0, ch0:ch0 + 256], True, True)
                    p.mm(pb[:, :256], hd2[:, tc * 128 + 1:tc * 128 + 129], f3t[:, o, 1, ch0:ch0 + 256], True, True)
                    wf = nwk(); wbk = nwk()
                    p.act(wf[:, :256], abt[:, ch0:ch0 + 256], AF.Exp, scale=ntf[:, tc:tc + 1])
                    p.act(wbk[:, :256], abt[:, ch0:ch0 + 256], AF.Exp, scale=ntb[:, tc:tc + 1])
                    p.tt(wf[:, :256], pf[:, :256], wf[:, :256], ALU.mult)
                    p.tt(wbk[:, :256], pb[:, :256], wbk[:, :256], ALU.mult)
                    p.tt(ksum[:, tc * 256:(tc + 1) * 256], wf[:, :256], wbk[:, :256], ALU.add)
                    p.tt(kdiff[:, tc * 256:(tc + 1) * 256], wf[:, :256], wbk[:, :256], ALU.subtract)
                    p.act(wf[:, :256], wf[:, :256], AF.Abs)
                    p.act(wbk[:, :256], wbk[:, :256], AF.Abs)
                    p.tt(wf[:, :256], wf[:, :256], wbk[:, :256], ALU.add)
                    for q in range(2):
                        p.mm(psS[:, q * 8:q * 8 + 1], wf[:, q * 128:(q + 1) * 128], ones[:, 0:1], True, True)
                    if tc == 0:
                        p.copy(nacc[:, 0:1], psS[:, 0:1]); p.copy(nacc[:, 1:2], psS[:, 8:9])
                    else:
                        p.tt(nacc[:, 0:1], nacc[:, 0:1], psS[:, 0:1], ALU.add); p.tt(nacc[:, 1:2], nacc[:, 1:2], psS[:, 8:9], ALU.add)
                p.ts(nrm[:, :], nacc[:, 0:2], float(L), EPS * L, ALU.mult, ALU.add)
                p.recip(nrm[:, :], nrm[:, :])

                def bodyP(rc, sl, fb, first, last):
                    for q in range(2):
                        p.mm(psA[q][:, :FB], ksum[:, rc * 256 + q * 128: rc * 256 + (q + 1) * 128], sl, first, last)

                def evacP(fb, FBw):
                    for q in range(2):
                        p.copy(prT[:, q * L + fb * FBw: q * L + (fb + 1) * FBw], psA[q][:, :FBw], eng="act")
                mat_pass(C["cs"], L, 2, bodyP, evacP)

                def bodyQ(rc, sl, fb, first, last):
                    for q in range(2):
                        p.mm(psA[q][:, :FB], kdiff[:, rc * 256 + q * 128: rc * 256 + (q + 1) * 128], sl, first, last)

                def evacQ(fb, FBw, o=o, cp=cp):
                    p.dma(pht[:, :, :FBw], C["ph"][:, :, fb * FBw:(fb + 1) * FBw])
                    for q in range(2):
                        pr = prT[:, q * L + fb * FBw: q * L + (fb + 1) * FBw]
                        t1 = nwk(); t2 = nwk(); kr = nwk(); ki = nwk()
                        p.tt(t1[:, :FBw], pr, pht[:, 0, :FBw], ALU.mult)
                        p.tt(t2[:, :FBw], psA[q][:, :FBw], pht[:, 1, :FBw], ALU.mult)
                        p.tt(t1[:, :FBw], t1[:, :FBw], t2[:, :FBw], ALU.add)
                        p.ts(kr[:, :FBw], t1[:, :FBw], nrm[:, q:q + 1], None, ALU.mult)
                        p.tt(t1[:, :FBw], pr, pht[:, 1, :FBw], ALU.mult)
                        p.tt(t2[:, :FBw], psA[q][:, :FBw], pht[:, 0, :FBw], ALU.mult)
                        p.tt(t1[:, :FBw], t1[:, :FBw], t2[:, :FBw], ALU.subtract)
                        p.ts(ki[:, :FBw], t1[:, :FBw], nrm[:, q:q + 1], None, ALU.mult)
                        p.dma(KT[(L, o, cp * 2 + q, 0)][:, fb * FBw:(fb + 1) * FBw], kr[:, :FBw])
                        p.dma(KT[(L, o, cp * 2 + q, 1)][:, fb * FBw:(fb + 1) * FBw], ki[:, :FBw])
                mat_pass(C["ss"], L, 2, bodyQ, evacQ)

        for cp in range(2):
            for o in range(2):
                ztm, aT, u1f, u2f = big[0], big[1], big[2], big[3]
                zsrc = [uc[cp * 2 + q] if o == 0 else z1[cp * 2 + q] for q in range(2)]
                gate = [uc[4 + 4 * o + cp * 2 + q] for q in range(2)]
                for q in range(2):
                    for b0 in range(0, L, 512):
                        W = min(512, L - b0)
                        zi = nwk()
                        p.dma(zi[:, :W], zsrc[q][:, s0 + b0:s0 + b0 + W])
                        for k in range(W // 128):
                            tc = b0 // 128 + k
                            pt = npt()
                            p.transpose(pt[:, :128], zi[:, k * 128:(k + 1) * 128], idt[:, :])
                            p.copy(ztm[:, tc * 256 + q * 128: tc * 256 + (q + 1) * 128], pt[:, :128], eng="act" if k % 2 else "dve")

                def bodyA(rc, sl, fb, first, last):
                    for q in range(2):
                        p.mm(psA[q][:, :FB], ztm[:, rc * 256 + q * 128: rc * 256 + (q + 1) * 128], sl, first, last)

                def evacA(fb, FBw):
                    for q in range(2):
                        p.copy(aT[:, q * L + fb * FBw: q * L + (fb + 1) * FBw], psA[q][:, :FBw], eng="act")
                mat_pass(C["cs"], L, 2, bodyA, evacA)

                def evacB(fb, FBw, o=o, cp=cp):
                    for q in range(2):
                        kr = nwk(); ki = nwk(); t1 = nwk(); t2 = nwk(); bT = nwk()
                        p.dma(kr[:, :FBw], KT[(L, o, cp * 2 + q, 0)][:, fb * FBw:(fb + 1) * FBw])
                        p.dma(ki[:, :FBw], KT[(L, o, cp * 2 + q, 1)][:, fb * FBw:(fb + 1) * FBw])
                        a_ = aT[:, q * L + fb * FBw: q * L + (fb + 1) * FBw]
                        p.copy(bT[:, :FBw], psA[q][:, :FBw], eng="act")
                        p.tt(t1[:, :FBw], kr[:, :FBw], a_, ALU.mult)
                        p.tt(t2[:, :FBw], ki[:, :FBw], bT[:, :FBw], ALU.mult)
                        p.tt(t1[:, :FBw], t1[:, :FBw], t2[:, :FBw], ALU.add)
                        p.tt(t2[:, :FBw], kr[:, :FBw], bT[:, :FBw], ALU.mult)
                        p.tt(kr[:, :FBw], ki[:, :FBw], a_, ALU.mult)
                        p.tt(t2[:, :FBw], t2[:, :FBw], kr[:, :FBw], ALU.subtract)
                        for k in range(FBw // 128):
                            fc = fb * (FBw // 128) + k
                            for (src, dstb) in ((t1, u1f), (t2, u2f)):
                                pt = npt()
                                p.transpose(pt[:, :128], src[:, k * 128:(k + 1) * 128], idt[:, :])
                                p.copy(dstb[:, fc * 256 + q * 128: fc * 256 + (q + 1) * 128], pt[:, :128], eng="act")
                mat_pass(C["ss"], L, 2, bodyA, evacB)

                FBi = min(512, L)
                NRC = L // 128
                csv = Tile(C["cs"].ap.rearrange("(rc p) f -> p rc f", p=128), "csv", C["cs"].buf)
                ssv = Tile(C["ss"].ap.rearrange("(rc p) f -> p rc f", p=128), "ssv", C["ss"].buf)
                for tb in range(L // FBi):
                    for mi, (Mv, uf) in enumerate(((csv, u1f), (ssv, u2f))):
                        for g in range(0, NRC, 4):
                            ng = min(4, NRC - g)
                            cnt["ms"] += 1
                            sl = mslab[cnt["ms"] % 2]
                            p.dma(sl[:, :ng, :FBi], Mv[:, g:g + ng, tb * FBi:(tb + 1) * FBi])
                            for qq in range(ng):
                                rc = g + qq
                                for q in range(2):
                                    p.mm(psA[2 + q][:, :FBi], uf[:, rc * 256 + q * 128: rc * 256 + (q + 1) * 128], sl[:, qq, :FBi],
                                         mi == 0 and rc == 0, mi == 1 and rc == NRC - 1)
                    c0 = s0 + tb * FBi
                    for q in range(2):
                        cq = cp * 2 + q
                        zi = nwk(); gi = nwk(); r_ = nwk()
                        p.dma(zi[:, :FBi], zsrc[q][:, c0:c0 + FBi])
                        p.dma(gi[:, :FBi], gate[q][:, c0:c0 + FBi])
                        p.stt(r_[:, :FBi], zi[:, :FBi], hbt[:, o, cq:cq + 1], psA[2 + q][:, :FBi], ALU.mult, ALU.add)
                        p.tt(r_[:, :FBi], r_[:, :FBi], gi[:, :FBi], ALU.mult)
                        if o == 0:
                            p.dma(z1[cq][:, c0:c0 + FBi], r_[:, :FBi])
                        else:
                            sq = nwk(); rs = nwk()
                            p.act(sq[:, :FBi], r_[:, :FBi], AF.Square)
                            pt = npt()
                            p.mm(pt[:, :FBi], ones[:, :], sq[:, :FBi], True, True)
                            p.ts(rs[:, :FBi], pt[:, :FBi], 1.0 / 128.0, EPS, ALU.mult, ALU.add)
                            p.act(rs[:, :FBi], rs[:, :FBi], AF.Sqrt)
                            p.recip(rs[:, :FBi], rs[:, :FBi])
                            p.stt(r_[:, :FBi], r_[:, :FBi], hnt[:, cq:cq + 1], rs[:, :FBi], ALU.mult, ALU.mult)
                            p.dma(yT[h * 512 + cq * 128:h * 512 + (cq + 1) * 128, c0:c0 + FBi], r_[:, :FBi])

    p.barrier()
    p.sb_off = mark_
    dbt = p.sb("dbt", [128, 16]); p.dma(dbt[:], dtb[:])
    an = p.sb("an", [128, 16]); p.dma(an[:], alog[:])
    p.act(an[:], an[:], AF.Exp)
    p.ts(an[:], an[:], -1.0, None, ALU.mult)
    dskt = p.sb("dskt", [128, 512]); p.dma(dskt[:], dsk[:])
    snt = p.sb("snt", [128, 512]); p.dma(snt[:], snw[:])
    NCH = NTOK // 128
    dtt = p.sb("dtt", [128, NCH, 16])
    att = p.sb("att", [128, NCH, 16])
    yac = Tile(big[0].ap, "yac", big[0].buf)
    yac2 = Tile(big[1].ap, "yac2", big[1].buf)
    yac3 = Tile(big[2].ap, "yac3", big[2].buf)

    def ysl(c):
        if c < 16:
            return big[0][:, c * 512:(c + 1) * 512]
        if c < 32:
            return big[1][:, (c - 16) * 512:(c - 15) * 512]
        return big[2][:, (c - 32) * 512:(c - 31) * 512]

    for b0 in range(0, NTOK, 512):
        W = min(512, NTOK - b0)
        di = nwk()
        p.dma(di[:16, :W], uT[5632 + h * 16:5632 + (h + 1) * 16, b0:b0 + W])
        for k in range(W // 128):
            c = b0 // 128 + k
            pt = npt()
            p.transpose(pt[:, :16], di[:16, k * 128:(k + 1) * 128], idt[:16, :16])
            p.tt(dtt[:, c, :], pt[:, :16], dbt[:, :], ALU.add)
    p.act(dtt[:], dtt[:], AF.Exp)
    p.act(dtt[:], dtt[:], AF.Ln, bias=1.0)
    for c in range(NCH):
        p.tt(att[:, c, :], dtt[:, c, :], an[:, :], ALU.mult)

    ST = p.sb("ST", [128, 512])
    xst = p.sb("xst", [128, 512]); xdt = p.sb("xdt", [128, 512]); xdd = p.sb("xdd", [128, 512])
    btk = p.sb("btk", [128, 128]); bT_ = p.sb("bT_", [128, 128]); cT_ = p.sb("cT_", [128, 128])
    cbm = p.sb("cbm", [128, 128]); cst = p.sb("cst", [128, 8]); tot = p.sb("tot", [128, 8]); dsc = p.sb("dsc", [128, 8]); dte = p.sb("dte", [128, 8])
    psY = psA[0]; psSt = psA[1]; psB = [psA[2], psA[3]]
    hb = [[p.sb("hb%d_%d" % (i, k), [128, 128]) for k in range(5)] for i in range(2)]

    for d in range(2):
        Xm = um[:, d, :]
        p.memset(ST[:], 0.0)
        order = [0, 1] + list(range(2, NCH)) if d == 0 else [1, 0] + list(range(NCH - 1, 1, -1))
        for c in order:
            t0 = c * 128
            for k in range(4):
                xi = nwk()
                p.dma(xi[:, :128], uc[12 + k][:, t0:t0 + 128])
                pt = npt()
                p.transpose(pt[:, :128], xi[:, :128], idt[:, :])
                p.copy(xst[:, k * 128:(k + 1) * 128], pt[:, :128], eng="act")
            p.dma(bT_[:], uc[16][:, t0:t0 + 128])
            p.dma(cT_[:], uc[17][:, t0:t0 + 128])
            pt = npt()
            p.transpose(pt[:, :128], bT_[:, :], idt[:, :])
            p.copy(btk[:], pt[:, :128], eng="act")
            a8 = att[:, c, d * 8:(d + 1) * 8]
            pt = npt()
            p.mm(pt[:, 0:8], Xm, a8, True, True)
            p.copy(cst[:], pt[:, 0:8])
            pt = npt()
            p.mm(pt[:, 0:8], ones[:, :], a8, True, True)
            p.copy(tot[:], pt[:, 0:8])
            p.tt(dte[:], tot[:], cst[:], ALU.subtract)
            p.act(dte[:], dte[:], AF.Exp)
            p.tt(dsc[:], dte[:], dtt[:, c, d * 8:(d + 1) * 8], ALU.mult)
            p.act(tot[:], tot[:], AF.Exp)
            pt = npt()
            p.mm(pt[:, :128], bT_[:, :], cT_[:, :], True, True)
            p.tt(cbm[:], pt[:, :128], Xm, ALU.mult)
            for r in range(8):
                rs_ = slice(r * 64, (r + 1) * 64)
                p.ts(xdt[:, rs_], xst[:, rs_], dtt[:, c, d * 8 + r:d * 8 + r + 1], None, ALU.mult)
                p.ts(xdd[:, rs_], xst[:, rs_], dsc[:, r:r + 1], None, ALU.mult)
            for r in range(8):
                rs_ = slice(r * 64, (r + 1) * 64)
                H = hb[r % 2]
                ax, dm, eb, csr, mT = H
                pb = psB[r % 2]
                p.ts(ax[:], Xm, att[:, c, d * 8 + r:d * 8 + r + 1], None, ALU.mult)
                p.mm(pb[:, :128], ones[:, :], ax[:], True, True)
                p.ts(dm[:], pb[:, :128], cst[:, r:r + 1], 0.0, ALU.subtract, ALU.min)
                p.act(dm[:], dm[:], AF.Exp)
                p.tt(mT[:], dm[:], cbm[:], ALU.mult)
                p.act(eb[:], pb[:, :128], AF.Exp)
                p.tt(csr[:], cT_[:], eb[:], ALU.mult)
                p.mm(psY[:, rs_], mT[:], xdt[:, rs_], True, False)
                p.mm(psY[:, rs_], csr[:], ST[:, rs_], False, True)
            ya = ysl(c)
            if d == 0:
                t_ = nwk()
                p.tt(t_[:, :], xst[:], dskt[:], ALU.mult)
                p.tt(ya, psY[:, :], t_[:, :], ALU.add)
            else:
                p.tt(ya, ya, psY[:, :], ALU.add)
            p.mm(psSt[:, :], btk[:], xdd[:], True, True)
            for r in range(8):
                rs_ = slice(r * 64, (r + 1) * 64)
                p.stt(ST[:, rs_], ST[:, rs_], tot[:, r:r + 1], psSt[:, rs_], ALU.mult, ALU.add)

    ztl = [p.sb("ztl%d" % i, [128, 512]) for i in range(2)]
    for c in range(NCH):
        t0 = c * 128
        zt = ztl[c % 2]
        for k in range(4):
            zi = nwk()
            p.dma(zi[:, :128], uT[rb + 2304 + k * 128:rb + 2304 + (k + 1) * 128, t0:t0 + 128])
            pt = npt()
            p.transpose(pt[:, :128], zi[:, :128], idt[:, :])
            p.act(zt[:, k * 128:(k + 1) * 128], pt[:, :128], AF.Silu)
        ya = ysl(c)
        p.tt(zt[:, :], zt[:, :], ya, ALU.mult)
        sq = nwk()
        ssq = nwk()
        p.act(sq[:, :], zt[:, :], AF.Square, accum_out=ssq[:, 0:1])
        p.ts(ssq[:, 0:1], ssq[:, 0:1], 1.0 / 512.0, EPS, ALU.mult, ALU.add)
        p.act(ssq[:, 0:1], ssq[:, 0:1], AF.Sqrt)
        p.recip(ssq[:, 0:1], ssq[:, 0:1])
        p.stt(zt[:, :], zt[:, :], ssq[:, 0:1], snt[:, :], ALU.mult, ALU.mult)
        for k in range(4):
            pt = npt()
            p.transpose(pt[:, :128], zt[:, k * 128:(k + 1) * 128], idt[:, :])
            o_ = nwk()
            p.copy(o_[:, :128], pt[:, :128], eng="act")
            p.dma(yT[1024 + h * 512 + k * 128:1024 + h * 512 + (k + 1) * 128, t0:t0 + 128], o_[:, :128])


B_PARAM_SHAPES = {"hcw": [128, 18, 3], "hcb": [128, 18], "hbias": [128, 2, 4], "hnw": [128, 4], "fw1": [33, 64], "fb1": [64, 1],
                  "ffq": [64, 1], "fw2": [64, 64], "fb2": [64, 1], "fw3": [64, 2, 2, 512], "absd": [128, 512],
                  "dtb": [128, 16], "alog": [128, 16], "dsk": [128, 512], "snw": [128, 512]}


def build_fused(nlayers=4, debug=False):
    nc = bass.Bass("TRN2", target_bir_lowering=False)
    xT0 = din(nc, "xT0", [D, NTOK])
    cc = din(nc, "cc", [128, 16, 2])
    wada = din(nc, "wada", [4, D, 12288]); bada = din(nc, "bada", [128, 4, 96])
    n1a = din(nc, "n1a", [4, 128, 16]); n2a = din(nc, "n2a", [4, 128, 16]); fnw = din(nc, "fnw", [128, 16])
    win = din(nc, "win", [4, D, PROJ]); wout = din(nc, "wout", [4, D, D]); w1 = din(nc, "w1", [4, D, DFF]); w2 = din(nc, "w2", [4, DFF, D])
    BP = {k: din(nc, "bp_" + k, [4, 2] + shp) for k, shp in B_PARAM_SHAPES.items()}
    ident = din(nc, "ident", [128, 128]); umat = din(nc, "umat", [128, 2, 128])
    consts = {}
    for (s0, L, RL) in SEQS:
        consts[L] = dict(feats=din(nc, "feats%d" % L, [33, L]), ntnf=din(nc, "ntnf%d" % L, [128, L // 128]),
                         ntnb=din(nc, "ntnb%d" % L, [128, L // 128]), cs=din(nc, "cs%d" % L, [L, L]), ss=din(nc, "ss%d" % L, [L, L]),
                         ph=din(nc, "ph%d" % L, [128, 2, L]))
    fo = dout(nc, "fo", [D, 4096])
    X = [dscr(nc, "X%d" % i, [D, NTOK], nodep=True) for i in range(2)]
    if debug:
        X[0] = dout(nc, "xdbg", [D, NTOK])
    uT = dscr(nc, "uT", [PROJ, NTOK], nodep=True)
    yT = dscr(nc, "yT", [D, NTOK], nodep=True)
    uc = [dscr(nc, "uc%d" % j, [128, NTOK]) for j in range(18)]
    z1 = [dscr(nc, "z1_%d" % j, [128, NTOK]) for j in range(4)]
    KT = {}
    for (s0, L, RL) in SEQS:
        for o in range(2):
            for cq in range(4):
                for ri in range(2):
                    KT[(L, o, cq, ri)] = dscr(nc, "kt_%d_%d_%d_%d" % (L, o, cq, ri), [128, L])
    xT0.buf.nodep = True

    p = Prog(nc)
    p.init_mem()
    mo = p.sb("mo", [128, 4, 96, 2])
    ones = p.sb("ones", [128, 128]); idt = p.sb("idt", [128, 128]); um = p.sb("um", [128, 2, 128])
    p.persist_done()
    p.memset(ones[:], 1.0)
    p.dma(idt[:], ident[:]); p.dma(um[:], umat[:])
    stage_P(p, cc, wada, bada, mo)
    xin = xT0
    for l in range(nlayers):
        mt = sub(mo, mo.ap[:, l])
        stage_A(p, ones, xin, mt, sub(n1a, n1a.ap[l]), sub(win, win.ap[l]), uT)
        for h in range(2):
            PR = {k: sub(t, t.ap[l, h]) for k, t in BP.items()}
            stage_B(p, nc, l, h, idt, um, ones, uT, yT, PR, consts, uc, z1, KT)
        last = (l == nlayers - 1)
        xo = X[l % 2]
        stage_C(p, ones, xin, yT, mt, sub(n2a, n2a.ap[l]), sub(wout, wout.ap[l]), sub(w1, w1.ap[l]), sub(w2, w2.ap[l]), xo,
                fnw=fnw if last else None, fo=fo if last else None)
        xin = xo
    p.barrier()
    p.final_wait()
    p.emit()
    return nc


def win_perm():
    idx = []
    for h in range(2):
        for base in (0, 1024, 2048):
            idx += list(range(base + h * 512, base + (h + 1) * 512))
        o0 = 3072
        idx += list(range(o0 + h * 512, o0 + (h + 1) * 512))
        idx += list(range(o0 + 1024 + h * 128, o0 + 1024 + (h + 1) * 128))
        idx += list(range(o0 + 1280 + h * 128, o0 + 1280 + (h + 1) * 128))
        o2 = 3072 + 1536 + 32
        idx += list(range(o2 + h * 512, o2 + (h + 1) * 512))
    o1 = 3072 + 1536
    for h in range(2):
        idx += list(range(o1 + h * 8, o1 + (h + 1) * 8))
        idx += list(range(o1 + 16 + h * 8, o1 + 16 + (h + 1) * 8))
    return np.array(idx)


def b_consts():
    out = {}
    out["ident"] = np.eye(128, dtype=np.float32)
    U = np.triu(np.ones((128, 128), np.float32))
    out["umat"] = np.ascontiguousarray(np.stack([U, U.T], axis=1))
    for L in (256, 4096):
        N = 2 * L
        t = np.linspace(0.0, 1.0, L, dtype=np.float32)
        bands = 16
        w = (2.0 * np.pi * np.arange(L, dtype=np.float32) / L).astype(np.float32)
        f = np.linspace(1e-4, bands - 1, bands, dtype=np.float32)
        fw = (f[None, :] * w[:, None]).astype(np.float32)
        feats = np.concatenate([t[:, None], np.cos(fw), -np.sin(fw)], axis=-1).astype(np.float32)
        out["feats%d" % L] = np.ascontiguousarray(feats.T)
        out["ntnf%d" % L] = np.ascontiguousarray((-t).reshape(L // 128, 128).T)
        tb = np.concatenate([t[1:], [0.0]]).astype(np.float32)
        out["ntnb%d" % L] = np.ascontiguousarray((-tb).reshape(L // 128, 128).T)
        idx = np.arange(L, dtype=np.float64) + 0.5
        ang = 2.0 * np.pi * np.outer(idx, idx) / N
        out["cs%d" % L] = np.cos(ang).astype(np.float32)
        out["ss%d" % L] = np.sin(ang).astype(np.float32)
        wv = 2.0 * np.pi * idx / N
        ph = np.stack([np.cos(wv / 2), np.sin(wv / 2)], axis=0).astype(np.float32)
        out["ph%d" % L] = np.ascontiguousarray(np.broadcast_to(ph[None], (128, 2, L)))
    return out


def b_params(inputs, l, h):
    P = {}
    hw = inputs["hy_conv_w"][l]; hb = inputs["hy_conv_b"][l]
    sw = inputs["ssd_conv_w"][l]; sbb = inputs["ssd_conv_b"][l]
    cw = np.zeros((128, 18, 3), np.float32); cb = np.zeros((128, 18), np.float32)
    for j in range(18):
        if j < 12:
            ch0 = (j // 4) * 1024 + h * 512 + (j % 4) * 128
            cw[:, j, :] = hw[:, ch0:ch0 + 128].T; cb[:, j] = hb[ch0:ch0 + 128]
        else:
            k = j - 12
            ch0 = h * 512 + k * 128 if k < 4 else (1024 + h * 128 if k == 4 else 1280 + h * 128)
            cw[:, j, :] = sw[:, ch0:ch0 + 128].T; cb[:, j] = sbb[ch0:ch0 + 128]
    P["hcw"] = cw; P["hcb"] = cb
    P["hbias"] = np.ascontiguousarray(inputs["hy_bias"][l][:, h * 512:(h + 1) * 512].reshape(2, 4, 128).transpose(2, 0, 1))
    P["hnw"] = np.ascontiguousarray(inputs["hy_norm_w"][l][h * 512:(h + 1) * 512].reshape(4, 128).T)
    P["fw1"] = np.ascontiguousarray(inputs["filt_w1"][l]); P["fb1"] = np.ascontiguousarray(inputs["filt_b1"][l][:, None])
    P["ffq"] = np.ascontiguousarray(inputs["filt_freq"][l][:, None]); P["fw2"] = np.ascontiguousarray(inputs["filt_w2"][l])
    P["fb2"] = np.ascontiguousarray(inputs["filt_b2"][l][:, None])
    P["fw3"] = np.ascontiguousarray(inputs["filt_w3"][l].reshape(64, 2, 2, 1024)[:, :, :, h * 512:(h + 1) * 512])
    deltas = np.linspace(np.log(1e-2) / 1.5, np.log(1e-2) / 0.3, 1024, dtype=np.float32)
    P["absd"] = np.ascontiguousarray(np.broadcast_to(np.abs(deltas)[h * 512:(h + 1) * 512][None], (128, 512)))
    dtb = np.concatenate([inputs["dt_bias"][l][0, h * 8:(h + 1) * 8], inputs["dt_bias"][l][1, h * 8:(h + 1) * 8]])
    al = np.concatenate([inputs["a_log"][l][0, h * 8:(h + 1) * 8], inputs["a_log"][l][1, h * 8:(h + 1) * 8]])
    P["dtb"] = np.ascontiguousarray(np.broadcast_to(dtb[None], (128, 16)))
    P["alog"] = np.ascontiguousarray(np.broadcast_to(al[None], (128, 16)))
    P["dsk"] = np.ascontiguousarray(np.broadcast_to(np.repeat(inputs["ssd_d"][l][h * 8:(h + 1) * 8], 64)[None], (128, 512)))
    P["snw"] = np.ascontiguousarray(np.broadcast_to(inputs["ssd_norm_w"][l][h * 512:(h + 1) * 512][None], (128, 512)))
    return P


def vecT(v):
    return np.ascontiguousarray(v.reshape(-1, 128).T)


def fused_in_maps(inputs):
    K = b_consts()
    perm = win_perm()
    shared = dict(K)
    shared["wada"] = inputs["w_ada"]
    shared["bada"] = np.ascontiguousarray(inputs["b_ada"].reshape(4, 96, 128).transpose(2, 0, 1))
    shared["n1a"] = np.stack([vecT(inputs["norm1_w"][l]) for l in range(4)])
    shared["n2a"] = np.stack([vecT(inputs["norm2_w"][l]) for l in range(4)])
    shared["fnw"] = vecT(inputs["final_norm_w"])
    shared["win"] = np.ascontiguousarray(inputs["w_in"][:, :, perm])
    shared["wout"] = inputs["w_out"]; shared["w1"] = inputs["w_mlp1"]; shared["w2"] = inputs["w_mlp2"]
    bp = [[b_params(inputs, l, h) for h in range(2)] for l in range(4)]
    for k in B_PARAM_SHAPES:
        shared["bp_" + k] = np.ascontiguousarray(np.stack([np.stack([bp[l][h][k] for h in range(2)]) for l in range(4)]))
    maps = []
    for core in range(8):
        b = core % 4
        m = dict(shared)
        m["xT0"] = np.ascontiguousarray(np.concatenate([inputs["ctx"][b].T, inputs["x"][b].T], axis=1))
        rows = np.stack([inputs["c"][b], inputs["c_ctx"]], axis=0)
        m["cc"] = np.ascontiguousarray(rows.reshape(2, 16, 128).transpose(2, 1, 0))
        maps.append(m)
    return maps


def kernel(**inputs):
    inputs = {k: np.asarray(v, dtype=np.float32) for k, v in inputs.items()}
    maps = fused_in_maps(inputs)
    res = run_bass_kernel_spmd(build_fused(4), maps, core_ids=list(range(8)))
    out = np.empty((4, 4096, 2048), np.float32)
    for b in range(4):
        out[b] = res.results[b]["fo"].T
    return out
```

```python
import numpy as np
import ml_dtypes
import concourse.bass as bass
import concourse.mybir as mybir
from concourse.bass_utils import run_bass_kernel_spmd

F32 = mybir.dt.float32
BF16 = mybir.dt.bfloat16
ALU = mybir.AluOpType
AF = mybir.ActivationFunctionType
AX = mybir.AxisListType

SAME_ENGINE_SYNC = True
NDMA_SLOTS = 6


class Buf:
    __slots__ = ("name", "w", "r", "untracked", "nodep")

    def __init__(self, name):
        self.name = name
        self.w = None
        self.r = {}
        self.untracked = False
        self.nodep = False


class V:
    __slots__ = ("buf", "ap")

    def __init__(self, buf, ap):
        self.buf = buf
        self.ap = ap

    def __getitem__(self, idx):
        return V(self.buf, self.ap[idx])


class Tile:
    def __init__(self, ap, name, buf=None):
        self.ap = ap
        self.buf = buf if buf is not None else Buf(name)

    def __getitem__(self, idx):
        return V(self.buf, self.ap[idx])

    def sub(self, idx, name):
        return Tile(self.ap[idx], name)


class Prog:
    ENG = ("pe", "dve", "act", "pool", "sp")

    def __init__(self, nc):
        self.nc = nc
        self.items = {e: [] for e in self.ENG}
        self.cnt = {e: 0 for e in self.ENG}
        self.waited = {e: {} for e in self.ENG}
        self.dma_use = {}
        self.dma_rr = {"sp": 0, "pool": 0, "act": 0}
        self.semkeys = set()
        self.out_deps = []
        self.nalloc = 0

    SB_WORDS = 53200

    def init_mem(self):
        self.sb_all = self.nc.alloc_sbuf_tensor("sb_all", [128, self.SB_WORDS], F32)
        self.ps_all = self.nc.alloc_psum_tensor("ps_all", [128, 4096], F32)
        self.sb_base = 0
        self.sb_off = 0
        self.ps_off = 0

    @staticmethod
    def _shape_view(ap, shape):
        if len(shape) == 2:
            return ap
        if len(shape) == 3:
            return ap.rearrange("p (a b) -> p a b", a=shape[1])
        return ap.rearrange("p (a b c) -> p a b c", a=shape[1], b=shape[2])

    def sb(self, name, shape, dtype=F32):
        ne = int(np.prod(shape[1:]))
        nw = ne if dtype == F32 else (ne + 1) // 2
        n = (nw + 7) // 8 * 8
        assert self.sb_off + n <= self.SB_WORDS, ("SBUF carve overflow", name, self.sb_off, n)
        ap = self.sb_all[0:shape[0], self.sb_off:self.sb_off + nw]
        if dtype != F32:
            ap = ap.bitcast(dtype)[:, 0:ne]
        self.sb_off += n
        return Tile(self._shape_view(ap, shape), name)

    def ps(self, name, shape, dtype=F32):
        assert shape[1] <= 512 and self.ps_off + 512 <= 4096, ("PSUM carve overflow", name)
        ap = self.ps_all[0:shape[0], self.ps_off:self.ps_off + shape[1]]
        self.ps_off += 512
        return Tile(ap, name)

    def persist_done(self):
        self.sb_base = self.sb_off

    def stage_begin(self):
        self.barrier()
        self.sb_off = self.sb_base
        self.ps_off = 0

    def barrier(self):
        latest = {}
        for e in self.ENG:
            if self.cnt[e] > 0:
                latest[("E", e)] = self.cnt[e]
        for key, uses in self.dma_use.items():
            latest[key] = 16 * uses
        for e in self.ENG:
            for key, val in latest.items():
                if key == ("E", "pe") and e == "pe":
                    continue
                self._need(e, key, val)

    def _need(self, eng, key, val):
        if self.waited[eng].get(key, 0) >= val:
            return
        self.waited[eng][key] = val
        self.items[eng].append(("wait", key, val))

    def _deps(self, eng, reads, writes, is_dma):
        deps = []
        for b in reads:
            if b.w is not None:
                deps.append(b.w)
        for b in writes:
            if b.w is not None:
                deps.append(b.w)
            deps.extend(b.r.items())
        for key, val in deps:
            if key[0] == "E" and key[1] == eng and not is_dma:
                if eng == "pe" or not SAME_ENGINE_SYNC:
                    continue
            self._need(eng, key, val)

    def _mark(self, reads, writes, key, val):
        for b in reads:
            b.r[key] = val
        for b in writes:
            b.w = (key, val)
            b.r = {}

    def op(self, eng, fn, reads, writes):
        reads = [x.buf if not isinstance(x, Buf) else x for x in reads]
        writes = [x.buf if not isinstance(x, Buf) else x for x in writes]
        self._deps(eng, reads, writes, False)
        self.cnt[eng] += 1
        key = ("E", eng)
        self.semkeys.add(key)
        self.items[eng].append(("ins", fn, key, 1))
        self._mark(reads, writes, key, self.cnt[eng])

    def dma(self, out, in_, queue=None, **kw):
        if queue is None:
            queue = "sp" if self.dma_rr["sp"] <= self.dma_rr["pool"] else "pool"
        slot = self.dma_rr[queue] % NDMA_SLOTS
        self.dma_rr[queue] += 1
        key = ("D", queue, slot)
        self.semkeys.add(key)
        uses = self.dma_use.get(key, 0)
        if uses > 0:
            self._need(queue, key, 16 * uses)
        wr = [] if (out.buf.untracked or out.buf.nodep) else [out.buf]
        rd_ = [] if in_.buf.nodep else [in_.buf]
        self._deps(queue, rd_, wr, True)
        self.dma_use[key] = uses + 1
        oap, iap = out.ap, in_.ap
        self.items[queue].append(("ins", lambda e: e.dma_start(out=oap, in_=iap, **kw), key, 16))
        self._mark(rd_, wr, key, 16 * (uses + 1))
        if out.buf.untracked:
            self.out_deps.append((key, 16 * (uses + 1)))

    def mm(self, out, lhsT, rhs, start, stop):
        o, l, r = out.ap, lhsT.ap, rhs.ap
        self.op("pe", lambda e: e.matmul(o, l, r, start=start, stop=stop), [lhsT, rhs], [out])

    def transpose(self, out, in_, ident):
        o, i, d = out.ap, in_.ap, ident.ap
        self.op("pe", lambda e: e.transpose(o, i, d), [in_, ident], [out])

    def act(self, out, in_, func, bias=0.0, scale=1.0, accum_out=None, eng="act"):
        rd = [in_]
        wr = [out]
        b = bias
        s = scale
        if isinstance(bias, V):
            rd.append(bias); b = bias.ap
        if isinstance(scale, V):
            rd.append(scale); s = scale.ap
        kw = {}
        if accum_out is not None:
            wr.append(accum_out); kw["accum_out"] = accum_out.ap
        o, i = out.ap, in_.ap
        self.op("act", lambda e: e.activation(o, i, func, bias=b, scale=s, **kw), rd, wr)

    def tt(self, out, in0, in1, op, eng="dve"):
        o, a, b = out.ap, in0.ap, in1.ap
        self.op(eng, lambda e: e.tensor_tensor(o, a, b, op), [in0, in1], [out])

    def ts(self, out, in0, s1, s2, op0, op1=None, eng="dve", accum_out=None):
        rd = [in0]
        wr = [out]
        a1, a2 = s1, s2
        if isinstance(s1, V):
            rd.append(s1); a1 = s1.ap
        if isinstance(s2, V):
            rd.append(s2); a2 = s2.ap
        o, i = out.ap, in0.ap
        kw = {}
        if accum_out is not None:
            wr.append(accum_out); kw["accum_out"] = accum_out.ap
        if op1 is None:
            self.op(eng, lambda e: e.tensor_scalar(o, i, a1, None, op0, **kw), rd, wr)
        else:
            self.op(eng, lambda e: e.tensor_scalar(o, i, a1, a2, op0, op1, **kw), rd, wr)

    def stt(self, out, in0, scalar, in1, op0, op1, eng="dve"):
        rd = [in0, in1]
        s = scalar
        if isinstance(scalar, V):
            rd.append(scalar); s = scalar.ap
        o, a, b = out.ap, in0.ap, in1.ap
        self.op(eng, lambda e: e.scalar_tensor_tensor(o, a, s, b, op0, op1), rd, [out])

    def copy(self, out, in_, eng="dve"):
        o, i = out.ap, in_.ap
        if eng == "act":
            self.op("act", lambda e: e.copy(o, i), [in_], [out])
        else:
            self.op(eng, lambda e: e.tensor_copy(o, i), [in_], [out])

    def memset(self, out, val, eng="dve"):
        o = out.ap
        self.op(eng, lambda e: e.memset(o, val), [], [out])

    def reduce(self, out, in_, op, axis=AX.X, eng="dve"):
        o, i = out.ap, in_.ap
        self.op(eng, lambda e: e.tensor_reduce(o, i, axis, op), [in_], [out])

    def recip(self, out, in_):
        o, i = out.ap, in_.ap
        self.op("dve", lambda e: e.reciprocal(o, i), [in_], [out])

    def final_wait(self, bufs=()):
        for b in bufs:
            b = b.buf if not isinstance(b, Buf) else b
            if b.w is not None:
                self._need("sp", b.w[0], b.w[1])
        for key, val in self.out_deps:
            self._need("sp", key, val)

    def emit(self):
        nc = self.nc
        sems = {}
        for key in sorted(self.semkeys):
            sems[key] = nc.alloc_semaphore("s_" + "_".join(str(k) for k in key))
        items = self.items

        def replay(eng_obj, lst):
            for it in lst:
                if it[0] == "wait":
                    eng_obj.wait_ge(sems[it[1]], it[2])
                else:
                    it[1](eng_obj).then_inc(sems[it[2]], it[3])

        with nc.Block() as block:
            @block.tensor
            def _(e):
                replay(e, items["pe"])

            @block.vector
            def _(e):
                replay(e, items["dve"])

            @block.scalar
            def _(e):
                replay(e, items["act"])

            @block.gpsimd
            def _(e):
                replay(e, items["pool"])

            @block.sync
            def _(e):
                replay(e, items["sp"])
        return sems


D = 2048
NKC = 16
NTOK = 4352
EPS = 1e-6
PROJ = 5664
DFF = 8192
SEQS = [(0, 256, 256), (256, 4096, 64)]
MAGIC = 12582912.0
TWO_PI = 2.0 * np.pi


def din(nc, name, shape, dtype=F32):
    return Tile(nc.dram_tensor(name, list(shape), dtype, kind="ExternalInput").ap(), name)


def dout(nc, name, shape):
    t = Tile(nc.dram_tensor(name, list(shape), F32, kind="ExternalOutput").ap(), name)
    t.buf.untracked = True
    return t


def dscr(nc, name, shape, nodep=False):
    t = Tile(nc.dram_tensor(name, list(shape), F32).ap(), name)
    t.buf.nodep = nodep
    return t


def sub(t, ap):
    return Tile(ap, t.buf.name, t.buf)


def stage_P(p, cc, wada, bada, mo):
    p.stage_begin()
    cct = p.sb("cct", [128, 16, 2])
    bt = p.sb("bt", [128, 4, 96])
    wb = [p.sb("wb%d" % i, [128, 16, 512]) for i in range(2)]
    pss = [p.ps("ps%d" % i, [128, 8]) for i in range(2)]
    p.dma(cct[:], cc[:])
    p.dma(bt[:], bada[:])
    p.act(cct[:], cct[:], AF.Silu)
    n = 0
    for l in range(4):
        wv = sub(wada, wada.ap[l].rearrange("(kc p) c -> p kc c", p=128))
        for blk in range(24):
            wt = wb[(l * 24 + blk) % 2]
            p.dma(wt[:], wv[:, :, blk * 512:(blk + 1) * 512])
            for j in range(4):
                ch = blk * 4 + j
                ps = pss[n % 2]
                n += 1
                for kc in range(16):
                    p.mm(ps[:, 0:2], wt[:, kc, j * 128:(j + 1) * 128], cct[:, kc, :], kc == 0, kc == 15)
                p.ts(mo[:, l, ch, :], ps[:, 0:2], bt[:, l, ch:ch + 1], None, ALU.add)


def rms_stats(p, xt, W, ones, sqb, ss_ps, rstd):
    for kc in range(NKC):
        sq = sqb[kc % 2]
        p.act(sq[:, :W], xt[:, kc, :W], AF.Square)
        p.mm(ss_ps[:, :W], ones[:, :], sq[:, :W], kc == 0, kc == NKC - 1)
    p.ts(rstd[:, :W], ss_ps[:, :W], 1.0 / D, EPS, ALU.mult, ALU.add)
    p.act(rstd[:, :W], rstd[:, :W], AF.Sqrt)
    p.recip(rstd[:, :W], rstd[:, :W])


def rms_mod(p, xt, W, hl, gs_col, sh_col, ones, sqb, ss_ps, rstd):
    rms_stats(p, xt, W, ones, sqb, ss_ps, rstd)
    for kc in range(NKC):
        tmp = sqb[kc % 2]
        p.stt(tmp[:, :W], xt[:, kc, :W], gs_col(kc), rstd[:, :W], ALU.mult, ALU.mult)
        p.act(hl[:, kc, :W], tmp[:, :W], AF.Identity, bias=sh_col(kc), scale=1.0)


TILES_A = [(0, 256, 1)] + [(256 + 512 * j, 512, 0) for j in range(8)]


def stage_A(p, ones, xT, mt, n1, win, uT):
    p.stage_begin()
    n1t = p.sb("n1t", [128, 16])
    gs = p.sb("gs", [128, 16, 2])
    p.dma(n1t[:], n1[:])
    for r in range(2):
        p.stt(gs[:, :, r], mt[:, 16:32, r], 1.0, n1t[:, :], ALU.add, ALU.mult)
    xb = [p.sb("xb%d" % i, [128, 16, 512]) for i in range(2)]
    hl = p.sb("hl", [128, 16, 512], BF16)
    sqb = [p.sb("sq%d" % i, [128, 512]) for i in range(2)]
    rstd = p.sb("rstd", [128, 512])
    wb = [p.sb("wb%d" % i, [128, 16, 512]) for i in range(2)]
    wbh = [p.sb("wbh%d" % i, [128, 16, 512], BF16) for i in range(2)]
    wdt = p.sb("wdt", [128, 16, 32])
    wdth = p.sb("wdth", [128, 16, 32], BF16)
    ob = [p.sb("ob%d" % i, [128, 512]) for i in range(3)]
    ss_ps = p.ps("ss_ps", [128, 512])
    pss = [p.ps("ps%d" % i, [128, 512]) for i in range(3)]
    xv = sub(xT, xT.ap.rearrange("(kc p) t -> p kc t", p=128))
    wv = sub(win, win.ap.rearrange("(kc p) c -> p kc c", p=128))
    p.dma(wdt[:], wv[:, :, 5632:5664])
    p.copy(wdth[:], wdt[:])
    nw = 0
    no = 0
    for ti, (t0, W, r) in enumerate(TILES_A):
        xt = xb[ti % 2]
        p.dma(xt[:, :, :W], xv[:, :, t0:t0 + W])
        rms_mod(p, xt, W, hl, lambda kc: gs[:, kc, r:r + 1], lambda kc: mt[:, kc, r:r + 1], ones, sqb, ss_ps, rstd)
        for og in range(12):
            if og < 11:
                wf_ = wb[nw % 2]
                wt = wbh[nw % 2]
                p.dma(wf_[:], wv[:, :, og * 512:(og + 1) * 512])
                p.copy(wt[:], wf_[:], eng="act" if nw % 2 else "dve")
                nw += 1
                subs = [(wt, j * 128, 128, (og * 4 + j) * 128) for j in range(4)]
            else:
                subs = [(wdth, 0, 32, 5632)]
            for (wt_, c0, M, row0) in subs:
                ps = pss[no % 3]
                o = ob[no % 3]
                for kc in range(NKC):
                    p.mm(ps[:M, :W], wt_[:, kc, c0:c0 + M], hl[:, kc, :W], kc == 0, kc == NKC - 1)
                p.copy(o[:M, :W], ps[:M, :W], eng="act" if no % 2 else "dve")
                p.dma(uT[row0:row0 + M, t0:t0 + W], o[:M, :W])
                no += 1


TILES_C = [(0, 256, 1)] + [(256 + 512 * j, 512, 0) for j in range(8)]


def stage_C(p, ones, xT, yT, mt, n2, wout, w1, w2, xo, fnw=None, fo=None):
    final = fo is not None
    p.stage_begin()
    n2t = p.sb("n2t", [128, 16])
    gs = p.sb("gs", [128, 16, 2])
    p.dma(n2t[:], n2[:])
    for r in range(2):
        p.stt(gs[:, :, r], mt[:, 64:80, r], 1.0, n2t[:, :], ALU.add, ALU.mult)
    if final:
        fnt = p.sb("fnt", [128, 16])
        p.dma(fnt[:], fnw[:])
    WT = 512
    xt = p.sb("xt", [128, 16, WT])
    yt = p.sb("yt", [128, 16, WT], BF16)
    yst = [p.sb("yst%d" % i, [128, WT]) for i in range(2)]
    hh = p.sb("hh", [128, 32, WT], BF16)
    sqb = [p.sb("sq%d" % i, [128, WT]) for i in range(2)]
    rstd = p.sb("rstd", [128, WT])
    rl = [p.sb("rl%d" % i, [128, WT]) for i in range(2)]
    wflat = [p.sb("wb%d" % i, [128, 8192]) for i in range(2)]
    wb = [sub(w, w.ap.rearrange("p (a b) -> p a b", a=16)) for w in wflat]
    wb2 = [sub(w, w.ap.rearrange("p (a b) -> p a b", a=64)) for w in wflat]
    whf = [p.sb("wh%d" % i, [128, 8192], BF16) for i in range(2)]
    wbh = [sub(w, w.ap.rearrange("p (a b) -> p a b", a=16)) for w in whf]
    wbh2 = [sub(w, w.ap.rearrange("p (a b) -> p a b", a=64)) for w in whf]
    ss_ps = p.ps("ss_ps", [128, WT])
    pss = [p.ps("ps%d" % i, [128, WT]) for i in range(3)]
    xv = sub(xT, xT.ap.rearrange("(kc p) t -> p kc t", p=128))
    yv = sub(yT, yT.ap.rearrange("(kc p) t -> p kc t", p=128))
    xov = sub(xo, xo.ap.rearrange("(kc p) t -> p kc t", p=128))
    wov = sub(wout, wout.ap.rearrange("(kc p) c -> p kc c", p=128))
    w1v = sub(w1, w1.ap.rearrange("(kc p) c -> p kc c", p=128))
    w2v = sub(w2, w2.ap.rearrange("(kc p) c -> p kc c", p=128))
    if final:
        fov = sub(fo, fo.ap.rearrange("(kc p) t -> p kc t", p=128))
    nw = 0
    npz = 0
    for ti, (t0, W, r) in enumerate(TILES_C):
        p.dma(xt[:, :, :W], xv[:, :, t0:t0 + W])
        for mc in range(NKC):
            ys_ = yst[mc % 2]
            p.dma(ys_[:, :W], yv[:, mc, t0:t0 + W])
            p.copy(yt[:, mc, :W], ys_[:, :W], eng="act" if mc % 2 else "dve")
        for dg in range(4):
            wt = wbh[nw % 2]
            p.dma(wb[nw % 2][:], wov[:, :, dg * 512:(dg + 1) * 512])
            p.copy(whf[nw % 2][:], wflat[nw % 2][:], eng="act" if nw % 2 else "dve")
            nw += 1
            for j in range(4):
                dc = dg * 4 + j
                ps = pss[npz % 3]
                npz += 1
                for mc in range(NKC):
                    p.mm(ps[:, :W], wt[:, mc, j * 128:(j + 1) * 128], yt[:, mc, :W], mc == 0, mc == NKC - 1)
                p.stt(xt[:, dc, :W], ps[:, :W], mt[:, 32 + dc, r:r + 1], xt[:, dc, :W], ALU.mult, ALU.add)
        rms_mod(p, xt, W, yt, lambda kc: gs[:, kc, r:r + 1], lambda kc: mt[:, 48 + kc, r:r + 1], ones, sqb, ss_ps, rstd)
        for hf in range(2):
            for hg in range(8):
                wt = wbh[nw % 2]
                c0 = (hf * 8 + hg) * 512
                p.dma(wb[nw % 2][:], w1v[:, :, c0:c0 + 512])
                p.copy(whf[nw % 2][:], wflat[nw % 2][:], eng="act" if nw % 2 else "dve")
                nw += 1
                for j in range(4):
                    hc = hg * 4 + j
                    ps = pss[npz % 3]
                    rr = rl[npz % 2]
                    npz += 1
                    for kc in range(NKC):
                        p.mm(ps[:, :W], wt[:, kc, j * 128:(j + 1) * 128], yt[:, kc, :W], kc == 0, kc == NKC - 1)
                    p.act(rr[:, :W], ps[:, :W], AF.Relu)
                    p.tt(hh[:, hc, :W], rr[:, :W], rr[:, :W], ALU.mult)
            for dg in range(8):
                wt = wbh2[nw % 2]
                for q in range(2):
                    dc = dg * 2 + q
                    p.dma(wb2[nw % 2][:, q * 32:(q + 1) * 32, :], w2v[:, hf * 32:(hf + 1) * 32, dc * 128:(dc + 1) * 128])
                p.copy(whf[nw % 2][:], wflat[nw % 2][:], eng="act" if nw % 2 else "dve")
                nw += 1
                for q in range(2):
                    dc = dg * 2 + q
                    ps = pss[npz % 3]
                    npz += 1
                    for hc in range(32):
                        p.mm(ps[:, :W], wt[:, q * 32 + hc, :], hh[:, hc, :W], hc == 0, hc == 31)
                    p.stt(xt[:, dc, :W], ps[:, :W], mt[:, 80 + dc, r:r + 1], xt[:, dc, :W], ALU.mult, ALU.add)
        p.dma(xov[:, :, t0:t0 + W], xt[:, :, :W])
        if final and r == 0:
            rms_stats(p, xt, W, ones, sqb, ss_ps, rstd)
            for kc in range(NKC):
                ys_ = yst[kc % 2]
                p.stt(ys_[:, :W], xt[:, kc, :W], fnt[:, kc:kc + 1], rstd[:, :W], ALU.mult, ALU.mult)
                p.dma(fov[:, kc, t0 - 256:t0 - 256 + W], ys_[:, :W])


def stage_B(p, nc, l, h, idt, um, ones, uT, yT, PR, consts, uc, z1, KT):
    p.stage_begin()
    rb = h * 2816
    hcw, hcb, hbias, hnw = PR["hcw"], PR["hcb"], PR["hbias"], PR["hnw"]
    fw1, fb1, ffq, fw2, fb2, fw3, absd = PR["fw1"], PR["fb1"], PR["ffq"], PR["fw2"], PR["fb2"], PR["fw3"], PR["absd"]
    dtb, alog, dsk, snw = PR["dtb"], PR["alog"], PR["dsk"], PR["snw"]
    cwt = p.sb("cwt", [128, 18, 3]); p.dma(cwt[:], hcw[:])
    cbt = p.sb("cbt", [128, 18]); p.dma(cbt[:], hcb[:])
    hbt = p.sb("hbt", [128, 2, 4]); p.dma(hbt[:], hbias[:])
    hnt = p.sb("hnt", [128, 4]); p.dma(hnt[:], hnw[:])
    big = [p.sb("big%d" % i, [128, 8192 + (8 if i == 3 else 0)]) for i in range(4)]
    wk = [p.sb("wk%d" % i, [128, 512]) for i in range(8)]
    bigh = [Tile(b_.ap[:, :].bitcast(BF16)[:, 0:8192], "bigh", b_.buf) for b_ in big]
    psA = [p.ps("psA%d" % i, [128, 512]) for i in range(4)]
    psT = [p.ps("psT%d" % i, [128, 512]) for i in range(2)]
    psS = p.ps("psS", [128, 512])
    cnt = {"wk": 0, "ms": 0, "pt": 0}

    def nwk():
        cnt["wk"] += 1
        return wk[cnt["wk"] % 8]

    def npt():
        cnt["pt"] += 1
        return psT[cnt["pt"] % 2]

    for j in range(18):
        srow = j * 128 if j < 12 else 1536 + (j - 12) * 128
        for (s0, L, RL) in SEQS:
            for b0 in range(0, L, 512):
                W = min(512, L - b0)
                c0 = s0 + b0
                ui = nwk(); uo = nwk()
                p.dma(ui[:, :W], uT[rb + srow:rb + srow + 128, c0:c0 + W])
                p.ts(uo[:, :W], ui[:, :W], cwt[:, j, 1:2], cbt[:, j:j + 1], ALU.mult, ALU.add)
                uiv = ui.ap[:, :W].rearrange("p (a b) -> p a b", b=RL)
                uov = uo.ap[:, :W].rearrange("p (a b) -> p a b", b=RL)
                p.stt(V(uo.buf, uov[:, :, 1:RL]), V(ui.buf, uiv[:, :, 0:RL - 1]), cwt[:, j, 0:1], V(uo.buf, uov[:, :, 1:RL]), ALU.mult, ALU.add)
                p.stt(V(uo.buf, uov[:, :, 0:RL - 1]), V(ui.buf, uiv[:, :, 1:RL]), cwt[:, j, 2:3], V(uo.buf, uov[:, :, 0:RL - 1]), ALU.mult, ALU.add)
                if j >= 12:
                    p.act(uo[:, :W], uo[:, :W], AF.Silu)
                p.dma(uc[j][:, c0:c0 + W], uo[:, :W])

    mark_ = p.sb_off
    mslab = [p.sb("msl%d" % i, [128, 4, 512], BF16) for i in range(2)]
    f1t = p.sb("f1t", [33, 64]); p.dma(f1t[:], fw1[:])
    f2t = p.sb("f2t", [64, 64]); p.dma(f2t[:], fw2[:])
    f3t = p.sb("f3t", [64, 2, 2, 512]); p.dma(f3t[:], fw3[:])
    fbt = p.sb("fbt", [64, 3]); p.dma(fbt[:, 0:1], fb1[:]); p.dma(fbt[:, 1:2], fb2[:]); p.dma(fbt[:, 2:3], ffq[:])
    abt = p.sb("abt", [128, 512]); p.dma(abt[:], absd[:])
    hd1 = Tile(big[3].ap[:64, 0:4096], "hd1", big[3].buf)
    hd2 = Tile(big[3].ap[:64, 4096:8200], "hd2", big[3].buf)
    ntf = p.sb("ntf", [128, 32]); ntb = p.sb("ntb", [128, 32])
    nrm = p.sb("nrm", [128, 2])
    nacc = p.sb("nacc", [128, 2])
    pht = p.sb("pht", [128, 2, 512])

    def sin_layer(dst, src_fn, lhsT, bias_col, L):
        for b0 in range(0, L, 512):
            W = min(512, L - b0)
            ps = npt()
            p.mm(ps[:64, :W], lhsT, src_fn(b0, W), True, True)
            a = nwk(); n = nwk()
            p.ts(a[:64, :W], ps[:64, :W], fbt[:, bias_col:bias_col + 1], fbt[:, 2:3], ALU.add, ALU.mult)
            p.ts(n[:64, :W], a[:64, :W], 1.0 / TWO_PI, MAGIC, ALU.mult, ALU.add)
            p.ts(n[:64, :W], n[:64, :W], MAGIC, None, ALU.subtract)
            p.stt(a[:64, :W], n[:64, :W], -TWO_PI, a[:64, :W], ALU.mult, ALU.add)
            p.act(dst[:, b0:b0 + W], a[:64, :W], AF.Sin)

    def mat_pass(M, L, nacc, body, evac):
        FB = min(512, L)
        NRC = L // 128
        Mv = Tile(M.ap.rearrange("(rc p) f -> p rc f", p=128), "mv", M.buf)
        for fb in range(L // FB):
            for g in range(0, NRC, 4):
                ng = min(4, NRC - g)
                cnt["ms"] += 1
                sl = mslab[cnt["ms"] % 2]
                p.dma(sl[:, :ng, :FB], Mv[:, g:g + ng, fb * FB:(fb + 1) * FB])
                for q in range(ng):
                    rc = g + q
                    body(rc, sl[:, q, :FB], fb, rc == 0, rc == NRC - 1)
            evac(fb, FB)

    for (s0, L, RL) in SEQS:
        C = consts[L]
        NTC = L // 128
        FB = min(512, L)
        p.dma(ntf[:, :NTC], C["ntnf"][:]); p.dma(ntb[:, :NTC], C["ntnb"][:])
        for b0 in range(0, L, 512):
            W = min(512, L - b0)
            ft = nwk()
            p.dma(ft[:33, :W], C["feats"][:, b0:b0 + W])
            ps = npt()
            p.mm(ps[:64, :W], f1t[:, :], ft[:33, :W], True, True)
            a = nwk(); n = nwk()
            p.ts(a[:64, :W], ps[:64, :W], fbt[:, 0:1], fbt[:, 2:3], ALU.add, ALU.mult)
            p.ts(n[:64, :W], a[:64, :W], 1.0 / TWO_PI, MAGIC, ALU.mult, ALU.add)
            p.ts(n[:64, :W], n[:64, :W], MAGIC, None, ALU.subtract)
            p.stt(a[:64, :W], n[:64, :W], -TWO_PI, a[:64, :W], ALU.mult, ALU.add)
            p.act(hd1[:, b0:b0 + W], a[:64, :W], AF.Sin)
        p.memset(hd2[:, L:L + 8], 0.0)
        sin_layer(hd2, lambda b0, W: hd1[:, b0:b0 + W], f2t[:, :], 1, L)

        for cp in range(2):
            ch0 = cp * 256
            for o in range(2):
                ksum, kdiff, prT = bigh[0], bigh[1], big[2]
                for tc in range(NTC):
                    pf = npt(); pb = npt()
                    p.mm(pf[:, :256], hd2[:, tc * 128:tc * 128 + 128], f3t[:, o, 0, ch0:ch0 + 256], True, True)
                    p.mm(pb[:, :256], hd2[:, tc * 128 + 1:tc * 128 + 129], f3t[:, o, 1, ch0:ch0 + 256], True, True)
                    wf = nwk(); wbk = nwk()
                    p.act(wf[:, :256], abt[:, ch0:ch0 + 256], AF.Exp, scale=ntf[:, tc:tc + 1])
                    p.act(wbk[:, :256], abt[:, ch0:ch0 + 256], AF.Exp, scale=ntb[:, tc:tc + 1])
                    p.tt(wf[:, :256], pf[:, :256], wf[:, :256], ALU.mult)
                    p.tt(wbk[:, :256], pb[:, :256], wbk[:, :256], ALU.mult)
                    p.tt(ksum[:, tc * 256:(tc + 1) * 256], wf[:, :256], wbk[:, :256], ALU.add)
                    p.tt(kdiff[:, tc * 256:(tc + 1) * 256], wf[:, :256], wbk[:, :256], ALU.subtract)
                    p.act(wf[:, :256], wf[:, :256], AF.Abs)
                    p.act(wbk[:, :256], wbk[:, :256], AF.Abs)
                    p.tt(wf[:, :256], wf[:, :256], wbk[:, :256], ALU.add)
                    for q in range(2):
                        p.mm(psS[:, q * 8:q * 8 + 1], wf[:, q * 128:(q + 1) * 128], ones[:, 0:1], True, True)
                    if tc == 0:
                        p.copy(nacc[:, 0:1], psS[:, 0:1]); p.copy(nacc[:, 1:2], psS[:, 8:9])
                    else:
                        p.tt(nacc[:, 0:1], nacc[:, 0:1], psS[:, 0:1], ALU.add); p.tt(nacc[:, 1:2], nacc[:, 1:2], psS[:, 8:9], ALU.add)
                p.ts(nrm[:, :], nacc[:, 0:2], float(L), EPS * L, ALU.mult, ALU.add)
                p.recip(nrm[:, :], nrm[:, :])

                def bodyP(rc, sl, fb, first, last):
                    for q in range(2):
                        p.mm(psA[q][:, :FB], ksum[:, rc * 256 + q * 128: rc * 256 + (q + 1) * 128], sl, first, last)

                def evacP(fb, FBw):
                    for q in range(2):
                        p.copy(prT[:, q * L + fb * FBw: q * L + (fb + 1) * FBw], psA[q][:, :FBw], eng="act")
                mat_pass(C["cs"], L, 2, bodyP, evacP)

                def bodyQ(rc, sl, fb, first, last):
                    for q in range(2):
                        p.mm(psA[q][:, :FB], kdiff[:, rc * 256 + q * 128: rc * 256 + (q + 1) * 128], sl, first, last)

                def evacQ(fb, FBw, o=o, cp=cp):
                    p.dma(pht[:, :, :FBw], C["ph"][:, :, fb * FBw:(fb + 1) * FBw])
                    for q in range(2):
                        pr = prT[:, q * L + fb * FBw: q * L + (fb + 1) * FBw]
                        t1 = nwk(); t2 = nwk(); kr = nwk(); ki = nwk()
                        p.tt(t1[:, :FBw], pr, pht[:, 0, :FBw], ALU.mult)
                        p.tt(t2[:, :FBw], psA[q][:, :FBw], pht[:, 1, :FBw], ALU.mult)
                        p.tt(t1[:, :FBw], t1[:, :FBw], t2[:, :FBw], ALU.add)
                        p.ts(kr[:, :FBw], t1[:, :FBw], nrm[:, q:q + 1], None, ALU.mult)
                        p.tt(t1[:, :FBw], pr, pht[:, 1, :FBw], ALU.mult)
                        p.tt(t2[:, :FBw], psA[q][:, :FBw], pht[:, 0, :FBw], ALU.mult)
                        p.tt(t1[:, :FBw], t1[:, :FBw], t2[:, :FBw], ALU.subtract)
                        p.ts(ki[:, :FBw], t1[:, :FBw], nrm[:, q:q + 1], None, ALU.mult)
                        p.dma(KT[(L, o, cp * 2 + q, 0)][:, fb * FBw:(fb + 1) * FBw], kr[:, :FBw])
                        p.dma(KT[(L, o, cp * 2 + q, 1)][:, fb * FBw:(fb + 1) * FBw], ki[:, :FBw])
                mat_pass(C["ss"], L, 2, bodyQ, evacQ)

        for cp in range(2):
            for o in range(2):
                ztm, aT, u1f, u2f = bigh[0], big[1], bigh[2], bigh[3]
                zsrc = [uc[cp * 2 + q] if o == 0 else z1[cp * 2 + q] for q in range(2)]
                gate = [uc[4 + 4 * o + cp * 2 + q] for q in range(2)]
                for q in range(2):
                    for b0 in range(0, L, 512):
                        W = min(512, L - b0)
                        zi = nwk()
                        p.dma(zi[:, :W], zsrc[q][:, s0 + b0:s0 + b0 + W])
                        for k in range(W // 128):
                            tc = b0 // 128 + k
                            pt = npt()
                            p.transpose(pt[:, :128], zi[:, k * 128:(k + 1) * 128], idt[:, :])
                            p.copy(ztm[:, tc * 256 + q * 128: tc * 256 + (q + 1) * 128], pt[:, :128], eng="act" if k % 2 else "dve")

                def bodyA(rc, sl, fb, first, last):
                    for q in range(2):
                        p.mm(psA[q][:, :FB], ztm[:, rc * 256 + q * 128: rc * 256 + (q + 1) * 128], sl, first, last)

                def evacA(fb, FBw):
                    for q in range(2):
                        p.copy(aT[:, q * L + fb * FBw: q * L + (fb + 1) * FBw], psA[q][:, :FBw], eng="act")
                mat_pass(C["cs"], L, 2, bodyA, evacA)

                def evacB(fb, FBw, o=o, cp=cp):
                    for q in range(2):
                        kr = nwk(); ki = nwk(); t1 = nwk(); t2 = nwk(); bT = nwk()
                        p.dma(kr[:, :FBw], KT[(L, o, cp * 2 + q, 0)][:, fb * FBw:(fb + 1) * FBw])
                        p.dma(ki[:, :FBw], KT[(L, o, cp * 2 + q, 1)][:, fb * FBw:(fb + 1) * FBw])
                        a_ = aT[:, q * L + fb * FBw: q * L + (fb + 1) * FBw]
                        p.copy(bT[:, :FBw], psA[q][:, :FBw], eng="act")
                        p.tt(t1[:, :FBw], kr[:, :FBw], a_, ALU.mult)
                        p.tt(t2[:, :FBw], ki[:, :FBw], bT[:, :FBw], ALU.mult)
                        p.tt(t1[:, :FBw], t1[:, :FBw], t2[:, :FBw], ALU.add)
                        p.tt(t2[:, :FBw], kr[:, :FBw], bT[:, :FBw], ALU.mult)
                        p.tt(kr[:, :FBw], ki[:, :FBw], a_, ALU.mult)
                        p.tt(t2[:, :FBw], t2[:, :FBw], kr[:, :FBw], ALU.subtract)
                        for k in range(FBw // 128):
                            fc = fb * (FBw // 128) + k
                            for (src, dstb) in ((t1, u1f), (t2, u2f)):
                                pt = npt()
                                p.transpose(pt[:, :128], src[:, k * 128:(k + 1) * 128], idt[:, :])
                                p.copy(dstb[:, fc * 256 + q * 128: fc * 256 + (q + 1) * 128], pt[:, :128], eng="act")
                mat_pass(C["ss"], L, 2, bodyA, evacB)

                FBi = min(512, L)
                NRC = L // 128
                csv = Tile(C["cs"].ap.rearrange("(rc p) f -> p rc f", p=128), "csv", C["cs"].buf)
                ssv = Tile(C["ss"].ap.rearrange("(rc p) f -> p rc f", p=128), "ssv", C["ss"].buf)
                for tb in range(L // FBi):
                    for mi, (Mv, uf) in enumerate(((csv, u1f), (ssv, u2f))):
                        for g in range(0, NRC, 4):
                            ng = min(4, NRC - g)
                            cnt["ms"] += 1
                            sl = mslab[cnt["ms"] % 2]
                            p.dma(sl[:, :ng, :FBi], Mv[:, g:g + ng, tb * FBi:(tb + 1) * FBi])
                            for qq in range(ng):
                                rc = g + qq
                                for q in range(2):
                                    p.mm(psA[2 + q][:, :FBi], uf[:, rc * 256 + q * 128: rc * 256 + (q + 1) * 128], sl[:, qq, :FBi],
                                         mi == 0 and rc == 0, mi == 1 and rc == NRC - 1)
                    c0 = s0 + tb * FBi
                    for q in range(2):
                        cq = cp * 2 + q
                        zi = nwk(); gi = nwk(); r_ = nwk()
                        p.dma(zi[:, :FBi], zsrc[q][:, c0:c0 + FBi])
                        p.dma(gi[:, :FBi], gate[q][:, c0:c0 + FBi])
                        p.stt(r_[:, :FBi], zi[:, :FBi], hbt[:, o, cq:cq + 1], psA[2 + q][:, :FBi], ALU.mult, ALU.add)
                        p.tt(r_[:, :FBi], r_[:, :FBi], gi[:, :FBi], ALU.mult)
                        if o == 0:
                            p.dma(z1[cq][:, c0:c0 + FBi], r_[:, :FBi])
                        else:
                            sq = nwk(); rs = nwk()
                            p.act(sq[:, :FBi], r_[:, :FBi], AF.Square)
                            pt = npt()
                            p.mm(pt[:, :FBi], ones[:, :], sq[:, :FBi], True, True)
                            p.ts(rs[:, :FBi], pt[:, :FBi], 1.0 / 128.0, EPS, ALU.mult, ALU.add)
                            p.act(rs[:, :FBi], rs[:, :FBi], AF.Sqrt)
                            p.recip(rs[:, :FBi], rs[:, :FBi])
                            p.stt(r_[:, :FBi], r_[:, :FBi], hnt[:, cq:cq + 1], rs[:, :FBi], ALU.mult, ALU.mult)
                            p.dma(yT[h * 512 + cq * 128:h * 512 + (cq + 1) * 128, c0:c0 + FBi], r_[:, :FBi])

    p.barrier()
    p.sb_off = mark_
    dbt = p.sb("dbt", [128, 16]); p.dma(dbt[:], dtb[:])
    an = p.sb("an", [128, 16]); p.dma(an[:], alog[:])
    p.act(an[:], an[:], AF.Exp)
    p.ts(an[:], an[:], -1.0, None, ALU.mult)
    dskt = p.sb("dskt", [128, 512]); p.dma(dskt[:], dsk[:])
    snt = p.sb("snt", [128, 512]); p.dma(snt[:], snw[:])
    NCH = NTOK // 128
    dtt = p.sb("dtt", [128, NCH, 16])
    att = p.sb("att", [128, NCH, 16])
    yac = Tile(big[0].ap, "yac", big[0].buf)
    yac2 = Tile(big[1].ap, "yac2", big[1].buf)
    yac3 = Tile(big[2].ap, "yac3", big[2].buf)

    def ysl(c):
        if c < 16:
            return big[0][:, c * 512:(c + 1) * 512]
        if c < 32:
            return big[1][:, (c - 16) * 512:(c - 15) * 512]
        return big[2][:, (c - 32) * 512:(c - 31) * 512]

    for b0 in range(0, NTOK, 512):
        W = min(512, NTOK - b0)
        di = nwk()
        p.dma(di[:16, :W], uT[5632 + h * 16:5632 + (h + 1) * 16, b0:b0 + W])
        for k in range(W // 128):
            c = b0 // 128 + k
            pt = npt()
            p.transpose(pt[:, :16], di[:16, k * 128:(k + 1) * 128], idt[:16, :16])
            p.tt(dtt[:, c, :], pt[:, :16], dbt[:, :], ALU.add)
    p.act(dtt[:], dtt[:], AF.Exp)
    p.act(dtt[:], dtt[:], AF.Ln, bias=1.0)
    for c in range(NCH):
        p.tt(att[:, c, :], dtt[:, c, :], an[:, :], ALU.mult)

    ST = p.sb("ST", [128, 512])
    xst = p.sb("xst", [128, 512]); xdt = p.sb("xdt", [128, 512]); xdd = p.sb("xdd", [128, 512])
    btk = p.sb("btk", [128, 128]); bT_ = p.sb("bT_", [128, 128]); cT_ = p.sb("cT_", [128, 128])
    cbm = p.sb("cbm", [128, 128]); cst = p.sb("cst", [128, 8]); tot = p.sb("tot", [128, 8]); dsc = p.sb("dsc", [128, 8]); dte = p.sb("dte", [128, 8])
    psY = psA[0]; psSt = psA[1]; psB = [psA[2], psA[3]]
    hb = [[p.sb("hb%d_%d" % (i, k), [128, 128]) for k in range(5)] for i in range(2)]

    for d in range(2):
        Xm = um[:, d, :]
        p.memset(ST[:], 0.0)
        order = [0, 1] + list(range(2, NCH)) if d == 0 else [1, 0] + list(range(NCH - 1, 1, -1))
        for c in order:
            t0 = c * 128
            for k in range(4):
                xi = nwk()
                p.dma(xi[:, :128], uc[12 + k][:, t0:t0 + 128])
                pt = npt()
                p.transpose(pt[:, :128], xi[:, :128], idt[:, :])
                p.copy(xst[:, k * 128:(k + 1) * 128], pt[:, :128], eng="act")
            p.dma(bT_[:], uc[16][:, t0:t0 + 128])
            p.dma(cT_[:], uc[17][:, t0:t0 + 128])
            pt = npt()
            p.transpose(pt[:, :128], bT_[:, :], idt[:, :])
            p.copy(btk[:], pt[:, :128], eng="act")
            a8 = att[:, c, d * 8:(d + 1) * 8]
            pt = npt()
            p.mm(pt[:, 0:8], Xm, a8, True, True)
            p.copy(cst[:], pt[:, 0:8])
            pt = npt()
            p.mm(pt[:, 0:8], ones[:, :], a8, True, True)
            p.copy(tot[:], pt[:, 0:8])
            p.tt(dte[:], tot[:], cst[:], ALU.subtract)
            p.act(dte[:], dte[:], AF.Exp)
            p.tt(dsc[:], dte[:], dtt[:, c, d * 8:(d + 1) * 8], ALU.mult)
            p.act(tot[:], tot[:], AF.Exp)
            pt = npt()
            p.mm(pt[:, :128], bT_[:, :], cT_[:, :], True, True)
            p.tt(cbm[:], pt[:, :128], Xm, ALU.mult)
            for r in range(8):
                rs_ = slice(r * 64, (r + 1) * 64)
                p.ts(xdt[:, rs_], xst[:, rs_], dtt[:, c, d * 8 + r:d * 8 + r + 1], None, ALU.mult)
                p.ts(xdd[:, rs_], xst[:, rs_], dsc[:, r:r + 1], None, ALU.mult)
            for r in range(8):
                rs_ = slice(r * 64, (r + 1) * 64)
                H = hb[r % 2]
                ax, dm, eb, csr, mT = H
                pb = psB[r % 2]
                p.ts(ax[:], Xm, att[:, c, d * 8 + r:d * 8 + r + 1], None, ALU.mult)
                p.mm(pb[:, :128], ones[:, :], ax[:], True, True)
                p.ts(dm[:], pb[:, :128], cst[:, r:r + 1], 0.0, ALU.subtract, ALU.min)
                p.act(dm[:], dm[:], AF.Exp)
                p.tt(mT[:], dm[:], cbm[:], ALU.mult)
                p.act(eb[:], pb[:, :128], AF.Exp)
                p.tt(csr[:], cT_[:], eb[:], ALU.mult)
                p.mm(psY[:, rs_], mT[:], xdt[:, rs_], True, False)
                p.mm(psY[:, rs_], csr[:], ST[:, rs_], False, True)
            ya = ysl(c)
            if d == 0:
                t_ = nwk()
                p.tt(t_[:, :], xst[:], dskt[:], ALU.mult)
                p.tt(ya, psY[:, :], t_[:, :], ALU.add)
            else:
                p.tt(ya, ya, psY[:, :], ALU.add)
            p.mm(psSt[:, :], btk[:], xdd[:], True, True)
            for r in range(8):
                rs_ = slice(r * 64, (r + 1) * 64)
                p.stt(ST[:, rs_], ST[:, rs_], tot[:, r:r + 1], psSt[:, rs_], ALU.mult, ALU.add)

    ztl = [p.sb("ztl%d" % i, [128, 512]) for i in range(2)]
    for c in range(NCH):
        t0 = c * 128
        zt = ztl[c % 2]
        for k in range(4):
            zi = nwk()
            p.dma(zi[:, :128], uT[rb + 2304 + k * 128:rb + 2304 + (k + 1) * 128, t0:t0 + 128])
            pt = npt()
            p.transpose(pt[:, :128], zi[:, :128], idt[:, :])
            p.act(zt[:, k * 128:(k + 1) * 128], pt[:, :128], AF.Silu)
        ya = ysl(c)
        p.tt(zt[:, :], zt[:, :], ya, ALU.mult)
        sq = nwk()
        ssq = nwk()
        p.act(sq[:, :], zt[:, :], AF.Square, accum_out=ssq[:, 0:1])
        p.ts(ssq[:, 0:1], ssq[:, 0:1], 1.0 / 512.0, EPS, ALU.mult, ALU.add)
        p.act(ssq[:, 0:1], ssq[:, 0:1], AF.Sqrt)
        p.recip(ssq[:, 0:1], ssq[:, 0:1])
        p.stt(zt[:, :], zt[:, :], ssq[:, 0:1], snt[:, :], ALU.mult, ALU.mult)
        for k in range(4):
            pt = npt()
            p.transpose(pt[:, :128], zt[:, k * 128:(k + 1) * 128], idt[:, :])
            o_ = nwk()
            p.copy(o_[:, :128], pt[:, :128], eng="act")
            p.dma(yT[1024 + h * 512 + k * 128:1024 + h * 512 + (k + 1) * 128, t0:t0 + 128], o_[:, :128])


B_PARAM_SHAPES = {"hcw": [128, 18, 3], "hcb": [128, 18], "hbias": [128, 2, 4], "hnw": [128, 4], "fw1": [33, 64], "fb1": [64, 1],
                  "ffq": [64, 1], "fw2": [64, 64], "fb2": [64, 1], "fw3": [64, 2, 2, 512], "absd": [128, 512],
                  "dtb": [128, 16], "alog": [128, 16], "dsk": [128, 512], "snw": [128, 512]}


def build_fused(nlayers=4, debug=False):
    nc = bass.Bass("TRN2", target_bir_lowering=False)
    xT0 = din(nc, "xT0", [D, NTOK])
    cc = din(nc, "cc", [128, 16, 2])
    wada = din(nc, "wada", [4, D, 12288]); bada = din(nc, "bada", [128, 4, 96])
    n1a = din(nc, "n1a", [4, 128, 16]); n2a = din(nc, "n2a", [4, 128, 16]); fnw = din(nc, "fnw", [128, 16])
    win = din(nc, "win", [4, D, PROJ]); wout = din(nc, "wout", [4, D, D]); w1 = din(nc, "w1", [4, D, DFF]); w2 = din(nc, "w2", [4, DFF, D])
    BP = {k: din(nc, "bp_" + k, [4, 2] + shp) for k, shp in B_PARAM_SHAPES.items()}
    ident = din(nc, "ident", [128, 128]); umat = din(nc, "umat", [128, 2, 128])
    consts = {}
    for (s0, L, RL) in SEQS:
        consts[L] = dict(feats=din(nc, "feats%d" % L, [33, L]), ntnf=din(nc, "ntnf%d" % L, [128, L // 128]),
                         ntnb=din(nc, "ntnb%d" % L, [128, L // 128]), cs=din(nc, "cs%d" % L, [L, L], BF16), ss=din(nc, "ss%d" % L, [L, L], BF16),
                         ph=din(nc, "ph%d" % L, [128, 2, L]))
    fo = dout(nc, "fo", [D, 4096])
    X = [dscr(nc, "X%d" % i, [D, NTOK], nodep=True) for i in range(2)]
    if debug:
        X[0] = dout(nc, "xdbg", [D, NTOK])
    uT = dscr(nc, "uT", [PROJ, NTOK], nodep=True)
    yT = dscr(nc, "yT", [D, NTOK], nodep=True)
    uc = [dscr(nc, "uc%d" % j, [128, NTOK]) for j in range(18)]
    z1 = [dscr(nc, "z1_%d" % j, [128, NTOK]) for j in range(4)]
    KT = {}
    for (s0, L, RL) in SEQS:
        for o in range(2):
            for cq in range(4):
                for ri in range(2):
                    KT[(L, o, cq, ri)] = dscr(nc, "kt_%d_%d_%d_%d" % (L, o, cq, ri), [128, L])
    xT0.buf.nodep = True

    p = Prog(nc)
    p.init_mem()
    mo = p.sb("mo", [128, 4, 96, 2])
    ones = p.sb("ones", [128, 128]); idt = p.sb("idt", [128, 128]); um = p.sb("um", [128, 2, 128])
    p.persist_done()
    p.memset(ones[:], 1.0)
    p.dma(idt[:], ident[:]); p.dma(um[:], umat[:])
    stage_P(p, cc, wada, bada, mo)
    xin = xT0
    for l in range(nlayers):
        mt = sub(mo, mo.ap[:, l])
        stage_A(p, ones, xin, mt, sub(n1a, n1a.ap[l]), sub(win, win.ap[l]), uT)
        for h in range(2):
            PR = {k: sub(t, t.ap[l, h]) for k, t in BP.items()}
            stage_B(p, nc, l, h, idt, um, ones, uT, yT, PR, consts, uc, z1, KT)
        last = (l == nlayers - 1)
        xo = X[l % 2]
        stage_C(p, ones, xin, yT, mt, sub(n2a, n2a.ap[l]), sub(wout, wout.ap[l]), sub(w1, w1.ap[l]), sub(w2, w2.ap[l]), xo,
                fnw=fnw if last else None, fo=fo if last else None)
        xin = xo
    p.barrier()
    p.final_wait()
    p.emit()
    return nc


def win_perm():
    idx = []
    for h in range(2):
        for base in (0, 1024, 2048):
            idx += list(range(base + h * 512, base + (h + 1) * 512))
        o0 = 3072
        idx += list(range(o0 + h * 512, o0 + (h + 1) * 512))
        idx += list(range(o0 + 1024 + h * 128, o0 + 1024 + (h + 1) * 128))
        idx += list(range(o0 + 1280 + h * 128, o0 + 1280 + (h + 1) * 128))
        o2 = 3072 + 1536 + 32
        idx += list(range(o2 + h * 512, o2 + (h + 1) * 512))
    o1 = 3072 + 1536
    for h in range(2):
        idx += list(range(o1 + h * 8, o1 + (h + 1) * 8))
        idx += list(range(o1 + 16 + h * 8, o1 + 16 + (h + 1) * 8))
    return np.array(idx)


def b_consts():
    out = {}
    out["ident"] = np.eye(128, dtype=np.float32)
    U = np.triu(np.ones((128, 128), np.float32))
    out["umat"] = np.ascontiguousarray(np.stack([U, U.T], axis=1))
    for L in (256, 4096):
        N = 2 * L
        t = np.linspace(0.0, 1.0, L, dtype=np.float32)
        bands = 16
        w = (2.0 * np.pi * np.arange(L, dtype=np.float32) / L).astype(np.float32)
        f = np.linspace(1e-4, bands - 1, bands, dtype=np.float32)
        fw = (f[None, :] * w[:, None]).astype(np.float32)
        feats = np.concatenate([t[:, None], np.cos(fw), -np.sin(fw)], axis=-1).astype(np.float32)
        out["feats%d" % L] = np.ascontiguousarray(feats.T)
        out["ntnf%d" % L] = np.ascontiguousarray((-t).reshape(L // 128, 128).T)
        tb = np.concatenate([t[1:], [0.0]]).astype(np.float32)
        out["ntnb%d" % L] = np.ascontiguousarray((-tb).reshape(L // 128, 128).T)
        idx = np.arange(L, dtype=np.float64) + 0.5
        ang = 2.0 * np.pi * np.outer(idx, idx) / N
        out["cs%d" % L] = np.cos(ang).astype(np.float32).astype(ml_dtypes.bfloat16)
        out["ss%d" % L] = np.sin(ang).astype(np.float32).astype(ml_dtypes.bfloat16)
        wv = 2.0 * np.pi * idx / N
        ph = np.stack([np.cos(wv / 2), np.sin(wv / 2)], axis=0).astype(np.float32)
        out["ph%d" % L] = np.ascontiguousarray(np.broadcast_to(ph[None], (128, 2, L)))
    return out


def b_params(inputs, l, h):
    P = {}
    hw = inputs["hy_conv_w"][l]; hb = inputs["hy_conv_b"][l]
    sw = inputs["ssd_conv_w"][l]; sbb = inputs["ssd_conv_b"][l]
    cw = np.zeros((128, 18, 3), np.float32); cb = np.zeros((128, 18), np.float32)
    for j in range(18):
        if j < 12:
            ch0 = (j // 4) * 1024 + h * 512 + (j % 4) * 128
            cw[:, j, :] = hw[:, ch0:ch0 + 128].T; cb[:, j] = hb[ch0:ch0 + 128]
        else:
            k = j - 12
            ch0 = h * 512 + k * 128 if k < 4 else (1024 + h * 128 if k == 4 else 1280 + h * 128)
            cw[:, j, :] = sw[:, ch0:ch0 + 128].T; cb[:, j] = sbb[ch0:ch0 + 128]
    P["hcw"] = cw; P["hcb"] = cb
    P["hbias"] = np.ascontiguousarray(inputs["hy_bias"][l][:, h * 512:(h + 1) * 512].reshape(2, 4, 128).transpose(2, 0, 1))
    P["hnw"] = np.ascontiguousarray(inputs["hy_norm_w"][l][h * 512:(h + 1) * 512].reshape(4, 128).T)
    P["fw1"] = np.ascontiguousarray(inputs["filt_w1"][l]); P["fb1"] = np.ascontiguousarray(inputs["filt_b1"][l][:, None])
    P["ffq"] = np.ascontiguousarray(inputs["filt_freq"][l][:, None]); P["fw2"] = np.ascontiguousarray(inputs["filt_w2"][l])
    P["fb2"] = np.ascontiguousarray(inputs["filt_b2"][l][:, None])
    P["fw3"] = np.ascontiguousarray(inputs["filt_w3"][l].reshape(64, 2, 2, 1024)[:, :, :, h * 512:(h + 1) * 512])
    deltas = np.linspace(np.log(1e-2) / 1.5, np.log(1e-2) / 0.3, 1024, dtype=np.float32)
    P["absd"] = np.ascontiguousarray(np.broadcast_to(np.abs(deltas)[h * 512:(h + 1) * 512][None], (128, 512)))
    dtb = np.concatenate([inputs["dt_bias"][l][0, h * 8:(h + 1) * 8], inputs["dt_bias"][l][1, h * 8:(h + 1) * 8]])
    al = np.concatenate([inputs["a_log"][l][0, h * 8:(h + 1) * 8], inputs["a_log"][l][1, h * 8:(h + 1) * 8]])
    P["dtb"] = np.ascontiguousarray(np.broadcast_to(dtb[None], (128, 16)))
    P["alog"] = np.ascontiguousarray(np.broadcast_to(al[None], (128, 16)))
    P["dsk"] = np.ascontiguousarray(np.broadcast_to(np.repeat(inputs["ssd_d"][l][h * 8:(h + 1) * 8], 64)[None], (128, 512)))
    P["snw"] = np.ascontiguousarray(np.broadcast_to(inputs["ssd_norm_w"][l][h * 512:(h + 1) * 512][None], (128, 512)))
    return P


def vecT(v):
    return np.ascontiguousarray(v.reshape(-1, 128).T)


def fused_in_maps(inputs):
    K = b_consts()
    perm = win_perm()
    shared = dict(K)
    shared["wada"] = inputs["w_ada"]
    shared["bada"] = np.ascontiguousarray(inputs["b_ada"].reshape(4, 96, 128).transpose(2, 0, 1))
    shared["n1a"] = np.stack([vecT(inputs["norm1_w"][l]) for l in range(4)])
    shared["n2a"] = np.stack([vecT(inputs["norm2_w"][l]) for l in range(4)])
    shared["fnw"] = vecT(inputs["final_norm_w"])
    shared["win"] = np.ascontiguousarray(inputs["w_in"][:, :, perm])
    shared["wout"] = inputs["w_out"]; shared["w1"] = inputs["w_mlp1"]; shared["w2"] = inputs["w_mlp2"]
    bp = [[b_params(inputs, l, h) for h in range(2)] for l in range(4)]
    for k in B_PARAM_SHAPES:
        shared["bp_" + k] = np.ascontiguousarray(np.stack([np.stack([bp[l][h][k] for h in range(2)]) for l in range(4)]))
    maps = []
    for core in range(8):
        b = core % 4
        m = dict(shared)
        m["xT0"] = np.ascontiguousarray(np.concatenate([inputs["ctx"][b].T, inputs["x"][b].T], axis=1))
        rows = np.stack([inputs["c"][b], inputs["c_ctx"]], axis=0)
        m["cc"] = np.ascontiguousarray(rows.reshape(2, 16, 128).transpose(2, 1, 0))
        maps.append(m)
    return maps


def kernel(**inputs):
    inputs = {k: np.asarray(v, dtype=np.float32) for k, v in inputs.items()}
    maps = fused_in_maps(inputs)
    res = run_bass_kernel_spmd(build_fused(4), maps, core_ids=list(range(8)))
    out = np.empty((4, 4096, 2048), np.float32)
    for b in range(4):
        out[b] = res.results[b]["fo"].T
    return out
```

```python
import numpy as np
import ml_dtypes
import concourse.bass as bass
import concourse.mybir as mybir
from concourse.bass_utils import run_bass_kernel_spmd

F32 = mybir.dt.float32
BF16 = mybir.dt.bfloat16
ALU = mybir.AluOpType
AF = mybir.ActivationFunctionType
AX = mybir.AxisListType

SAME_ENGINE_SYNC = True
NDMA_SLOTS = 6


class Buf:
    __slots__ = ("name", "w", "r", "untracked", "nodep")

    def __init__(self, name):
        self.name = name
        self.w = None
        self.r = {}
        self.untracked = False
        self.nodep = False


class V:
    __slots__ = ("buf", "ap")

    def __init__(self, buf, ap):
        self.buf = buf
        self.ap = ap

    def __getitem__(self, idx):
        return V(self.buf, self.ap[idx])


class Tile:
    def __init__(self, ap, name, buf=None):
        self.ap = ap
        self.buf = buf if buf is not None else Buf(name)

    def __getitem__(self, idx):
        return V(self.buf, self.ap[idx])

    def sub(self, idx, name):
        return Tile(self.ap[idx], name)


class Prog:
    ENG = ("pe", "dve", "act", "pool", "sp")

    def __init__(self, nc):
        self.nc = nc
        self.items = {e: [] for e in self.ENG}
        self.cnt = {e: 0 for e in self.ENG}
        self.waited = {e: {} for e in self.ENG}
        self.dma_use = {}
        self.dma_rr = {"sp": 0, "pool": 0, "act": 0}
        self.semkeys = set()
        self.out_deps = []
        self.nalloc = 0

    SB_WORDS = 53200

    def init_mem(self):
        self.sb_all = self.nc.alloc_sbuf_tensor("sb_all", [128, self.SB_WORDS], F32)
        self.ps_all = self.nc.alloc_psum_tensor("ps_all", [128, 4096], F32)
        self.sb_base = 0
        self.sb_off = 0
        self.ps_off = 0

    @staticmethod
    def _shape_view(ap, shape):
        if len(shape) == 2:
            return ap
        if len(shape) == 3:
            return ap.rearrange("p (a b) -> p a b", a=shape[1])
        return ap.rearrange("p (a b c) -> p a b c", a=shape[1], b=shape[2])

    def sb(self, name, shape, dtype=F32):
        ne = int(np.prod(shape[1:]))
        nw = ne if dtype == F32 else (ne + 1) // 2
        n = (nw + 7) // 8 * 8
        assert self.sb_off + n <= self.SB_WORDS, ("SBUF carve overflow", name, self.sb_off, n)
        ap = self.sb_all[0:shape[0], self.sb_off:self.sb_off + nw]
        if dtype != F32:
            ap = ap.bitcast(dtype)[:, 0:ne]
        self.sb_off += n
        return Tile(self._shape_view(ap, shape), name)

    def ps(self, name, shape, dtype=F32):
        assert shape[1] <= 512 and self.ps_off + 512 <= 4096, ("PSUM carve overflow", name)
        ap = self.ps_all[0:shape[0], self.ps_off:self.ps_off + shape[1]]
        self.ps_off += 512
        return Tile(ap, name)

    def persist_done(self):
        self.sb_base = self.sb_off

    def stage_begin(self):
        self.barrier()
        self.sb_off = self.sb_base
        self.ps_off = 0

    def barrier(self):
        latest = {}
        for e in self.ENG:
            if self.cnt[e] > 0:
                latest[("E", e)] = self.cnt[e]
        for key, uses in self.dma_use.items():
            latest[key] = 16 * uses
        for e in self.ENG:
            for key, val in latest.items():
                if key == ("E", "pe") and e == "pe":
                    continue
                self._need(e, key, val)

    def _need(self, eng, key, val):
        if self.waited[eng].get(key, 0) >= val:
            return
        self.waited[eng][key] = val
        self.items[eng].append(("wait", key, val))

    def _deps(self, eng, reads, writes, is_dma):
        deps = []
        for b in reads:
            if b.w is not None:
                deps.append(b.w)
        for b in writes:
            if b.w is not None:
                deps.append(b.w)
            deps.extend(b.r.items())
        for key, val in deps:
            if key[0] == "E" and key[1] == eng and not is_dma:
                if eng == "pe" or not SAME_ENGINE_SYNC:
                    continue
            self._need(eng, key, val)

    def _mark(self, reads, writes, key, val):
        for b in reads:
            b.r[key] = val
        for b in writes:
            b.w = (key, val)
            b.r = {}

    def op(self, eng, fn, reads, writes):
        reads = [x.buf if not isinstance(x, Buf) else x for x in reads]
        writes = [x.buf if not isinstance(x, Buf) else x for x in writes]
        self._deps(eng, reads, writes, False)
        self.cnt[eng] += 1
        key = ("E", eng)
        self.semkeys.add(key)
        self.items[eng].append(("ins", fn, key, 1))
        self._mark(reads, writes, key, self.cnt[eng])

    def dma(self, out, in_, queue=None, **kw):
        if queue is None:
            queue = "sp" if self.dma_rr["sp"] <= self.dma_rr["pool"] else "pool"
        slot = self.dma_rr[queue] % NDMA_SLOTS
        self.dma_rr[queue] += 1
        key = ("D", queue, slot)
        self.semkeys.add(key)
        uses = self.dma_use.get(key, 0)
        if uses > 0:
            self._need(queue, key, 16 * uses)
        wr = [] if (out.buf.untracked or out.buf.nodep) else [out.buf]
        rd_ = [] if in_.buf.nodep else [in_.buf]
        self._deps(queue, rd_, wr, True)
        self.dma_use[key] = uses + 1
        oap, iap = out.ap, in_.ap
        self.items[queue].append(("ins", lambda e: e.dma_start(out=oap, in_=iap, **kw), key, 16))
        self._mark(rd_, wr, key, 16 * (uses + 1))
        if out.buf.untracked:
            self.out_deps.append((key, 16 * (uses + 1)))

    def mm(self, out, lhsT, rhs, start, stop):
        o, l, r = out.ap, lhsT.ap, rhs.ap
        self.op("pe", lambda e: e.matmul(o, l, r, start=start, stop=stop), [lhsT, rhs], [out])

    def transpose(self, out, in_, ident):
        o, i, d = out.ap, in_.ap, ident.ap
        self.op("pe", lambda e: e.transpose(o, i, d), [in_, ident], [out])

    def act(self, out, in_, func, bias=0.0, scale=1.0, accum_out=None, eng="act"):
        rd = [in_]
        wr = [out]
        b = bias
        s = scale
        if isinstance(bias, V):
            rd.append(bias); b = bias.ap
        if isinstance(scale, V):
            rd.append(scale); s = scale.ap
        kw = {}
        if accum_out is not None:
            wr.append(accum_out); kw["accum_out"] = accum_out.ap
        o, i = out.ap, in_.ap
        self.op("act", lambda e: e.activation(o, i, func, bias=b, scale=s, **kw), rd, wr)

    def tt(self, out, in0, in1, op, eng="dve"):
        o, a, b = out.ap, in0.ap, in1.ap
        self.op(eng, lambda e: e.tensor_tensor(o, a, b, op), [in0, in1], [out])

    def ts(self, out, in0, s1, s2, op0, op1=None, eng="dve", accum_out=None):
        rd = [in0]
        wr = [out]
        a1, a2 = s1, s2
        if isinstance(s1, V):
            rd.append(s1); a1 = s1.ap
        if isinstance(s2, V):
            rd.append(s2); a2 = s2.ap
        o, i = out.ap, in0.ap
        kw = {}
        if accum_out is not None:
            wr.append(accum_out); kw["accum_out"] = accum_out.ap
        if op1 is None:
            self.op(eng, lambda e: e.tensor_scalar(o, i, a1, None, op0, **kw), rd, wr)
        else:
            self.op(eng, lambda e: e.tensor_scalar(o, i, a1, a2, op0, op1, **kw), rd, wr)

    def stt(self, out, in0, scalar, in1, op0, op1, eng="dve"):
        rd = [in0, in1]
        s = scalar
        if isinstance(scalar, V):
            rd.append(scalar); s = scalar.ap
        o, a, b = out.ap, in0.ap, in1.ap
        self.op(eng, lambda e: e.scalar_tensor_tensor(o, a, s, b, op0, op1), rd, [out])

    def copy(self, out, in_, eng="dve"):
        o, i = out.ap, in_.ap
        if eng == "act":
            self.op("act", lambda e: e.copy(o, i), [in_], [out])
        else:
            self.op(eng, lambda e: e.tensor_copy(o, i), [in_], [out])

    def memset(self, out, val, eng="dve"):
        o = out.ap
        self.op(eng, lambda e: e.memset(o, val), [], [out])

    def reduce(self, out, in_, op, axis=AX.X, eng="dve"):
        o, i = out.ap, in_.ap
        self.op(eng, lambda e: e.tensor_reduce(o, i, axis, op), [in_], [out])

    def recip(self, out, in_):
        o, i = out.ap, in_.ap
        self.op("dve", lambda e: e.reciprocal(o, i), [in_], [out])

    def final_wait(self, bufs=()):
        for b in bufs:
            b = b.buf if not isinstance(b, Buf) else b
            if b.w is not None:
                self._need("sp", b.w[0], b.w[1])
        for key, val in self.out_deps:
            self._need("sp", key, val)

    def emit(self):
        nc = self.nc
        sems = {}
        for key in sorted(self.semkeys):
            sems[key] = nc.alloc_semaphore("s_" + "_".join(str(k) for k in key))
        items = self.items

        def replay(eng_obj, lst):
            for it in lst:
                if it[0] == "wait":
                    eng_obj.wait_ge(sems[it[1]], it[2])
                else:
                    it[1](eng_obj).then_inc(sems[it[2]], it[3])

        with nc.Block() as block:
            @block.tensor
            def _(e):
                replay(e, items["pe"])

            @block.vector
            def _(e):
                replay(e, items["dve"])

            @block.scalar
            def _(e):
                replay(e, items["act"])

            @block.gpsimd
            def _(e):
                replay(e, items["pool"])

            @block.sync
            def _(e):
                replay(e, items["sp"])
        return sems


D = 2048
NKC = 16
NTOK = 4352
EPS = 1e-6
PROJ = 5664
DFF = 8192
SEQS = [(0, 256, 256), (256, 4096, 64)]
MAGIC = 12582912.0
TWO_PI = 2.0 * np.pi


def din(nc, name, shape, dtype=F32):
    return Tile(nc.dram_tensor(name, list(shape), dtype, kind="ExternalInput").ap(), name)


def dout(nc, name, shape):
    t = Tile(nc.dram_tensor(name, list(shape), F32, kind="ExternalOutput").ap(), name)
    t.buf.untracked = True
    return t


def dscr(nc, name, shape, nodep=False):
    t = Tile(nc.dram_tensor(name, list(shape), F32).ap(), name)
    t.buf.nodep = nodep
    return t


def sub(t, ap):
    return Tile(ap, t.buf.name, t.buf)


def stage_P(p, cc, wada, bada, mo):
    p.stage_begin()
    cct = p.sb("cct", [128, 16, 2])
    bt = p.sb("bt", [128, 4, 96])
    wb = [p.sb("wb%d" % i, [128, 16, 512]) for i in range(2)]
    pss = [p.ps("ps%d" % i, [128, 8]) for i in range(2)]
    p.dma(cct[:], cc[:])
    p.dma(bt[:], bada[:])
    p.act(cct[:], cct[:], AF.Silu)
    n = 0
    for l in range(4):
        wv = sub(wada, wada.ap[l].rearrange("(kc p) c -> p kc c", p=128))
        for blk in range(24):
            wt = wb[(l * 24 + blk) % 2]
            p.dma(wt[:], wv[:, :, blk * 512:(blk + 1) * 512])
            for j in range(4):
                ch = blk * 4 + j
                ps = pss[n % 2]
                n += 1
                for kc in range(16):
                    p.mm(ps[:, 0:2], wt[:, kc, j * 128:(j + 1) * 128], cct[:, kc, :], kc == 0, kc == 15)
                p.ts(mo[:, l, ch, :], ps[:, 0:2], bt[:, l, ch:ch + 1], None, ALU.add)


def rms_stats(p, xt, W, ones, sqb, ss_ps, rstd):
    for kc in range(NKC):
        sq = sqb[kc % 2]
        p.act(sq[:, :W], xt[:, kc, :W], AF.Square)
        p.mm(ss_ps[:, :W], ones[:, :], sq[:, :W], kc == 0, kc == NKC - 1)
    p.ts(rstd[:, :W], ss_ps[:, :W], 1.0 / D, EPS, ALU.mult, ALU.add)
    p.act(rstd[:, :W], rstd[:, :W], AF.Sqrt)
    p.recip(rstd[:, :W], rstd[:, :W])


def rms_mod(p, xt, W, hl, gs_col, sh_col, ones, sqb, ss_ps, rstd):
    rms_stats(p, xt, W, ones, sqb, ss_ps, rstd)
    for kc in range(NKC):
        tmp = sqb[kc % 2]
        p.stt(tmp[:, :W], xt[:, kc, :W], gs_col(kc), rstd[:, :W], ALU.mult, ALU.mult)
        p.act(hl[:, kc, :W], tmp[:, :W], AF.Identity, bias=sh_col(kc), scale=1.0)


def stage_W(p, pairs):
    p.stage_begin()
    fbuf = [p.sb("cf%d" % i, [128, 8192]) for i in range(3)]
    hbuf = [p.sb("ch%d" % i, [128, 8192], BF16) for i in range(3)]
    n = 0
    for (src, dst, rows, cols, dcmajor) in pairs:
        g = max(1, min(4, 8192 // cols))
        sv = sub(src, src.ap.rearrange("(rc p) c -> p rc c", p=128))
        if dcmajor:
            dv = sub(dst, dst.ap.rearrange("dc p rc c -> p rc dc c"))
        else:
            dv = sub(dst, dst.ap.rearrange("(rc p) c -> p rc c", p=128))
        for r0 in range(0, rows // 128, g):
            f_ = fbuf[n % 3]; h_ = hbuf[n % 3]
            fv = sub(f_, f_.ap[:, 0:g * cols].rearrange("p (a b) -> p a b", a=g))
            p.dma(fv[:], sv[:, r0:r0 + g, :])
            p.copy(h_[:, 0:g * cols], f_[:, 0:g * cols], eng="act" if n % 2 else "dve")
            if dcmajor:
                hv = sub(h_, h_.ap[:, 0:g * cols].rearrange("p (a d c) -> p a d c", a=g, c=128))
                for a_ in range(g):
                    p.dma(dv[:, r0 + a_, :, :], hv[:, a_, :, :])
            else:
                hv = sub(h_, h_.ap[:, 0:g * cols].rearrange("p (a b) -> p a b", a=g))
                p.dma(dv[:, r0:r0 + g, :], hv[:])
            n += 1


TILES_A = [(0, 256, 1)] + [(256 + 512 * j, 512, 0) for j in range(8)]


def stage_A(p, ones, xT, mt, n1, win, uT):
    p.stage_begin()
    n1t = p.sb("n1t", [128, 16])
    gs = p.sb("gs", [128, 16, 2])
    p.dma(n1t[:], n1[:])
    for r in range(2):
        p.stt(gs[:, :, r], mt[:, 16:32, r], 1.0, n1t[:, :], ALU.add, ALU.mult)
    xb = [p.sb("xb%d" % i, [128, 16, 512]) for i in range(2)]
    hl = p.sb("hl", [128, 16, 512], BF16)
    sqb = [p.sb("sq%d" % i, [128, 512]) for i in range(2)]
    rstd = p.sb("rstd", [128, 512])
    wbh = [p.sb("wbh%d" % i, [128, 16, 512], BF16) for i in range(3)]
    wdth = p.sb("wdth", [128, 16, 32], BF16)
    ob = [p.sb("ob%d" % i, [128, 512]) for i in range(3)]
    ss_ps = p.ps("ss_ps", [128, 512])
    pss = [p.ps("ps%d" % i, [128, 512]) for i in range(3)]
    xv = sub(xT, xT.ap.rearrange("(kc p) t -> p kc t", p=128))
    wv = sub(win, win.ap.rearrange("(kc p) c -> p kc c", p=128))
    p.dma(wdth[:], wv[:, :, 5632:5664])
    nw = 0
    no = 0
    for ti, (t0, W, r) in enumerate(TILES_A):
        xt = xb[ti % 2]
        p.dma(xt[:, :, :W], xv[:, :, t0:t0 + W])
        rms_mod(p, xt, W, hl, lambda kc: gs[:, kc, r:r + 1], lambda kc: mt[:, kc, r:r + 1], ones, sqb, ss_ps, rstd)
        for og in range(12):
            if og < 11:
                wt = wbh[nw % 3]
                p.dma(wt[:], wv[:, :, og * 512:(og + 1) * 512])
                nw += 1
                subs = [(wt, j * 128, 128, (og * 4 + j) * 128) for j in range(4)]
            else:
                subs = [(wdth, 0, 32, 5632)]
            for (wt_, c0, M, row0) in subs:
                ps = pss[no % 3]
                o = ob[no % 3]
                for kc in range(NKC):
                    p.mm(ps[:M, :W], wt_[:, kc, c0:c0 + M], hl[:, kc, :W], kc == 0, kc == NKC - 1)
                p.copy(o[:M, :W], ps[:M, :W], eng="act" if no % 2 else "dve")
                p.dma(uT[row0:row0 + M, t0:t0 + W], o[:M, :W])
                no += 1


TILES_C = [(0, 256, 1)] + [(256 + 512 * j, 512, 0) for j in range(8)]


def stage_C(p, ones, xT, yT, mt, n2, wout, w1, w2, xo, fnw=None, fo=None):
    final = fo is not None
    p.stage_begin()
    n2t = p.sb("n2t", [128, 16])
    gs = p.sb("gs", [128, 16, 2])
    p.dma(n2t[:], n2[:])
    for r in range(2):
        p.stt(gs[:, :, r], mt[:, 64:80, r], 1.0, n2t[:, :], ALU.add, ALU.mult)
    if final:
        fnt = p.sb("fnt", [128, 16])
        p.dma(fnt[:], fnw[:])
    WT = 512
    xt = p.sb("xt", [128, 16, WT])
    yt = p.sb("yt", [128, 16, WT], BF16)
    yst = [p.sb("yst%d" % i, [128, WT]) for i in range(2)]
    hh = p.sb("hh", [128, 32, WT], BF16)
    sqb = [p.sb("sq%d" % i, [128, WT]) for i in range(2)]
    rstd = p.sb("rstd", [128, WT])
    rl = [p.sb("rl%d" % i, [128, WT]) for i in range(2)]
    whf = [p.sb("wh%d" % i, [128, 8192], BF16) for i in range(3)]
    wbh = [sub(w, w.ap.rearrange("p (a b) -> p a b", a=16)) for w in whf]
    wbh2 = [sub(w, w.ap.rearrange("p (a b) -> p a b", a=64)) for w in whf]
    ss_ps = p.ps("ss_ps", [128, WT])
    pss = [p.ps("ps%d" % i, [128, WT]) for i in range(3)]
    xv = sub(xT, xT.ap.rearrange("(kc p) t -> p kc t", p=128))
    yv = sub(yT, yT.ap.rearrange("(kc p) t -> p kc t", p=128))
    xov = sub(xo, xo.ap.rearrange("(kc p) t -> p kc t", p=128))
    wov = sub(wout, wout.ap.rearrange("(kc p) c -> p kc c", p=128))
    w1v = sub(w1, w1.ap.rearrange("(kc p) c -> p kc c", p=128))
    w2v = w2
    if final:
        fov = sub(fo, fo.ap.rearrange("(kc p) t -> p kc t", p=128))
    nw = 0
    npz = 0
    for ti, (t0, W, r) in enumerate(TILES_C):
        p.dma(xt[:, :, :W], xv[:, :, t0:t0 + W])
        for mc in range(NKC):
            ys_ = yst[mc % 2]
            p.dma(ys_[:, :W], yv[:, mc, t0:t0 + W])
            p.copy(yt[:, mc, :W], ys_[:, :W], eng="act" if mc % 2 else "dve")
        for dg in range(4):
            wt = wbh[nw % 3]
            p.dma(wt[:], wov[:, :, dg * 512:(dg + 1) * 512])
            nw += 1
            for j in range(4):
                dc = dg * 4 + j
                ps = pss[npz % 3]
                npz += 1
                for mc in range(NKC):
                    p.mm(ps[:, :W], wt[:, mc, j * 128:(j + 1) * 128], yt[:, mc, :W], mc == 0, mc == NKC - 1)
                p.stt(xt[:, dc, :W], ps[:, :W], mt[:, 32 + dc, r:r + 1], xt[:, dc, :W], ALU.mult, ALU.add)
        rms_mod(p, xt, W, yt, lambda kc: gs[:, kc, r:r + 1], lambda kc: mt[:, 48 + kc, r:r + 1], ones, sqb, ss_ps, rstd)
        for hf in range(2):
            for hg in range(8):
                wt = wbh[nw % 3]
                c0 = (hf * 8 + hg) * 512
                p.dma(wt[:], w1v[:, :, c0:c0 + 512])
                nw += 1
                for j in range(4):
                    hc = hg * 4 + j
                    ps = pss[npz % 3]
                    rr = rl[npz % 2]
                    npz += 1
                    for kc in range(NKC):
                        p.mm(ps[:, :W], wt[:, kc, j * 128:(j + 1) * 128], yt[:, kc, :W], kc == 0, kc == NKC - 1)
                    p.act(rr[:, :W], ps[:, :W], AF.Relu)
                    p.tt(hh[:, hc, :W], rr[:, :W], rr[:, :W], ALU.mult)
            for dg in range(8):
                wt = wbh2[nw % 3]
                for q in range(2):
                    dc = dg * 2 + q
                    p.dma(wt[:, q * 32:(q + 1) * 32, :], w2v[dc, :, hf * 32:(hf + 1) * 32, :])
                nw += 1
                for q in range(2):
                    dc = dg * 2 + q
                    ps = pss[npz % 3]
                    npz += 1
                    for hc in range(32):
                        p.mm(ps[:, :W], wt[:, q * 32 + hc, :], hh[:, hc, :W], hc == 0, hc == 31)
                    p.stt(xt[:, dc, :W], ps[:, :W], mt[:, 80 + dc, r:r + 1], xt[:, dc, :W], ALU.mult, ALU.add)
        p.dma(xov[:, :, t0:t0 + W], xt[:, :, :W])
        if final and r == 0:
            rms_stats(p, xt, W, ones, sqb, ss_ps, rstd)
            for kc in range(NKC):
                ys_ = yst[kc % 2]
                p.stt(ys_[:, :W], xt[:, kc, :W], fnt[:, kc:kc + 1], rstd[:, :W], ALU.mult, ALU.mult)
                p.dma(fov[:, kc, t0 - 256:t0 - 256 + W], ys_[:, :W])


def stage_B(p, nc, l, h, idt, um, ones, uT, yT, PR, consts, uc, z1, KT):
    p.stage_begin()
    rb = h * 2816
    hcw, hcb, hbias, hnw = PR["hcw"], PR["hcb"], PR["hbias"], PR["hnw"]
    fw1, fb1, ffq, fw2, fb2, fw3, absd = PR["fw1"], PR["fb1"], PR["ffq"], PR["fw2"], PR["fb2"], PR["fw3"], PR["absd"]
    dtb, alog, dsk, snw = PR["dtb"], PR["alog"], PR["dsk"], PR["snw"]
    cwt = p.sb("cwt", [128, 18, 3]); p.dma(cwt[:], hcw[:])
    cbt = p.sb("cbt", [128, 18]); p.dma(cbt[:], hcb[:])
    hbt = p.sb("hbt", [128, 2, 4]); p.dma(hbt[:], hbias[:])
    hnt = p.sb("hnt", [128, 4]); p.dma(hnt[:], hnw[:])
    big = [p.sb("big%d" % i, [128, 8192 + (8 if i == 3 else 0)]) for i in range(4)]
    wk = [p.sb("wk%d" % i, [128, 512]) for i in range(8)]
    bigh = [Tile(b_.ap[:, :].bitcast(BF16)[:, 0:8192], "bigh", b_.buf) for b_ in big]
    psA = [p.ps("psA%d" % i, [128, 512]) for i in range(4)]
    psT = [p.ps("psT%d" % i, [128, 512]) for i in range(2)]
    psS = p.ps("psS", [128, 512])
    cnt = {"wk": 0, "ms": 0, "pt": 0}

    def nwk():
        cnt["wk"] += 1
        return wk[cnt["wk"] % 8]

    def npt():
        cnt["pt"] += 1
        return psT[cnt["pt"] % 2]

    for j in range(18):
        srow = j * 128 if j < 12 else 1536 + (j - 12) * 128
        for (s0, L, RL) in SEQS:
            for b0 in range(0, L, 512):
                W = min(512, L - b0)
                c0 = s0 + b0
                ui = nwk(); uo = nwk()
                p.dma(ui[:, :W], uT[rb + srow:rb + srow + 128, c0:c0 + W])
                p.ts(uo[:, :W], ui[:, :W], cwt[:, j, 1:2], cbt[:, j:j + 1], ALU.mult, ALU.add)
                uiv = ui.ap[:, :W].rearrange("p (a b) -> p a b", b=RL)
                uov = uo.ap[:, :W].rearrange("p (a b) -> p a b", b=RL)
                p.stt(V(uo.buf, uov[:, :, 1:RL]), V(ui.buf, uiv[:, :, 0:RL - 1]), cwt[:, j, 0:1], V(uo.buf, uov[:, :, 1:RL]), ALU.mult, ALU.add)
                p.stt(V(uo.buf, uov[:, :, 0:RL - 1]), V(ui.buf, uiv[:, :, 1:RL]), cwt[:, j, 2:3], V(uo.buf, uov[:, :, 0:RL - 1]), ALU.mult, ALU.add)
                if j >= 12:
                    p.act(uo[:, :W], uo[:, :W], AF.Silu)
                p.dma(uc[j][:, c0:c0 + W], uo[:, :W])

    mark_ = p.sb_off
    mslab = [p.sb("msl%d" % i, [128, 4, 512], BF16) for i in range(2)]
    f1t = p.sb("f1t", [33, 64]); p.dma(f1t[:], fw1[:])
    f2t = p.sb("f2t", [64, 64]); p.dma(f2t[:], fw2[:])
    f3t = p.sb("f3t", [64, 2, 2, 512]); p.dma(f3t[:], fw3[:])
    fbt = p.sb("fbt", [64, 3]); p.dma(fbt[:, 0:1], fb1[:]); p.dma(fbt[:, 1:2], fb2[:]); p.dma(fbt[:, 2:3], ffq[:])
    abt = p.sb("abt", [128, 512]); p.dma(abt[:], absd[:])
    hd1 = Tile(big[3].ap[:64, 0:4096], "hd1", big[3].buf)
    hd2 = Tile(big[3].ap[:64, 4096:8200], "hd2", big[3].buf)
    ntf = p.sb("ntf", [128, 32]); ntb = p.sb("ntb", [128, 32])
    nrm = p.sb("nrm", [128, 2])
    nacc = p.sb("nacc", [128, 2])
    pht = p.sb("pht", [128, 2, 512])

    def sin_layer(dst, src_fn, lhsT, bias_col, L):
        for b0 in range(0, L, 512):
            W = min(512, L - b0)
            ps = npt()
            p.mm(ps[:64, :W], lhsT, src_fn(b0, W), True, True)
            a = nwk(); n = nwk()
            p.ts(a[:64, :W], ps[:64, :W], fbt[:, bias_col:bias_col + 1], fbt[:, 2:3], ALU.add, ALU.mult)
            p.ts(n[:64, :W], a[:64, :W], 1.0 / TWO_PI, MAGIC, ALU.mult, ALU.add)
            p.ts(n[:64, :W], n[:64, :W], MAGIC, None, ALU.subtract)
            p.stt(a[:64, :W], n[:64, :W], -TWO_PI, a[:64, :W], ALU.mult, ALU.add)
            p.act(dst[:, b0:b0 + W], a[:64, :W], AF.Sin)

    def mat_pass(M, L, nacc, body, evac):
        FB = min(512, L)
        NRC = L // 128
        Mv = Tile(M.ap.rearrange("(rc p) f -> p rc f", p=128), "mv", M.buf)
        for fb in range(L // FB):
            for g in range(0, NRC, 4):
                ng = min(4, NRC - g)
                cnt["ms"] += 1
                sl = mslab[cnt["ms"] % 2]
                p.dma(sl[:, :ng, :FB], Mv[:, g:g + ng, fb * FB:(fb + 1) * FB])
                for q in range(ng):
                    rc = g + q
                    body(rc, sl[:, q, :FB], fb, rc == 0, rc == NRC - 1)
            evac(fb, FB)

    for (s0, L, RL) in SEQS:
        C = consts[L]
        NTC = L // 128
        FB = min(512, L)
        p.dma(ntf[:, :NTC], C["ntnf"][:]); p.dma(ntb[:, :NTC], C["ntnb"][:])
        for b0 in range(0, L, 512):
            W = min(512, L - b0)
            ft = nwk()
            p.dma(ft[:33, :W], C["feats"][:, b0:b0 + W])
            ps = npt()
            p.mm(ps[:64, :W], f1t[:, :], ft[:33, :W], True, True)
            a = nwk(); n = nwk()
            p.ts(a[:64, :W], ps[:64, :W], fbt[:, 0:1], fbt[:, 2:3], ALU.add, ALU.mult)
            p.ts(n[:64, :W], a[:64, :W], 1.0 / TWO_PI, MAGIC, ALU.mult, ALU.add)
            p.ts(n[:64, :W], n[:64, :W], MAGIC, None, ALU.subtract)
            p.stt(a[:64, :W], n[:64, :W], -TWO_PI, a[:64, :W], ALU.mult, ALU.add)
            p.act(hd1[:, b0:b0 + W], a[:64, :W], AF.Sin)
        p.memset(hd2[:, L:L + 8], 0.0)
        sin_layer(hd2, lambda b0, W: hd1[:, b0:b0 + W], f2t[:, :], 1, L)

        for cp in range(2):
            ch0 = cp * 256
            for o in range(2):
                ksum, kdiff, prT = bigh[0], bigh[1], big[2]
                for tc in range(NTC):
                    pf = npt(); pb = npt()
                    p.mm(pf[:, :256], hd2[:, tc * 128:tc * 128 + 128], f3t[:, o, 0, ch0:ch0 + 256], True, True)
                    p.mm(pb[:, :256], hd2[:, tc * 128 + 1:tc * 128 + 129], f3t[:, o, 1, ch0:ch0 + 256], True, True)
                    wf = nwk(); wbk = nwk()
                    p.act(wf[:, :256], abt[:, ch0:ch0 + 256], AF.Exp, scale=ntf[:, tc:tc + 1])
                    p.act(wbk[:, :256], abt[:, ch0:ch0 + 256], AF.Exp, scale=ntb[:, tc:tc + 1])
                    p.tt(wf[:, :256], pf[:, :256], wf[:, :256], ALU.mult)
                    p.tt(wbk[:, :256], pb[:, :256], wbk[:, :256], ALU.mult)
                    p.tt(ksum[:, tc * 256:(tc + 1) * 256], wf[:, :256], wbk[:, :256], ALU.add)
                    p.tt(kdiff[:, tc * 256:(tc + 1) * 256], wf[:, :256], wbk[:, :256], ALU.subtract)
                    p.act(wf[:, :256], wf[:, :256], AF.Abs)
                    p.act(wbk[:, :256], wbk[:, :256], AF.Abs)
                    p.tt(wf[:, :256], wf[:, :256], wbk[:, :256], ALU.add)
                    for q in range(2):
                        p.mm(psS[:, q * 8:q * 8 + 1], wf[:, q * 128:(q + 1) * 128], ones[:, 0:1], True, True)
                    if tc == 0:
                        p.copy(nacc[:, 0:1], psS[:, 0:1]); p.copy(nacc[:, 1:2], psS[:, 8:9])
                    else:
                        p.tt(nacc[:, 0:1], nacc[:, 0:1], psS[:, 0:1], ALU.add); p.tt(nacc[:, 1:2], nacc[:, 1:2], psS[:, 8:9], ALU.add)
                p.ts(nrm[:, :], nacc[:, 0:2], float(L), EPS * L, ALU.mult, ALU.add)
                p.recip(nrm[:, :], nrm[:, :])

                def bodyP(rc, sl, fb, first, last):
                    for q in range(2):
                        p.mm(psA[q][:, :FB], ksum[:, rc * 256 + q * 128: rc * 256 + (q + 1) * 128], sl, first, last)

                def evacP(fb, FBw):
                    for q in range(2):
                        p.copy(prT[:, q * L + fb * FBw: q * L + (fb + 1) * FBw], psA[q][:, :FBw], eng="act")
                mat_pass(C["cs"], L, 2, bodyP, evacP)

                def bodyQ(rc, sl, fb, first, last):
                    for q in range(2):
                        p.mm(psA[q][:, :FB], kdiff[:, rc * 256 + q * 128: rc * 256 + (q + 1) * 128], sl, first, last)

                def evacQ(fb, FBw, o=o, cp=cp):
                    p.dma(pht[:, :, :FBw], C["ph"][:, :, fb * FBw:(fb + 1) * FBw])
                    for q in range(2):
                        pr = prT[:, q * L + fb * FBw: q * L + (fb + 1) * FBw]
                        t1 = nwk(); t2 = nwk(); kr = nwk(); ki = nwk()
                        p.tt(t1[:, :FBw], pr, pht[:, 0, :FBw], ALU.mult)
                        p.tt(t2[:, :FBw], psA[q][:, :FBw], pht[:, 1, :FBw], ALU.mult)
                        p.tt(t1[:, :FBw], t1[:, :FBw], t2[:, :FBw], ALU.add)
                        p.ts(kr[:, :FBw], t1[:, :FBw], nrm[:, q:q + 1], None, ALU.mult)
                        p.tt(t1[:, :FBw], pr, pht[:, 1, :FBw], ALU.mult)
                        p.tt(t2[:, :FBw], psA[q][:, :FBw], pht[:, 0, :FBw], ALU.mult)
                        p.tt(t1[:, :FBw], t1[:, :FBw], t2[:, :FBw], ALU.subtract)
                        p.ts(ki[:, :FBw], t1[:, :FBw], nrm[:, q:q + 1], None, ALU.mult)
                        p.dma(KT[(L, o, cp * 2 + q, 0)][:, fb * FBw:(fb + 1) * FBw], kr[:, :FBw])
                        p.dma(KT[(L, o, cp * 2 + q, 1)][:, fb * FBw:(fb + 1) * FBw], ki[:, :FBw])
                mat_pass(C["ss"], L, 2, bodyQ, evacQ)

        for cp in range(2):
            for o in range(2):
                ztm, aT, u1f, u2f = bigh[0], big[1], bigh[2], bigh[3]
                zsrc = [uc[cp * 2 + q] if o == 0 else z1[cp * 2 + q] for q in range(2)]
                gate = [uc[4 + 4 * o + cp * 2 + q] for q in range(2)]
                for q in range(2):
                    for b0 in range(0, L, 512):
                        W = min(512, L - b0)
                        zi = nwk()
                        p.dma(zi[:, :W], zsrc[q][:, s0 + b0:s0 + b0 + W])
                        for k in range(W // 128):
                            tc = b0 // 128 + k
                            pt = npt()
                            p.transpose(pt[:, :128], zi[:, k * 128:(k + 1) * 128], idt[:, :])
                            p.copy(ztm[:, tc * 256 + q * 128: tc * 256 + (q + 1) * 128], pt[:, :128], eng="act" if k % 2 else "dve")

                def bodyA(rc, sl, fb, first, last):
                    for q in range(2):
                        p.mm(psA[q][:, :FB], ztm[:, rc * 256 + q * 128: rc * 256 + (q + 1) * 128], sl, first, last)

                def evacA(fb, FBw):
                    for q in range(2):
                        p.copy(aT[:, q * L + fb * FBw: q * L + (fb + 1) * FBw], psA[q][:, :FBw], eng="act")
                mat_pass(C["cs"], L, 2, bodyA, evacA)

                def evacB(fb, FBw, o=o, cp=cp):
                    for q in range(2):
                        kr = nwk(); ki = nwk(); t1 = nwk(); t2 = nwk(); bT = nwk()
                        p.dma(kr[:, :FBw], KT[(L, o, cp * 2 + q, 0)][:, fb * FBw:(fb + 1) * FBw])
                        p.dma(ki[:, :FBw], KT[(L, o, cp * 2 + q, 1)][:, fb * FBw:(fb + 1) * FBw])
                        a_ = aT[:, q * L + fb * FBw: q * L + (fb + 1) * FBw]
                        p.copy(bT[:, :FBw], psA[q][:, :FBw], eng="act")
                        p.tt(t1[:, :FBw], kr[:, :FBw], a_, ALU.mult)
                        p.tt(t2[:, :FBw], ki[:, :FBw], bT[:, :FBw], ALU.mult)
                        p.tt(t1[:, :FBw], t1[:, :FBw], t2[:, :FBw], ALU.add)
                        p.tt(t2[:, :FBw], kr[:, :FBw], bT[:, :FBw], ALU.mult)
                        p.tt(kr[:, :FBw], ki[:, :FBw], a_, ALU.mult)
                        p.tt(t2[:, :FBw], t2[:, :FBw], kr[:, :FBw], ALU.subtract)
                        for k in range(FBw // 128):
                            fc = fb * (FBw // 128) + k
                            for (src, dstb) in ((t1, u1f), (t2, u2f)):
                                pt = npt()
                                p.transpose(pt[:, :128], src[:, k * 128:(k + 1) * 128], idt[:, :])
                                p.copy(dstb[:, fc * 256 + q * 128: fc * 256 + (q + 1) * 128], pt[:, :128], eng="act")
                mat_pass(C["ss"], L, 2, bodyA, evacB)

                FBi = min(512, L)
                NRC = L // 128
                csv = Tile(C["cs"].ap.rearrange("(rc p) f -> p rc f", p=128), "csv", C["cs"].buf)
                ssv = Tile(C["ss"].ap.rearrange("(rc p) f -> p rc f", p=128), "ssv", C["ss"].buf)
                for tb in range(L // FBi):
                    for mi, (Mv, uf) in enumerate(((csv, u1f), (ssv, u2f))):
                        for g in range(0, NRC, 4):
                            ng = min(4, NRC - g)
                            cnt["ms"] += 1
                            sl = mslab[cnt["ms"] % 2]
                            p.dma(sl[:, :ng, :FBi], Mv[:, g:g + ng, tb * FBi:(tb + 1) * FBi])
                            for qq in range(ng):
                                rc = g + qq
                                for q in range(2):
                                    p.mm(psA[2 + q][:, :FBi], uf[:, rc * 256 + q * 128: rc * 256 + (q + 1) * 128], sl[:, qq, :FBi],
                                         mi == 0 and rc == 0, mi == 1 and rc == NRC - 1)
                    c0 = s0 + tb * FBi
                    for q in range(2):
                        cq = cp * 2 + q
                        zi = nwk(); gi = nwk(); r_ = nwk()
                        p.dma(zi[:, :FBi], zsrc[q][:, c0:c0 + FBi])
                        p.dma(gi[:, :FBi], gate[q][:, c0:c0 + FBi])
                        p.stt(r_[:, :FBi], zi[:, :FBi], hbt[:, o, cq:cq + 1], psA[2 + q][:, :FBi], ALU.mult, ALU.add)
                        p.tt(r_[:, :FBi], r_[:, :FBi], gi[:, :FBi], ALU.mult)
                        if o == 0:
                            p.dma(z1[cq][:, c0:c0 + FBi], r_[:, :FBi])
                        else:
                            sq = nwk(); rs = nwk()
                            p.act(sq[:, :FBi], r_[:, :FBi], AF.Square)
                            pt = npt()
                            p.mm(pt[:, :FBi], ones[:, :], sq[:, :FBi], True, True)
                            p.ts(rs[:, :FBi], pt[:, :FBi], 1.0 / 128.0, EPS, ALU.mult, ALU.add)
                            p.act(rs[:, :FBi], rs[:, :FBi], AF.Sqrt)
                            p.recip(rs[:, :FBi], rs[:, :FBi])
                            p.stt(r_[:, :FBi], r_[:, :FBi], hnt[:, cq:cq + 1], rs[:, :FBi], ALU.mult, ALU.mult)
                            p.dma(yT[h * 512 + cq * 128:h * 512 + (cq + 1) * 128, c0:c0 + FBi], r_[:, :FBi])

    p.barrier()
    p.sb_off = mark_
    dbt = p.sb("dbt", [128, 16]); p.dma(dbt[:], dtb[:])
    an = p.sb("an", [128, 16]); p.dma(an[:], alog[:])
    p.act(an[:], an[:], AF.Exp)
    p.ts(an[:], an[:], -1.0, None, ALU.mult)
    dskt = p.sb("dskt", [128, 512]); p.dma(dskt[:], dsk[:])
    snt = p.sb("snt", [128, 512]); p.dma(snt[:], snw[:])
    NCH = NTOK // 128
    dtt = p.sb("dtt", [128, NCH, 16])
    att = p.sb("att", [128, NCH, 16])
    yac = Tile(big[0].ap, "yac", big[0].buf)
    yac2 = Tile(big[1].ap, "yac2", big[1].buf)
    yac3 = Tile(big[2].ap, "yac3", big[2].buf)

    def ysl(c):
        if c < 16:
            return big[0][:, c * 512:(c + 1) * 512]
        if c < 32:
            return big[1][:, (c - 16) * 512:(c - 15) * 512]
        return big[2][:, (c - 32) * 512:(c - 31) * 512]

    for b0 in range(0, NTOK, 512):
        W = min(512, NTOK - b0)
        di = nwk()
        p.dma(di[:16, :W], uT[5632 + h * 16:5632 + (h + 1) * 16, b0:b0 + W])
        for k in range(W // 128):
            c = b0 // 128 + k
            pt = npt()
            p.transpose(pt[:, :16], di[:16, k * 128:(k + 1) * 128], idt[:16, :16])
            p.tt(dtt[:, c, :], pt[:, :16], dbt[:, :], ALU.add)
    p.act(dtt[:], dtt[:], AF.Exp)
    p.act(dtt[:], dtt[:], AF.Ln, bias=1.0)
    for c in range(NCH):
        p.tt(att[:, c, :], dtt[:, c, :], an[:, :], ALU.mult)

    ST = p.sb("ST", [128, 512])
    xst = p.sb("xst", [128, 512]); xdt = p.sb("xdt", [128, 512]); xdd = p.sb("xdd", [128, 512])
    btk = p.sb("btk", [128, 128]); bT_ = p.sb("bT_", [128, 128]); cT_ = p.sb("cT_", [128, 128])
    cbm = p.sb("cbm", [128, 128]); cst = p.sb("cst", [128, 8]); tot = p.sb("tot", [128, 8]); dsc = p.sb("dsc", [128, 8]); dte = p.sb("dte", [128, 8])
    psY = psA[0]; psSt = psA[1]; psB = [psA[2], psA[3]]
    hb = [[p.sb("hb%d_%d" % (i, k), [128, 128]) for k in range(5)] for i in range(2)]

    for d in range(2):
        Xm = um[:, d, :]
        p.memset(ST[:], 0.0)
        order = [0, 1] + list(range(2, NCH)) if d == 0 else [1, 0] + list(range(NCH - 1, 1, -1))
        for c in order:
            t0 = c * 128
            for k in range(4):
                xi = nwk()
                p.dma(xi[:, :128], uc[12 + k][:, t0:t0 + 128])
                pt = npt()
                p.transpose(pt[:, :128], xi[:, :128], idt[:, :])
                p.copy(xst[:, k * 128:(k + 1) * 128], pt[:, :128], eng="act")
            p.dma(bT_[:], uc[16][:, t0:t0 + 128])
            p.dma(cT_[:], uc[17][:, t0:t0 + 128])
            pt = npt()
            p.transpose(pt[:, :128], bT_[:, :], idt[:, :])
            p.copy(btk[:], pt[:, :128], eng="act")
            a8 = att[:, c, d * 8:(d + 1) * 8]
            pt = npt()
            p.mm(pt[:, 0:8], Xm, a8, True, True)
            p.copy(cst[:], pt[:, 0:8])
            pt = npt()
            p.mm(pt[:, 0:8], ones[:, :], a8, True, True)
            p.copy(tot[:], pt[:, 0:8])
            p.tt(dte[:], tot[:], cst[:], ALU.subtract)
            p.act(dte[:], dte[:], AF.Exp)
            p.tt(dsc[:], dte[:], dtt[:, c, d * 8:(d + 1) * 8], ALU.mult)
            p.act(tot[:], tot[:], AF.Exp)
            pt = npt()
            p.mm(pt[:, :128], bT_[:, :], cT_[:, :], True, True)
            p.tt(cbm[:], pt[:, :128], Xm, ALU.mult)
            for r in range(8):
                rs_ = slice(r * 64, (r + 1) * 64)
                p.ts(xdt[:, rs_], xst[:, rs_], dtt[:, c, d * 8 + r:d * 8 + r + 1], None, ALU.mult)
                p.ts(xdd[:, rs_], xst[:, rs_], dsc[:, r:r + 1], None, ALU.mult)
            for r in range(8):
                rs_ = slice(r * 64, (r + 1) * 64)
                H = hb[r % 2]
                ax, dm, eb, csr, mT = H
                pb = psB[r % 2]
                p.ts(ax[:], Xm, att[:, c, d * 8 + r:d * 8 + r + 1], None, ALU.mult)
                p.mm(pb[:, :128], ones[:, :], ax[:], True, True)
                p.ts(dm[:], pb[:, :128], cst[:, r:r + 1], 0.0, ALU.subtract, ALU.min)
                p.act(dm[:], dm[:], AF.Exp)
                p.tt(mT[:], dm[:], cbm[:], ALU.mult)
                p.act(eb[:], pb[:, :128], AF.Exp)
                p.tt(csr[:], cT_[:], eb[:], ALU.mult)
                p.mm(psY[:, rs_], mT[:], xdt[:, rs_], True, False)
                p.mm(psY[:, rs_], csr[:], ST[:, rs_], False, True)
            ya = ysl(c)
            if d == 0:
                t_ = nwk()
                p.tt(t_[:, :], xst[:], dskt[:], ALU.mult)
                p.tt(ya, psY[:, :], t_[:, :], ALU.add)
            else:
                p.tt(ya, ya, psY[:, :], ALU.add)
            p.mm(psSt[:, :], btk[:], xdd[:], True, True)
            for r in range(8):
                rs_ = slice(r * 64, (r + 1) * 64)
                p.stt(ST[:, rs_], ST[:, rs_], tot[:, r:r + 1], psSt[:, rs_], ALU.mult, ALU.add)

    ztl = [p.sb("ztl%d" % i, [128, 512]) for i in range(2)]
    for c in range(NCH):
        t0 = c * 128
        zt = ztl[c % 2]
        for k in range(4):
            zi = nwk()
            p.dma(zi[:, :128], uT[rb + 2304 + k * 128:rb + 2304 + (k + 1) * 128, t0:t0 + 128])
            pt = npt()
            p.transpose(pt[:, :128], zi[:, :128], idt[:, :])
            p.act(zt[:, k * 128:(k + 1) * 128], pt[:, :128], AF.Silu)
        ya = ysl(c)
        p.tt(zt[:, :], zt[:, :], ya, ALU.mult)
        sq = nwk()
        ssq = nwk()
        p.act(sq[:, :], zt[:, :], AF.Square, accum_out=ssq[:, 0:1])
        p.ts(ssq[:, 0:1], ssq[:, 0:1], 1.0 / 512.0, EPS, ALU.mult, ALU.add)
        p.act(ssq[:, 0:1], ssq[:, 0:1], AF.Sqrt)
        p.recip(ssq[:, 0:1], ssq[:, 0:1])
        p.stt(zt[:, :], zt[:, :], ssq[:, 0:1], snt[:, :], ALU.mult, ALU.mult)
        for k in range(4):
            pt = npt()
            p.transpose(pt[:, :128], zt[:, k * 128:(k + 1) * 128], idt[:, :])
            o_ = nwk()
            p.copy(o_[:, :128], pt[:, :128], eng="act")
            p.dma(yT[1024 + h * 512 + k * 128:1024 + h * 512 + (k + 1) * 128, t0:t0 + 128], o_[:, :128])


B_PARAM_SHAPES = {"hcw": [128, 18, 3], "hcb": [128, 18], "hbias": [128, 2, 4], "hnw": [128, 4], "fw1": [33, 64], "fb1": [64, 1],
                  "ffq": [64, 1], "fw2": [64, 64], "fb2": [64, 1], "fw3": [64, 2, 2, 512], "absd": [128, 512],
                  "dtb": [128, 16], "alog": [128, 16], "dsk": [128, 512], "snw": [128, 512]}


def build_fused(nlayers=4, debug=False):
    nc = bass.Bass("TRN2", target_bir_lowering=False)
    xT0 = din(nc, "xT0", [D, NTOK])
    cc = din(nc, "cc", [128, 16, 2])
    wada = din(nc, "wada", [4, D, 12288]); bada = din(nc, "bada", [128, 4, 96])
    n1a = din(nc, "n1a", [4, 128, 16]); n2a = din(nc, "n2a", [4, 128, 16]); fnw = din(nc, "fnw", [128, 16])
    win = din(nc, "win", [4, D, PROJ]); wout = din(nc, "wout", [4, D, D]); w1 = din(nc, "w1", [4, D, DFF]); w2 = din(nc, "w2", [4, DFF, D])
    BP = {k: din(nc, "bp_" + k, [4, 2] + shp) for k, shp in B_PARAM_SHAPES.items()}
    ident = din(nc, "ident", [128, 128]); umat = din(nc, "umat", [128, 2, 128])
    consts = {}
    for (s0, L, RL) in SEQS:
        consts[L] = dict(feats=din(nc, "feats%d" % L, [33, L]), ntnf=din(nc, "ntnf%d" % L, [128, L // 128]),
                         ntnb=din(nc, "ntnb%d" % L, [128, L // 128]), cs=din(nc, "cs%d" % L, [L, L], BF16), ss=din(nc, "ss%d" % L, [L, L], BF16),
                         ph=din(nc, "ph%d" % L, [128, 2, L]))
    fo = dout(nc, "fo", [D, 4096])
    X = [dscr(nc, "X%d" % i, [D, NTOK], nodep=True) for i in range(2)]
    if debug:
        X[0] = dout(nc, "xdbg", [D, NTOK])
    uT = dscr(nc, "uT", [PROJ, NTOK], nodep=True)
    yT = dscr(nc, "yT", [D, NTOK], nodep=True)
    uc = [dscr(nc, "uc%d" % j, [128, NTOK]) for j in range(18)]
    z1 = [dscr(nc, "z1_%d" % j, [128, NTOK]) for j in range(4)]
    KT = {}
    for (s0, L, RL) in SEQS:
        for o in range(2):
            for cq in range(4):
                for ri in range(2):
                    KT[(L, o, cq, ri)] = dscr(nc, "kt_%d_%d_%d_%d" % (L, o, cq, ri), [128, L])
    xT0.buf.nodep = True
    winb = Tile(nc.dram_tensor("winb", [D, PROJ], BF16).ap(), "winb"); woutb = Tile(nc.dram_tensor("woutb", [D, D], BF16).ap(), "woutb")
    w1b = Tile(nc.dram_tensor("w1b", [D, DFF], BF16).ap(), "w1b"); w2b = Tile(nc.dram_tensor("w2b", [16, 128, 64, 128], BF16).ap(), "w2b")
    for t_ in (winb, woutb, w1b, w2b):
        t_.buf.nodep = True

    p = Prog(nc)
    p.init_mem()
    mo = p.sb("mo", [128, 4, 96, 2])
    ones = p.sb("ones", [128, 128]); idt = p.sb("idt", [128, 128]); um = p.sb("um", [128, 2, 128])
    p.persist_done()
    p.memset(ones[:], 1.0)
    p.dma(idt[:], ident[:]); p.dma(um[:], umat[:])
    stage_P(p, cc, wada, bada, mo)
    xin = xT0
    for l in range(nlayers):
        mt = sub(mo, mo.ap[:, l])
        stage_W(p, [(sub(win, win.ap[l]), winb, D, PROJ, False), (sub(wout, wout.ap[l]), woutb, D, D, False),
                    (sub(w1, w1.ap[l]), w1b, D, DFF, False), (sub(w2, w2.ap[l]), w2b, DFF, D, True)])
        stage_A(p, ones, xin, mt, sub(n1a, n1a.ap[l]), winb, uT)
        for h in range(2):
            PR = {k: sub(t, t.ap[l, h]) for k, t in BP.items()}
            stage_B(p, nc, l, h, idt, um, ones, uT, yT, PR, consts, uc, z1, KT)
        last = (l == nlayers - 1)
        xo = X[l % 2]
        stage_C(p, ones, xin, yT, mt, sub(n2a, n2a.ap[l]), woutb, w1b, w2b, xo,
                fnw=fnw if last else None, fo=fo if last else None)
        xin = xo
    p.barrier()
    p.final_wait()
    p.emit()
    return nc


def win_perm():
    idx = []
    for h in range(2):
        for base in (0, 1024, 2048):
            idx += list(range(base + h * 512, base + (h + 1) * 512))
        o0 = 3072
        idx += list(range(o0 + h * 512, o0 + (h + 1) * 512))
        idx += list(range(o0 + 1024 + h * 128, o0 + 1024 + (h + 1) * 128))
        idx += list(range(o0 + 1280 + h * 128, o0 + 1280 + (h + 1) * 128))
        o2 = 3072 + 1536 + 32
        idx += list(range(o2 + h * 512, o2 + (h + 1) * 512))
    o1 = 3072 + 1536
    for h in range(2):
        idx += list(range(o1 + h * 8, o1 + (h + 1) * 8))
        idx += list(range(o1 + 16 + h * 8, o1 + 16 + (h + 1) * 8))
    return np.array(idx)


def b_consts():
    out = {}
    out["ident"] = np.eye(128, dtype=np.float32)
    U = np.triu(np.ones((128, 128), np.float32))
    out["umat"] = np.ascontiguousarray(np.stack([U, U.T], axis=1))
    for L in (256, 4096):
        N = 2 * L
        t = np.linspace(0.0, 1.0, L, dtype=np.float32)
        bands = 16
        w = (2.0 * np.pi * np.arange(L, dtype=np.float32) / L).astype(np.float32)
        f = np.linspace(1e-4, bands - 1, bands, dtype=np.float32)
        fw = (f[None, :] * w[:, None]).astype(np.float32)
        feats = np.concatenate([t[:, None], np.cos(fw), -np.sin(fw)], axis=-1).astype(np.float32)
        out["feats%d" % L] = np.ascontiguousarray(feats.T)
        out["ntnf%d" % L] = np.ascontiguousarray((-t).reshape(L // 128, 128).T)
        tb = np.concatenate([t[1:], [0.0]]).astype(np.float32)
        out["ntnb%d" % L] = np.ascontiguousarray((-tb).reshape(L // 128, 128).T)
        idx = np.arange(L, dtype=np.float64) + 0.5
        ang = 2.0 * np.pi * np.outer(idx, idx) / N
        out["cs%d" % L] = np.cos(ang).astype(np.float32).astype(ml_dtypes.bfloat16)
        out["ss%d" % L] = np.sin(ang).astype(np.float32).astype(ml_dtypes.bfloat16)
        wv = 2.0 * np.pi * idx / N
        ph = np.stack([np.cos(wv / 2), np.sin(wv / 2)], axis=0).astype(np.float32)
        out["ph%d" % L] = np.ascontiguousarray(np.broadcast_to(ph[None], (128, 2, L)))
    return out


def b_params(inputs, l, h):
    P = {}
    hw = inputs["hy_conv_w"][l]; hb = inputs["hy_conv_b"][l]
    sw = inputs["ssd_conv_w"][l]; sbb = inputs["ssd_conv_b"][l]
    cw = np.zeros((128, 18, 3), np.float32); cb = np.zeros((128, 18), np.float32)
    for j in range(18):
        if j < 12:
            ch0 = (j // 4) * 1024 + h * 512 + (j % 4) * 128
            cw[:, j, :] = hw[:, ch0:ch0 + 128].T; cb[:, j] = hb[ch0:ch0 + 128]
        else:
            k = j - 12
            ch0 = h * 512 + k * 128 if k < 4 else (1024 + h * 128 if k == 4 else 1280 + h * 128)
            cw[:, j, :] = sw[:, ch0:ch0 + 128].T; cb[:, j] = sbb[ch0:ch0 + 128]
    P["hcw"] = cw; P["hcb"] = cb
    P["hbias"] = np.ascontiguousarray(inputs["hy_bias"][l][:, h * 512:(h + 1) * 512].reshape(2, 4, 128).transpose(2, 0, 1))
    P["hnw"] = np.ascontiguousarray(inputs["hy_norm_w"][l][h * 512:(h + 1) * 512].reshape(4, 128).T)
    P["fw1"] = np.ascontiguousarray(inputs["filt_w1"][l]); P["fb1"] = np.ascontiguousarray(inputs["filt_b1"][l][:, None])
    P["ffq"] = np.ascontiguousarray(inputs["filt_freq"][l][:, None]); P["fw2"] = np.ascontiguousarray(inputs["filt_w2"][l])
    P["fb2"] = np.ascontiguousarray(inputs["filt_b2"][l][:, None])
    P["fw3"] = np.ascontiguousarray(inputs["filt_w3"][l].reshape(64, 2, 2, 1024)[:, :, :, h * 512:(h + 1) * 512])
    deltas = np.linspace(np.log(1e-2) / 1.5, np.log(1e-2) / 0.3, 1024, dtype=np.float32)
    P["absd"] = np.ascontiguousarray(np.broadcast_to(np.abs(deltas)[h * 512:(h + 1) * 512][None], (128, 512)))
    dtb = np.concatenate([inputs["dt_bias"][l][0, h * 8:(h + 1) * 8], inputs["dt_bias"][l][1, h * 8:(h + 1) * 8]])
    al = np.concatenate([inputs["a_log"][l][0, h * 8:(h + 1) * 8], inputs["a_log"][l][1, h * 8:(h + 1) * 8]])
    P["dtb"] = np.ascontiguousarray(np.broadcast_to(dtb[None], (128, 16)))
    P["alog"] = np.ascontiguousarray(np.broadcast_to(al[None], (128, 16)))
    P["dsk"] = np.ascontiguousarray(np.broadcast_to(np.repeat(inputs["ssd_d"][l][h * 8:(h + 1) * 8], 64)[None], (128, 512)))
    P["snw"] = np.ascontiguousarray(np.broadcast_to(inputs["ssd_norm_w"][l][h * 512:(h + 1) * 512][None], (128, 512)))
    return P


def vecT(v):
    return np.ascontiguousarray(v.reshape(-1, 128).T)


def fused_in_maps(inputs):
    K = b_consts()
    perm = win_perm()
    shared = dict(K)
    shared["wada"] = inputs["w_ada"]
    shared["bada"] = np.ascontiguousarray(inputs["b_ada"].reshape(4, 96, 128).transpose(2, 0, 1))
    shared["n1a"] = np.stack([vecT(inputs["norm1_w"][l]) for l in range(4)])
    shared["n2a"] = np.stack([vecT(inputs["norm2_w"][l]) for l in range(4)])
    shared["fnw"] = vecT(inputs["final_norm_w"])
    shared["win"] = np.ascontiguousarray(inputs["w_in"][:, :, perm])
    shared["wout"] = inputs["w_out"]; shared["w1"] = inputs["w_mlp1"]; shared["w2"] = inputs["w_mlp2"]
    bp = [[b_params(inputs, l, h) for h in range(2)] for l in range(4)]
    for k in B_PARAM_SHAPES:
        shared["bp_" + k] = np.ascontiguousarray(np.stack([np.stack([bp[l][h][k] for h in range(2)]) for l in range(4)]))
    maps = []
    for core in range(8):
        b = core % 4
        m = dict(shared)
        m["xT0"] = np.ascontiguousarray(np.concatenate([inputs["ctx"][b].T, inputs["x"][b].T], axis=1))
        rows = np.stack([inputs["c"][b], inputs["c_ctx"]], axis=0)
        m["cc"] = np.ascontiguousarray(rows.reshape(2, 16, 128).transpose(2, 1, 0))
        maps.append(m)
    return maps


def kernel(**inputs):
    inputs = {k: np.asarray(v, dtype=np.float32) for k, v in inputs.items()}
    maps = fused_in_maps(inputs)
    res = run_bass_kernel_spmd(build_fused(4), maps, core_ids=list(range(8)))
    out = np.empty((4, 4096, 2048), np.float32)
    for b in range(4):
        out[b] = res.results[b]["fo"].T
    return out
```
